# Optimizing a Trainium2 kernel written in Bass

```python
import math
import jax, jax.numpy as jnp
from jax import lax
import numpy as np

D_MODEL = 1024
BATCH = 16
SEQ = 4096
DEPTH = 1
DEC_BATCH = 8
DEC_SEQ = 64
PAST_LEN = 1024

CHUNK = 64
Q_BLOCK = 128
ROPE_THETA = 10000.0
NORM_EPS = 1e-6
NEG_INF = -1e30

MIX_WIDTH = D_MODEL
DIFF_HEADS = 4
DIFF_HEAD_DIM = 64
DIFF_WIDTH = DIFF_HEADS * 2 * DIFF_HEAD_DIM
MLA_HEADS = 4
MLA_Q_LORA = 256
MLA_KV_LORA = 128
MLA_NOPE = 128
MLA_ROPE = 64
MLA_V = 128
MLA_WIDTH = MLA_HEADS * MLA_V
MLA_SCALE = (MLA_NOPE + MLA_ROPE) ** -0.5
OFF_DQ = 0
OFF_DK = OFF_DQ + DIFF_WIDTH
OFF_DV = OFF_DK + DIFF_WIDTH
OFF_CQ = OFF_DV + DIFF_WIDTH
OFF_CKV = OFF_CQ + MLA_Q_LORA
OFF_KR = OFF_CKV + MLA_KV_LORA
IN_COLS = OFF_KR + MLA_ROPE
D_FF = 2816
CONV_W = 3

kernel_name = "hymba_diff_mla_convffn_stream_step"


def rmsnorm(x, g):
    xf = x.astype(jnp.float32)
    y = xf * lax.rsqrt(jnp.mean(xf * xf, axis=-1, keepdims=True) + NORM_EPS)
    return (y * g.astype(jnp.float32)).astype(x.dtype)


def rope(x, pos):
    dim = x.shape[-1]
    half = dim // 2
    inv = ROPE_THETA ** (-jnp.arange(half, dtype=jnp.float32) * (2.0 / dim))
    ang = pos.astype(jnp.float32)[:, None] * inv[None, :]
    shp = (ang.shape[0],) + (1,) * (x.ndim - 3) + (half,)
    cos = jnp.cos(ang).reshape(shp)
    sin = jnp.sin(ang).reshape(shp)
    xf = x.astype(jnp.float32)
    x1, x2 = xf[..., :half], xf[..., half:]
    return jnp.concatenate([x1 * cos - x2 * sin, x2 * cos + x1 * sin], axis=-1).astype(x.dtype)


def diff_attention(q, k, v, q_chunk, k_chunk, lam, lam_init, g_sub):
    s = jnp.einsum('bqhmd,bkhmd->bhmqk', q, k).astype(jnp.float32) * (DIFF_HEAD_DIM ** -0.5)
    vis = k_chunk[None, :] <= q_chunk[:, None]
    p = jax.nn.softmax(jnp.where(vis, s, NEG_INF), axis=-1)
    a = (p[:, :, 0] - lam * p[:, :, 1]).astype(v.dtype)
    o = jnp.einsum('bhqk,bkhe->bqhe', a, v)
    o = rmsnorm(o, g_sub) * (1.0 - lam_init)
    return o.reshape(o.shape[0], o.shape[1], DIFF_WIDTH)


def mla_attention(q_nope, q_rope, k_nope, k_rope, v, q_chunk, k_chunk):
    s = (jnp.einsum('bqhd,bkhd->bhqk', q_nope, k_nope)
         + jnp.einsum('bqhr,bkr->bhqk', q_rope, k_rope)).astype(jnp.float32) * MLA_SCALE
    vis = k_chunk[None, :] <= q_chunk[:, None]
    p = jax.nn.softmax(jnp.where(vis, s, NEG_INF), axis=-1)
    o = jnp.einsum('bhqk,bkhd->bqhd', p.astype(v.dtype), v)
    return o.reshape(o.shape[0], o.shape[1], MLA_WIDTH)


def over_query_blocks(fn, q_arrays, q_chunk):
    n_blk = q_chunk.shape[0] // Q_BLOCK

    def split(a):
        a = a.reshape((a.shape[0], n_blk, Q_BLOCK) + a.shape[2:])
        return jnp.moveaxis(a, 1, 0)

    xs = tuple(split(a) for a in q_arrays) + (q_chunk.reshape(n_blk, Q_BLOCK),)
    out = lax.map(lambda blk: fn(*blk), xs)
    out = jnp.moveaxis(out, 0, 1)
    return out.reshape(out.shape[0], n_blk * Q_BLOCK, out.shape[-1])


def hybrid_layer(x, pos, past, lam_init, g_attn, w_in, lq1, lk1, lq2, lk2, g_sub,
                 g_qa, w_qb, g_kva, w_kvb, w_out, g_ffn, w_up, w_conv, b_conv, w_down):
    B, T, _ = x.shape
    h = rmsnorm(x, g_attn)
    proj = h @ w_in
    q_d = rope(proj[..., OFF_DQ:OFF_DK].reshape(B, T, DIFF_HEADS, 2, DIFF_HEAD_DIM), pos)
    k_d = rope(proj[..., OFF_DK:OFF_DV].reshape(B, T, DIFF_HEADS, 2, DIFF_HEAD_DIM), pos)
    v_d = proj[..., OFF_DV:OFF_CQ].reshape(B, T, DIFF_HEADS, 2 * DIFF_HEAD_DIM)
    c_q = rmsnorm(proj[..., OFF_CQ:OFF_CKV], g_qa)
    c_kv = rmsnorm(proj[..., OFF_CKV:OFF_KR], g_kva)
    k_r = rope(proj[..., OFF_KR:IN_COLS], pos)
    q_m = (c_q @ w_qb).reshape(B, T, MLA_HEADS, MLA_NOPE + MLA_ROPE)
    q_nope = q_m[..., :MLA_NOPE]
    q_rope = rope(q_m[..., MLA_NOPE:], pos)

    if past is None:
        dk_all, dv_all, ckv_all, kr_all, k_pos = k_d, v_d, c_kv, k_r, pos
        conv_prev = jnp.zeros((B, CONV_W - 1, D_FF), x.dtype)
    else:
        p_dk, p_dv, p_ckv, p_kr, conv_prev = past
        P = p_dk.shape[1]
        dk_all = jnp.concatenate([p_dk, k_d], axis=1)
        dv_all = jnp.concatenate([p_dv, v_d], axis=1)
        ckv_all = jnp.concatenate([p_ckv, c_kv], axis=1)
        kr_all = jnp.concatenate([p_kr, k_r], axis=1)
        k_pos = jnp.concatenate([jnp.arange(P, dtype=jnp.int32), pos])
    Tk = ckv_all.shape[1]
    k_chunk = k_pos // CHUNK
    q_chunk = pos // CHUNK
    kv = (ckv_all @ w_kvb).reshape(B, Tk, MLA_HEADS, MLA_NOPE + MLA_V)
    k_nope, v_m = kv[..., :MLA_NOPE], kv[..., MLA_NOPE:]

    lam = (jnp.exp(jnp.sum(lq1.astype(jnp.float32) * lk1.astype(jnp.float32)))
           - jnp.exp(jnp.sum(lq2.astype(jnp.float32) * lk2.astype(jnp.float32))) + lam_init)

    def attend(qd, qn, qr, qc):
        od = diff_attention(qd, dk_all, dv_all, qc, k_chunk, lam, lam_init, g_sub)
        om = mla_attention(qn, qr, k_nope, kr_all, v_m, qc, k_chunk)
        return jnp.concatenate([od, om], axis=-1)

    if past is None:
        mix = over_query_blocks(attend, (q_d, q_nope, q_rope), q_chunk)
    else:
        mix = attend(q_d, q_nope, q_rope, q_chunk)
    x1 = x + mix @ w_out

    h2 = rmsnorm(x1, g_ffn)
    up = h2 @ w_up
    u, g = up[..., :D_FF], up[..., D_FF:]
    g_ext = jnp.concatenate([conv_prev, g], axis=1)
    conv = b_conv + sum(g_ext[:, j:j + T] * w_conv[j] for j in range(CONV_W))
    y = x1 + (jax.nn.silu(conv) * u) @ w_down
    new_conv = g_ext[:, -(CONV_W - 1):]
    return y, (k_d, v_d, c_kv, k_r, new_conv)


def setup_inputs(seed: int = 0) -> dict:
    key = jax.random.key(seed)
    ks = jax.random.split(key, 32)
    f32 = jnp.float32

    def nrm(k, shape, scale=1.0):
        return jax.random.normal(k, shape, f32) * scale

    def gain(k, shape):
        return 1.0 + 0.05 * jax.random.normal(k, shape, f32)

    return {
        "x_prompt": nrm(ks[0], (BATCH, SEQ, D_MODEL)),
        "x_sample": nrm(ks[1], (DEC_BATCH, DEC_SEQ, D_MODEL)),
        "cache_diff_k": nrm(ks[2], (DEPTH, DEC_BATCH, PAST_LEN, DIFF_HEADS, 2, DIFF_HEAD_DIM)),
        "cache_diff_v": nrm(ks[3], (DEPTH, DEC_BATCH, PAST_LEN, DIFF_HEADS, 2 * DIFF_HEAD_DIM)),
        "cache_mla_ckv": nrm(ks[4], (DEPTH, DEC_BATCH, PAST_LEN, MLA_KV_LORA)),
        "cache_mla_krope": nrm(ks[5], (DEPTH, DEC_BATCH, PAST_LEN, MLA_ROPE)),
        "state_conv": nrm(ks[6], (DEPTH, DEC_BATCH, CONV_W - 1, D_FF)),
        "g_attn": gain(ks[7], (DEPTH, D_MODEL)),
        "w_in": nrm(ks[8], (DEPTH, D_MODEL, IN_COLS), D_MODEL ** -0.5),
        "lambda_q1": nrm(ks[9], (DEPTH, DIFF_HEAD_DIM), 0.1),
        "lambda_k1": nrm(ks[10], (DEPTH, DIFF_HEAD_DIM), 0.1),
        "lambda_q2": nrm(ks[11], (DEPTH, DIFF_HEAD_DIM), 0.1),
        "lambda_k2": nrm(ks[12], (DEPTH, DIFF_HEAD_DIM), 0.1),
        "g_diff_sub": gain(ks[13], (DEPTH, 2 * DIFF_HEAD_DIM)),
        "g_q_lora": gain(ks[14], (DEPTH, MLA_Q_LORA)),
        "w_q_b": nrm(ks[15], (DEPTH, MLA_Q_LORA, MLA_HEADS * (MLA_NOPE + MLA_ROPE)), MLA_Q_LORA ** -0.5),
        "g_kv_lora": gain(ks[16], (DEPTH, MLA_KV_LORA)),
        "w_kv_b": nrm(ks[17], (DEPTH, MLA_KV_LORA, MLA_HEADS * (MLA_NOPE + MLA_V)), MLA_KV_LORA ** -0.5),
        "w_out": nrm(ks[18], (DEPTH, MIX_WIDTH, D_MODEL), MIX_WIDTH ** -0.5),
        "g_ffn": gain(ks[19], (DEPTH, D_MODEL)),
        "w_up": nrm(ks[20], (DEPTH, D_MODEL, 2 * D_FF), D_MODEL ** -0.5),
        "w_conv": nrm(ks[21], (DEPTH, CONV_W, D_FF), CONV_W ** -0.5),
        "b_conv": nrm(ks[22], (DEPTH, D_FF), 0.01),
        "w_down": nrm(ks[23], (DEPTH, D_FF, D_MODEL), D_FF ** -0.5),
        "g_final": gain(ks[24], (D_MODEL,)),
    }


def reference(x_prompt, x_sample, cache_diff_k, cache_diff_v, cache_mla_ckv, cache_mla_krope,
              state_conv, g_attn, w_in, lambda_q1, lambda_k1, lambda_q2, lambda_k2, g_diff_sub,
              g_q_lora, w_q_b, g_kv_lora, w_kv_b, w_out, g_ffn, w_up, w_conv, b_conv, w_down,
              g_final):
    S = x_prompt.shape[1]
    T = x_sample.shape[1]
    P = cache_diff_k.shape[2]
    pos_p = jnp.arange(S, dtype=jnp.int32)
    pos_s = P + jnp.arange(T, dtype=jnp.int32)
    hp, hs = x_prompt, x_sample
    st_p, st_s = [], []
    for l in range(DEPTH):
        lam_init = 0.8 - 0.6 * math.exp(-0.3 * l)
        wl = (g_attn[l], w_in[l], lambda_q1[l], lambda_k1[l], lambda_q2[l], lambda_k2[l],
              g_diff_sub[l], g_q_lora[l], w_q_b[l], g_kv_lora[l], w_kv_b[l], w_out[l],
              g_ffn[l], w_up[l], w_conv[l], b_conv[l], w_down[l])
        hp, sp = hybrid_layer(hp, pos_p, None, lam_init, *wl)
        past = (cache_diff_k[l], cache_diff_v[l], cache_mla_ckv[l], cache_mla_krope[l], state_conv[l])
        hs, ss = hybrid_layer(hs, pos_s, past, lam_init, *wl)
        st_p.append(sp)
        st_s.append(ss)
    y_prompt = rmsnorm(hp, g_final)
    y_sample = rmsnorm(hs, g_final)
    new_diff_k_p = jnp.stack([s[0] for s in st_p], 0)
    new_diff_v_p = jnp.stack([s[1] for s in st_p], 0)
    new_mla_ckv_p = jnp.stack([s[2] for s in st_p], 0)
    new_mla_krope_p = jnp.stack([s[3] for s in st_p], 0)
    new_conv_p = jnp.stack([s[4] for s in st_p], 0)
    new_diff_k_s = jnp.stack([s[0] for s in st_s], 0)
    new_diff_v_s = jnp.stack([s[1] for s in st_s], 0)
    new_mla_ckv_s = jnp.stack([s[2] for s in st_s], 0)
    new_mla_krope_s = jnp.stack([s[3] for s in st_s], 0)
    new_conv_s = jnp.stack([s[4] for s in st_s], 0)
    return (y_prompt, y_sample, new_diff_k_p, new_diff_v_p, new_mla_ckv_p, new_mla_krope_p,
            new_conv_p, new_diff_k_s, new_diff_v_s, new_mla_ckv_s, new_mla_krope_s, new_conv_s)
```

```python
import math
from contextlib import ExitStack

import numpy as np
import ml_dtypes

import concourse.bass as bass
import concourse.mybir as mybir
from concourse.bass_utils import run_bass_kernel_spmd

F32 = mybir.dt.float32
BF16 = mybir.dt.bfloat16
ALU = mybir.AluOpType
AF = mybir.ActivationFunctionType

D = 1024
NCORES = 8
SEQ = 4096
DEC_SEQ = 64
PAST = 1024
IN_COLS = 1984
D_FF = 2816
NCH = 22
EPS = 1e-6
LAM_INIT = 0.8 - 0.6 * math.exp(-0.3 * 0)
MLA_SCALE = (128 + 64) ** -0.5
DIFF_SCALE = 64 ** -0.5
RING_SLOTS = 6
RINGD_SLOTS = 6
RING_W = 2048


class Res:
    __slots__ = ("name", "w", "r")

    def __init__(self, name):
        self.name = name
        self.w = None
        self.r = []


class Stream:
    def __init__(self, sched, name, eng, inc, limit):
        self.sched, self.name, self.eng, self.inc, self.limit = sched, name, eng, inc, limit
        self.epoch = 0
        self.sem = sched.new_sem(name + "_0")
        self.count = 0
        self.total = 0
        self.seen = {}

    def bump(self):
        self.count += 1
        self.total += 1
        ev = ((self.name, self.epoch), self.sem, self.count * self.inc, self.name)
        if self.count >= self.limit:
            self.epoch += 1
            self.sem = self.sched.new_sem("%s_%d" % (self.name, self.epoch))
            self.count = 0
        return ev


class Sched:
    def __init__(self, nc, stack):
        self.nc = nc
        self.stack = stack
        self.nsem = 0
        self.pe = Stream(self, "pe", nc.tensor, 1, 20000)
        self.act = Stream(self, "act", nc.scalar, 1, 20000)
        self.dve = Stream(self, "dve", nc.vector, 1, 20000)
        self.pool = Stream(self, "pool", nc.gpsimd, 1, 20000)
        self.sp = nc.sync
        self.sp_seen = {}
        self.dma_streams = []
        self.last_ev = {}

    def new_sem(self, name):
        self.nsem += 1
        return self.stack.enter_context(self.nc.semaphore(name))

    def dma_stream(self, name):
        s = Stream(self, name, None, 16, 2000)
        self.dma_streams.append(s)
        return s

    def _deps(self, stream_name, reads, writes, same_engine_raw=True):
        need = {}

        def add(ev, same_ok):
            if ev is None:
                return
            key, sem, val, sname = ev
            if sname == stream_name and not same_ok:
                return
            if key not in need or need[key][1] < val:
                need[key] = (sem, val)

        for r in reads:
            add(r.w, same_engine_raw)
        for w in writes:
            add(w.w, same_engine_raw)
            for ev in w.r:
                add(ev, same_engine_raw)
        return need

    @staticmethod
    def _record(ev, reads, writes):
        for r in reads:
            r.r.append(ev)
            if len(r.r) > 48:
                best = {}
                for e in r.r:
                    if e[0] not in best or best[e[0]][2] < e[2]:
                        best[e[0]] = e
                r.r = list(best.values())
        for w in writes:
            w.w = ev
            w.r = []

    def op(self, st, fn, reads=(), writes=()):
        need = self._deps(st.name, reads, writes, same_engine_raw=(st is not self.pe))
        for key, (sem, val) in need.items():
            if st.seen.get(key, 0) < val:
                st.eng.wait_ge(sem, val)
                st.seen[key] = val
        ins = fn()
        ins.then_inc(st.sem, 1)
        ev = st.bump()
        self.last_ev[st.name] = ev
        self._record(ev, reads, writes)
        return ins

    def dma(self, ds, out, in_, reads=(), writes=(), **kw):
        need = self._deps("sp:" + ds.name, reads, writes)
        for key, (sem, val) in need.items():
            if self.sp_seen.get(key, 0) < val:
                self.sp.wait_ge(sem, val)
                self.sp_seen[key] = val
        ins = self.sp.dma_start(out=out, in_=in_, **kw)
        ins.then_inc(ds.sem, 16)
        ev = ds.bump()
        self.last_ev[ds.name] = ev
        self._record(ev, reads, writes)
        return ins

    def barrier(self):
        evs = list(self.last_ev.values())
        for st in (self.pe, self.act, self.dve, self.pool):
            for key, sem, val, sname in evs:
                if sname == st.name:
                    continue
                if st.seen.get(key, 0) < val:
                    st.eng.wait_ge(sem, val)
                    st.seen[key] = val
        for key, sem, val, sname in evs:
            if self.sp_seen.get(key, 0) < val:
                self.sp.wait_ge(sem, val)
                self.sp_seen[key] = val


def build(n_prompt=2, seq=SEQ, sample=True, past=PAST, dec=DEC_SEQ):
    nc = bass.Bass("TRN2", target_bir_lowering=False)
    NT = seq // 128
    NKT = max(NT, past // 128 + 1)
    NTAB = NT + 1

    def din(name, shape, dt=F32):
        return nc.dram_tensor(name, list(shape), dt, kind="ExternalInput").ap()

    def dout(name, shape, dt=F32):
        return nc.dram_tensor(name, list(shape), dt, kind="ExternalOutput").ap()

    xp = din("xp", [n_prompt, seq, D])
    xs = din("xs", [dec, D])
    cdk = din("cdk", [past, 512])
    cdv = din("cdv", [past, 512])
    cckv = din("cckv", [past, 128])
    ckr = din("ckr", [past, 64])
    sconv = din("sconv", [2, D_FF])
    g_attn_pk = din("g_attn_pk", [128, 8])
    g_ffn_pk = din("g_ffn_pk", [128, 8])
    g_q_pk = din("g_q_pk", [128, 2])
    g_sub_p = din("g_sub_p", [128, 1])
    g_kv = din("g_kv", [1, 128])
    g_final = din("g_final", [1, D])
    lamv = din("lamv", [1, 256])
    wconv_pk = din("wconv_pk", [128, NCH, 3])
    bconv_pk = din("bconv_pk", [128, NCH])
    w_in = din("w_in", [D, IN_COLS])
    w_qb = din("w_qb", [256, 768])
    w_kvb = din("w_kvb", [128, 1024])
    w_out = din("w_out", [D, D])
    w_up = din("w_up", [D, 2 * D_FF])
    w_down = din("w_down", [D_FF, D])
    ident_d = din("ident", [128, 128], BF16)
    cos_d = din("cos_t", [128, NTAB, 32])
    sin_d = din("sin_t", [128, NTAB, 32])
    yp = dout("yp", [n_prompt, seq, D])
    dkp = dout("dkp", [n_prompt, seq, 512])
    dvp = dout("dvp", [n_prompt, seq, 512])
    ckvp = dout("ckvp", [n_prompt, seq, 128])
    krp = dout("krp", [n_prompt, seq, 64])
    convp = dout("convp", [n_prompt, 2, D_FF])
    ys = dout("ys", [dec, D])
    dks = dout("dks", [dec, 512])
    dvs = dout("dvs", [dec, 512])
    ckvs = dout("ckvs", [dec, 128])
    krs = dout("krs", [dec, 64])
    convs = dout("convs", [2, D_FF])
    win_s = nc.dram_tensor("win_s", [4, 128, 8, 512], BF16).ap()
    wout_s = nc.dram_tensor("wout_s", [2, 128, 8, 512], BF16).ap()
    ffn_s = nc.dram_tensor("ffn_s", [NCH, 128, 3072], BF16).ap()

    with ExitStack() as st:
        S = Sched(nc, st)
        pe, act, dve, pool = S.pe, S.act, S.dve, S.pool

        def sb(name, shape, dt):
            return st.enter_context(nc.sbuf_tensor("s_" + name, list(shape), dt))

        banks = [st.enter_context(nc.psum_tensor("bank%d" % i, [128, 512], F32)) for i in (0, 1)]
        pT = st.enter_context(nc.psum_tensor("bankT", [128, 1024], BF16))
        banks += [None]
        pS_all = st.enter_context(nc.psum_tensor("bankS", [128, 2048], F32))
        banks += [pS_all[:, i * 512:(i + 1) * 512] for i in range(4)]
        banks += [st.enter_context(nc.psum_tensor("bank7", [128, 512], F32))]
        bres = [Res("bank%d" % i) for i in range(8)]
        pTr = bres[2]

        out_res = Res("outputs")
        scr_res = {"win": Res("win_s"), "wout": Res("wout_s"), "ffn": Res("ffn_s")}

        ident = sb("ident", [128, 128], BF16)
        cos_t = sb("cos_t", [128, NTAB, 32], F32)
        sin_t = sb("sin_t", [128, NTAB, 32], F32)
        gkv_b = sb("gkv_b", [128, 128], F32)
        gfin_b = sb("gfin_b", [128, D], F32)
        wconv = sb("wconv", [128, NCH, 3], F32)
        bconv = sb("bconv", [128, NCH], F32)
        neg_lam = sb("neg_lam", [128, 1], F32)
        wq_eff = sb("wq_eff", [128, 2, 4, 128], BF16)
        wqr = sb("wqr", [128, 2, 4, 64], BF16)
        R = {}

        def res(name):
            if name not in R:
                R[name] = Res(name)
            return R[name]

        _stat_next = [0]

        def stat(name, width=1):
            i = _stat_next[0]
            _stat_next[0] += width
            assert _stat_next[0] <= 64
            return st_small[:, i:i + width], res("stat_" + name)

        dq = {}

        def dstream(name):
            if getattr(S, "dry", False):
                return None
            if name not in dq:
                dq[name] = S.dma_stream("d_" + name)
            return dq[name]

        def mm(out, lhsT, rhs, start, stop, reads, writes):
            S.op(pe, lambda: nc.tensor.matmul(out, lhsT=lhsT, rhs=rhs, start=start, stop=stop,
                                              skip_group_check=True), reads, writes)

        def tr(out, in_, reads, writes):
            n = in_.shape[0]
            S.op(pe, lambda: nc.tensor.transpose(out=out, in_=in_, identity=ident[0:n, 0:n]),
                 list(reads) + [res("ident")], writes)

        def eng_of(stm):
            return {"act": nc.scalar, "dve": nc.vector, "pool": nc.gpsimd}[stm.name]

        def copy(stm, out, in_, reads, writes):
            if stm is act:
                S.op(act, lambda: nc.scalar.copy(out=out, in_=in_), reads, writes)
            else:
                e = eng_of(stm)
                S.op(stm, lambda: e.tensor_copy(out=out, in_=in_), reads, writes)

        def rstd_from_ssq(ssq_ap, ssq_res, n_feat, nm, n):
            ms, msr = stat(nm + "_ms")
            rs, rsr = stat(nm + "_rs")
            S.op(dve, lambda: nc.vector.tensor_scalar(out=ms[0:n], in0=ssq_ap[0:n], scalar1=1.0 / n_feat,
                                                      scalar2=EPS, op0=ALU.mult, op1=ALU.add),
                 [ssq_res], [msr])
            S.op(act, lambda: nc.scalar.activation(out=ms[0:n], in_=ms[0:n], func=AF.Ln), [msr], [msr])
            S.op(act, lambda: nc.scalar.activation(out=rs[0:n], in_=ms[0:n], func=AF.Exp, scale=-0.5),
                 [msr], [rsr])
            return rs, rsr

        S.dma(dstream("k0"), ident[:], ident_d[:], writes=[res("ident")])
        S.dma(dstream("k1"), cos_t[:], cos_d[:], writes=[res("tab")])
        S.dma(dstream("k2"), sin_t[:], sin_d[:], writes=[res("tab")])
        S.dma(dstream("k3"), gkv_b[:], g_kv.partition_broadcast(128), writes=[res("gkv")])
        S.dma(dstream("k4"), gfin_b[:], g_final.partition_broadcast(128), writes=[res("gfin")])
        S.dma(dstream("k5"), wconv[:], wconv_pk[:], writes=[res("wconv")])
        S.dma(dstream("k6"), bconv[:], bconv_pk[:], writes=[res("wconv")])

        with ExitStack() as st2:
            def sb2(name, shape, dt):
                return st2.enter_context(nc.sbuf_tensor("t_" + name, list(shape), dt))

            gA = sb2("gA", [128, 8], F32)
            gF = sb2("gF", [128, 8], F32)
            gQ = sb2("gQ", [128, 2], F32)
            gS = sb2("gS", [128, 1], F32)
            lamb = sb2("lamb", [128, 4, 64], F32)
            lj = sb2("lj", [128, 64], F32)
            ls = sb2("ls", [128, 4], F32)
            S.dma(dstream("k7"), gA[:], g_attn_pk[:], writes=[res("gA")])
            S.dma(dstream("k8"), gF[:], g_ffn_pk[:], writes=[res("gF")])
            S.dma(dstream("k9"), gQ[:], g_q_pk[:], writes=[res("gQ")])
            S.dma(dstream("k10"), gS[:], g_sub_p[:], writes=[res("gS")])
            S.dma(dstream("klam"), lamb[:].rearrange("p a b -> p (a b)"), lamv.partition_broadcast(128),
                  writes=[res("lamb")])
            for i in range(2):
                S.op(dve, lambda i=i: nc.vector.scalar_tensor_tensor(
                    out=lj[:], in0=lamb[:, 2 * i, :], scalar=1.0, in1=lamb[:, 2 * i + 1, :],
                    op0=ALU.mult, op1=ALU.mult, accum_out=ls[:, i:i + 1]),
                    [res("lamb")], [res("lj"), res("ls")])
            S.op(act, lambda: nc.scalar.activation(out=ls[:, 2:4], in_=ls[:, 0:2], func=AF.Exp),
                 [res("ls")], [res("ls2")])
            S.op(dve, lambda: nc.vector.scalar_tensor_tensor(
                out=neg_lam[:], in0=ls[:, 3:4], scalar=-LAM_INIT, in1=ls[:, 2:3],
                op0=ALU.add, op1=ALU.subtract), [res("ls2")], [res("neg_lam")])
            S.op(dve, lambda: nc.vector.tensor_scalar(out=gS[:], in0=gS[:], scalar1=1.0 - LAM_INIT, scalar2=None,
                                                      op0=ALU.mult), [res("gS")], [res("gS")])

            NB = 4
            wst = sb2("wst", [128, NB, 2048], F32)
            wbf = sb2("wbf", [128, NB, 4, 512], BF16)
            S.op(pool, lambda: nc.gpsimd.memset(wbf[:], 0.0), [], [res("wbf%d" % b) for b in range(NB)])
            win_v = win_s.rearrange("s p k n -> p s k n")
            def win_load(k):
                b = k % NB
                S.dma(dstream("w%d" % b), wst[:, b, 0:IN_COLS], w_in[k * 128:(k + 1) * 128, :],
                      writes=[res("wst%d" % b)])
            for k in range(NB - 2):
                win_load(k)
            for k in range(8):
                b = k % NB
                if k + NB - 2 < 8:
                    win_load(k + NB - 2)
                for s_ in range(4):
                    wd = 512 if s_ < 3 else 448
                    if s_ % 2 == 0:
                        S.op(dve, lambda s_=s_, wd=wd, b=b, k=k: nc.vector.tensor_scalar(
                            out=wbf[:, b, s_, 0:wd], in0=wst[:, b, s_ * 512:s_ * 512 + wd],
                            scalar1=gA[:, k:k + 1], scalar2=None, op0=ALU.mult),
                            [res("wst%d" % b), res("gA")], [res("wbf%d" % b)])
                    else:
                        S.op(act, lambda s_=s_, wd=wd, b=b, k=k: nc.scalar.activation(
                            out=wbf[:, b, s_, 0:wd], in_=wst[:, b, s_ * 512:s_ * 512 + wd], func=AF.Copy,
                            scale=gA[:, k:k + 1]), [res("wst%d" % b), res("gA")], [res("wbf%d" % b)])
                S.dma(dstream("ws%d" % b), win_v[:, :, k, :], wbf[:, b], reads=[res("wbf%d" % b)],
                      writes=[scr_res["win"]])

            wqf = sb2("wqf", [128, 2, 768], F32)
            wqb_bf = sb2("wqb_bf", [128, 2, 768], BF16)
            wkvf = sb2("wkvf", [128, 1024], F32)
            wkvb_bf = sb2("wkvb_bf", [128, 1024], BF16)
            wkT = sb2("wkT", [128, 4, 128], BF16)
            wvT = sb2("wvT", [128, 4, 128], BF16)
            wqnT = sb2("wqnT", [128, 4, 2, 128], BF16)
            S.dma(dstream("k12"), wqf[:], w_qb.rearrange("(c p) n -> p c n", p=128), writes=[res("wqf")])
            S.dma(dstream("k13"), wkvf[:], w_kvb[:], writes=[res("wkvf")])
            for c in range(2):
                S.op(dve, lambda c=c: nc.vector.tensor_scalar(out=wqb_bf[:, c, :], in0=wqf[:, c, :],
                                                              scalar1=gQ[:, c:c + 1], scalar2=None, op0=ALU.mult),
                     [res("wqf"), res("gQ")], [res("wqb_bf")])
            S.op(pool, lambda: nc.gpsimd.tensor_copy(out=wkvb_bf[:], in_=wkvf[:]), [res("wkvf")], [res("wkvb_bf")])
            wqb_v = wqb_bf[:].rearrange("p c (h n) -> p c h n", h=4)
            S.op(dve, lambda: nc.vector.tensor_copy(out=wqr[:], in_=wqb_v[:, :, :, 128:192]),
                 [res("wqb_bf")], [res("wqr")])
            for h in range(4):
                tr(pT[:, h * 128:(h + 1) * 128], wkvb_bf[:, h * 256:h * 256 + 128], [res("wkvb_bf")], [pTr])
                tr(pT[:, 512 + h * 128:512 + (h + 1) * 128], wkvb_bf[:, h * 256 + 128:h * 256 + 256],
                   [res("wkvb_bf")], [pTr])
            S.op(dve, lambda: nc.vector.tensor_copy(out=wkT[:], in_=pT[:, 0:512].rearrange("p (h l) -> p h l", h=4)),
                 [pTr], [pTr, res("wkT")])
            S.op(dve, lambda: nc.vector.tensor_copy(out=wvT[:], in_=pT[:, 512:1024].rearrange("p (h l) -> p h l", h=4)),
                 [pTr], [pTr, res("wvT")])
            for h in range(4):
                for c in range(2):
                    tr(pT[:, (h * 2 + c) * 128:(h * 2 + c + 1) * 128], wqb_v[:, c, h, 0:128], [res("wqb_bf")], [pTr])
            S.op(dve, lambda: nc.vector.tensor_copy(
                out=wqnT[:], in_=pT[:, 0:1024].rearrange("p (h c q) -> p h c q", h=4, c=2)),
                [pTr], [pTr, res("wqnT")])
            for c in range(2):
                for h in range(4):
                    mm(banks[c][:, h * 128:(h + 1) * 128], wqnT[:, h, c, :], wkT[:, h, :], True, True,
                       [res("wqnT"), res("wkT")], [bres[c]])
                S.op(dve, lambda c=c: nc.vector.tensor_copy(
                    out=wq_eff[:, c], in_=banks[c][:, 0:512].rearrange("p (h l) -> p h l", h=4)),
                    [], [bres[c], res("wq_eff")])

            wof = sb2("wof", [128, 2, 1024], F32)
            wob = sb2("wob", [128, 2, 1024], BF16)
            woe = sb2("woe", [128, 8, 1024], BF16)
            for c in range(8):
                b = c % 2
                S.dma(dstream("w%d" % b), wof[:, b, :], w_out[c * 128:(c + 1) * 128, :], writes=[res("wof%d" % b)])
                if c < 4:
                    S.op(dve, lambda c=c, b=b: nc.vector.tensor_scalar(
                        out=woe[:, c, :], in0=wof[:, b, :], scalar1=gS[:, 0:1], scalar2=None, op0=ALU.mult),
                        [res("wof%d" % b), res("gS")], [res("woe")])
                else:
                    h = c - 4
                    S.op(pool, lambda b=b: nc.gpsimd.tensor_copy(out=wob[:, b, :], in_=wof[:, b, :]),
                         [res("wof%d" % b)], [res("wob%d" % b)])
                    for half in range(2):
                        mm(banks[half][:, 0:512], wvT[:, h, :], wob[:, b, half * 512:(half + 1) * 512], True, True,
                           [res("wvT"), res("wob%d" % b)], [bres[half]])
                        S.op(act, lambda c=c, half=half: nc.scalar.copy(
                            out=woe[:, c, half * 512:(half + 1) * 512], in_=banks[half][:, 0:512]),
                            [], [bres[half], res("woe")])
            for half in range(2):
                S.dma(dstream("ws%d" % half), wout_s[half], woe[:, :, half * 512:(half + 1) * 512],
                      reads=[res("woe")], writes=[scr_res["wout"]])

            NB = 5
            fst = sb2("fst", [128, NB, 8, 2, 128], F32)
            fdn = sb2("fdn", [128, NB, 1024], F32)
            fbf = sb2("fbf", [128, NB, 3072], BF16)
            w_up_v = w_up.rearrange("(k p) (u j c) -> p k u j c", p=128, u=2, c=128)
            def ffn_load(j):
                b = j % NB
                for u in range(2):
                    S.dma(dstream("f%d" % b), fst[:, b, :, u, :], w_up_v[:, :, u, j, :],
                          writes=[res("fst%d" % b)])
                S.dma(dstream("fd%d" % b), fdn[:, b, :], w_down[j * 128:(j + 1) * 128, :], writes=[res("fdn%d" % b)])
            for j in range(NB - 2):
                ffn_load(j)
            for j in range(NCH):
                b = j % NB
                if j + NB - 2 < NCH:
                    ffn_load(j + NB - 2)
                S.op(dve, lambda b=b: nc.vector.tensor_tensor(
                    out=fbf[:, b, 0:2048].rearrange("p (k x) -> p k x", k=8),
                    in0=fst[:, b].rearrange("p k u c -> p k (u c)"),
                    in1=gF[:, :].unsqueeze(2).to_broadcast([128, 8, 256]), op=ALU.mult),
                    [res("fst%d" % b), res("gF")], [res("fbfu%d" % b)])
                S.op(act, lambda b=b: nc.scalar.copy(out=fbf[:, b, 2048:3072], in_=fdn[:, b, :]),
                     [res("fdn%d" % b)], [res("fbfd%d" % b)])
                S.dma(dstream("fs%d" % b), ffn_s[j], fbf[:, b, :], reads=[res("fbfu%d" % b), res("fbfd%d" % b)],
                      writes=[scr_res["ffn"]])
            S.barrier()

        KT = sb("KT", [128, 4, NKT * 128], BF16)
        VX = sb("VX", [128, NKT, 4, 130], BF16)
        CT = sb("CT", [128, NKT * 128], BF16)
        RT = sb("RT", [128, NKT * 128], BF16)
        CX = sb("CX", [128, NKT, 130], BF16)
        ring = sb("ring", [128, RING_SLOTS, RING_W], BF16)
        ringD = sb("ringD", [128, RINGD_SLOTS, 1024], BF16)
        xg = sb("xg", [128, 2, 2, D], F32)
        xn = sb("xn", [128, D], BF16)
        xnT = sb("xnT", [128, 8, 256], BF16)
        mixT = xnT
        h2T = xnT
        qf = sb("qf", [128, 1, 512], F32)
        ra = sb("ra", [128, 1, 512], F32)
        rb = sb("rb", [128, 1, 512], F32)
        qrb = sb("qrb", [128, 2, 512], BF16)
        kout = sb("kout", [128, 1, 512], F32)
        vout = sb("vout", [128, 1, 512], F32)
        mf = qf
        cko = sb("cko", [128, 1, 128], F32)
        kro = sb("kro", [128, 2, 64], F32)
        krd = sb("krd", [128, 2, 128], BF16)
        cqn = sb("cqn", [128, 2, 256], BF16)
        cqT = sb("cqT", [128, 2, 256], BF16)
        QT = sb("QT", [128, 4, 256], BF16)
        QpT = sb("QpT", [128, 4, 256], BF16)
        QRT = sb("QRT", [128, 2, 256], BF16)
        PT = sb("PT", [128, 2, 2, 2, 256], BF16)
        st_small = sb("st_small", [128, 64], F32)
        t0 = sb("t0", [128, 2, 128], F32)
        od = sb("od", [128, 2, 128], F32)
        gbuf = sb("gbuf", [128, 2, 258], F32)
        cbuf = sb("cbuf", [128, 2, 256], F32)
        actT = sb("actT", [128, 3, 256], BF16)
        gstate = sb("gstate", [128, NCH, 2], F32)
        ub = sb("ub", [128, 2, 256], F32)

        xn2 = sb("xn2", [128, D], BF16)
        h2T = sb("h2T_", [128, 8, 256], BF16)
        osb = sb("osb", [128, 2, 2, 129], F32)
        onall = sb("onall", [128, 2, 2, 2, 128], BF16)

        S.op(pool, lambda: nc.gpsimd.memset(VX[:, :, :, 128:130], 1.0), [], [res("VXones")])
        S.op(pool, lambda: nc.gpsimd.memset(CX[:, :, 128:130], 1.0), [], [res("CXones")])
        mhalf = sb("mhalf", [128, 1], F32)
        S.op(pool, lambda: nc.gpsimd.memset(mhalf[:], -0.5), [], [res("mhalf")])
        S.barrier()

        statcache = {}
        rstdcache = {}

        def statc(name, width=1):
            if name not in statcache:
                statcache[name] = stat(name, width)
            return statcache[name]

        def rstd_cached(nm, ssq_ap, ssq_res, n_feat, n):
            if nm not in rstdcache:
                rstdcache[nm] = (statc(nm + "_ms"), statc(nm + "_rs"))
            (ms, msr), (rs, rsr) = rstdcache[nm]
            S.op(dve, lambda: nc.vector.tensor_scalar(out=ms[0:n], in0=ssq_ap[0:n], scalar1=1.0 / n_feat,
                                                      scalar2=EPS, op0=ALU.mult, op1=ALU.add), [ssq_res], [msr])
            S.op(pool, lambda: nc.gpsimd.tensor_tensor(out=rs[0:n], in0=ms[0:n], in1=mhalf[0:n], op=ALU.pow),
                 [msr, res("mhalf")], [rsr])
            return rs, rsr

        real_S = S

        class DrySched:
            dry = True

            def __init__(self):
                class _E:
                    def __init__(self, n):
                        self.name = n
                self.pe, self.act, self.dve, self.pool = _E("pe"), _E("act"), _E("dve"), _E("pool")

            def op(self, *a, **k):
                return None

            def dma(self, *a, **k):
                return None

            def barrier(self):
                return None

            def dma_stream(self, name):
                return None

        class Group:
            pass

        groups = []
        for sq_i in range(n_prompt):
            ng = seq // 256
            for g in range(ng):
                G = Group()
                G.x_ap, G.tsz, G.tpg, G.g, G.ng, G.npast, G.tab0 = xp[sq_i], 128, 2, g, ng, 0, 0
                G.outs = {"y": yp[sq_i], "dk": dkp[sq_i], "dv": dvp[sq_i], "ckv": ckvp[sq_i], "kr": krp[sq_i],
                          "conv": convp[sq_i]}
                G.conv_init = None
                G.sample = False
                groups.append(G)
        if sample:
            G = Group()
            G.x_ap, G.tsz, G.tpg, G.g, G.ng, G.npast, G.tab0 = xs, dec, 1, 0, 1, past // 128, NT
            G.outs = {"y": ys, "dk": dks, "dv": dvs, "ckv": ckvs, "kr": krs, "conv": convs}
            G.conv_init = sconv
            G.sample = True
            groups.append(G)
        for f_, G in enumerate(groups):
            G.f = f_
            G.slot = f_ % 2
            G.nq = G.tsz * G.tpg

        def emit_main(plan):
            nonlocal S, pe, act, dve, pool
            dry = plan is None
            if dry:
                S = DrySched()
            else:
                S = real_S
            pe, act, dve, pool = S.pe, S.act, S.dve, S.pool
            RINGS = {"M": (ring, RING_SLOTS), "D": (ringD, RINGD_SLOTS)}
            ring_log = {"M": [], "D": []}
            ring_res = {k: [Res("ring%s%d" % (k, i)) for i in range(v[1])] for k, v in RINGS.items()}
            rstate = {k: {"loaded": 0, "next": 0, "released": set()} for k in RINGS}
            kvres = {}

            def kvr(buf, kt):
                key = (buf, kt)
                if key not in kvres:
                    kvres[key] = Res("%s%d" % (buf, kt))
                return kvres[key]

            def ring_src(spec):
                kind, a, b = spec
                if kind == "win":
                    return win_s[a][:, b * 4:(b + 1) * 4, :], 2048, scr_res["win"]
                if kind == "wout":
                    return wout_s[a][:, b * 4:(b + 1) * 4, :], 2048, scr_res["wout"]
                if kind == "up":
                    return ffn_s[a][:, 0:2048], 2048, scr_res["ffn"]
                return ffn_s[a][:, 2048:3072], 1024, scr_res["ffn"]

            def ring_prefetch():
                if dry:
                    return
                progress = True
                while progress:
                    progress = False
                    for k in ("M", "D"):
                        rs_, (rt, nsl), pl = rstate[k], RINGS[k], plan[k]
                        u = rs_["loaded"]
                        if u >= len(pl):
                            continue
                        if u >= nsl and (u - nsl) not in rs_["released"]:
                            continue
                        if u > rs_["next"] + nsl - 1:
                            continue
                        ap, wd, sres = ring_src(pl[u])
                        slot = u % nsl
                        if pl[u][0] in ("win", "wout"):
                            dst = rt[:, slot, 0:2048].rearrange("p (k n) -> p k n", k=4)
                        else:
                            dst = rt[:, slot, 0:wd]
                        S.dma(dstream("ring%s%d" % (k, slot)), dst, ap, reads=[sres], writes=[ring_res[k][slot]])
                        rs_["loaded"] += 1
                        progress = True

            def ring_next(spec):
                k = "D" if spec[0] == "down" else "M"
                rs_, (rt, nsl) = rstate[k], RINGS[k]
                u = rs_["next"]
                ring_log[k].append(spec)
                if not dry:
                    assert plan[k][u] == spec, (k, u, plan[k][u], spec)
                rs_["next"] += 1
                ring_prefetch()
                if not dry:
                    assert rs_["loaded"] > u, "ring %s deadlock: unit %d not loadable" % (k, u)
                slot = u % nsl
                return (k, u), rt[:, slot, :], ring_res[k][slot]

            def ring_release(h):
                rstate[h[0]]["released"].add(h[1])
                ring_prefetch()

            def rope(src, n, nh, tab_i, out_ap, reads, out_res_list, e1, e2):
                xv = src.rearrange("p (h t j) -> p h t j", h=nh, t=2)
                ov = out_ap.rearrange("p (h t j) -> p h t j", h=nh, t=2)
                av = ra[0:n, 0, 0:nh * 64].rearrange("p (h t j) -> p h t j", h=nh, t=2)
                bv = rb[0:n, 0, 0:nh * 64].rearrange("p (h t j) -> p h t j", h=nh, t=2)
                cosb = cos_t[0:n, tab_i, :].unsqueeze(1).unsqueeze(1).to_broadcast([n, nh, 2, 32])
                sinb = sin_t[0:n, tab_i, :].unsqueeze(1).to_broadcast([n, nh, 32])
                rar, rbr = res("ra0"), res("rb0")
                S.op(e1, lambda: eng_of(e1).tensor_tensor(out=av, in0=xv, in1=cosb, op=ALU.mult),
                     list(reads) + [res("tab")], [rar])
                S.op(e2, lambda: eng_of(e2).tensor_tensor(out=bv[:, :, 0, :], in0=xv[:, :, 1, :], in1=sinb,
                                                          op=ALU.mult), list(reads) + [res("tab")], [rbr])
                S.op(e2, lambda: eng_of(e2).tensor_tensor(out=bv[:, :, 1, :], in0=xv[:, :, 0, :], in1=sinb,
                                                          op=ALU.mult), list(reads) + [res("tab")], [rbr])
                S.op(e1, lambda: eng_of(e1).tensor_tensor(out=ov[:, :, 0, :], in0=av[:, :, 0, :], in1=bv[:, :, 0, :],
                                                          op=ALU.subtract), [rar, rbr], out_res_list)
                S.op(e1, lambda: eng_of(e1).tensor_tensor(out=ov[:, :, 1, :], in0=av[:, :, 1, :], in1=bv[:, :, 1, :],
                                                          op=ALU.add), [rar, rbr], out_res_list)

            def load_x(G):
                for tl in range(G.tpg):
                    ti = G.g * G.tpg + tl
                    S.dma(dstream("x%d%d" % (G.slot, tl)), xg[0:G.tsz, G.slot, tl, :],
                          G.x_ap[ti * G.tsz:(ti + 1) * G.tsz, :], writes=[res("xg%d%d" % (G.slot, tl))])

            PA = 7

            def phaseA_stages(G):
                n, tpg, nq = G.tsz, G.tpg, G.nq
                stages = []
                xnb = [xn, xn2]

                if G.sample:
                    for kt in range(past // 128):
                        def st_past(kt=kt):
                            r0 = kt * 128
                            S.dma(dstream("c0"), qf[:, 0, :], cdk[r0:r0 + 128, :], writes=[res("qf0")])
                            S.dma(dstream("c1"), ra[:, 0, :], cdv[r0:r0 + 128, :], writes=[res("ra0")])
                            S.dma(dstream("c2"), rb[:, 0, 0:128], cckv[r0:r0 + 128, :], writes=[res("rb0")])
                            S.dma(dstream("c2"), rb[:, 0, 128:192], ckr[r0:r0 + 128, :], writes=[res("rb0")])
                            S.op(pool, lambda: nc.gpsimd.tensor_copy(out=qrb[:, 0, :], in_=qf[:, 0, :]),
                                 [res("qf0")], [res("qrb0")])
                            for h in range(4):
                                tr(pT[:, h * 128:(h + 1) * 128], qrb[:, 0, h * 128:(h + 1) * 128], [res("qrb0")], [pTr])
                            S.op(dve, lambda: nc.vector.tensor_copy(
                                out=KT[:, :, r0:r0 + 128], in_=pT[:, 0:512].rearrange("p (h t) -> p h t", h=4)),
                                [], [pTr, kvr("KT", kt)])
                            S.op(pool, lambda: nc.gpsimd.tensor_copy(
                                out=VX[:, kt, :, 0:128], in_=ra[:, 0, :].rearrange("p (h e) -> p h e", h=4)),
                                [res("ra0")], [kvr("VX", kt)])
                            S.op(pool, lambda: nc.gpsimd.tensor_copy(out=CX[:, kt, 0:128], in_=rb[:, 0, 0:128]),
                                 [res("rb0")], [kvr("CX", kt)])
                            for dd in range(2):
                                S.op(pool, lambda dd=dd: nc.gpsimd.tensor_copy(
                                    out=krd[:, 0, dd * 64:(dd + 1) * 64], in_=rb[:, 0, 128:192]),
                                    [res("rb0")], [res("krd0")])
                            tr(pT[:, 512:640], CX[:, kt, 0:128], [kvr("CX", kt)], [pTr])
                            tr(pT[:, 640:768], krd[:, 0, :], [res("krd0")], [pTr])
                            S.op(dve, lambda: nc.vector.tensor_copy(out=CT[:, r0:r0 + 128], in_=pT[:, 512:640]),
                                 [], [pTr, kvr("CT", kt)])
                            S.op(dve, lambda: nc.vector.tensor_copy(out=RT[:, r0:r0 + 128], in_=pT[:, 640:768]),
                                 [], [pTr, kvr("RT", kt)])
                        stages.append(st_past)

                def st_norm():
                    for tl in range(tpg):
                        xr = res("xg%d%d" % (G.slot, tl))
                        xt = xg[0:n, G.slot, tl, :]
                        xb = xnb[tl]
                        ssq, ssqr = statc("a_ssq%d" % tl)
                        S.op(act, lambda: nc.scalar.activation(out=xb[0:n, :], in_=xt, func=AF.Square,
                                                               accum_out=ssq[0:n]), [xr], [res("xn%d" % tl), ssqr])
                        rs, rsr = rstd_cached("a%d" % tl, ssq, ssqr, D, n)
                        S.op(dve, lambda: nc.vector.tensor_scalar(out=xb[0:n, :], in0=xt, scalar1=rs[0:n],
                                                                  scalar2=None, op0=ALU.mult),
                             [xr, rsr], [res("xn%d" % tl)])
                stages.append(st_norm)
                stages.extend([None, None])

                for tl in range(tpg):
                    def st_xT(tl=tl):
                        xb = xnb[tl]
                        for k in range(8):
                            tr(pT[:, k * 128:k * 128 + n], xb[0:n, k * 128:(k + 1) * 128], [res("xn%d" % tl)], [pTr])
                        S.op(dve, lambda: nc.vector.tensor_copy(
                            out=xnT[:, :, tl * 128:tl * 128 + n],
                            in_=pT[:, :].rearrange("p (k t) -> p k t", k=8)[:, :, 0:n]), [],
                            [pTr, res("xnT%d" % tl)] + [res("mixT%d" % c) for c in range(8)])
                    stages.append(st_xT)

                pending_T = []
                stage_no = [0]

                def flush_T(all_=False):
                    stage_no[0] += 1
                    while pending_T and (all_ or pending_T[0][0] <= stage_no[0] - tpg):
                        pending_T.pop(0)[1]()

                def defer_T(fn):
                    pending_T.append((stage_no[0], fn))

                for s_ in (3, 0, 1, 2):
                    wd = 512 if s_ < 3 else 448
                    units = {}
                    for tl in range(tpg):
                        def st_seg(s_=s_, tl=tl, wd=wd, units=units):
                            flush_T()
                            ti = G.g * tpg + tl
                            kt = G.npast + ti
                            tok0 = kt * 128
                            tab_i = G.tab0 + ti
                            if tl == 0:
                                for hk in range(2):
                                    units[hk] = ring_next(("win", s_, hk))
                            for k in range(8):
                                u_, unit, ur = units[k // 4]
                                uv = unit[:, 0:2048].rearrange("p (k n) -> p k n", k=4)
                                mm(banks[PA][0:n, 0:wd], xnT[:, k, tl * 128:tl * 128 + n], uv[:, k % 4, 0:wd],
                                   k == 0, k == 7, [res("xnT%d" % tl), ur], [bres[PA]])
                            if tl == tpg - 1:
                                for hk in range(2):
                                    ring_release(units[hk][0])
                            bk = PA
                            if s_ == 0:
                                copy(act, qf[0:n, 0, :], banks[bk][0:n, 0:512], [], [bres[bk], res("qf0")])
                                rope(qf[0:n, 0, :], n, 8, tab_i, qrb[0:n, tl, :], [res("qf0")],
                                     [res("qrb%d" % tl)], dve, pool)

                                def tq(tl=tl):
                                    for h in range(4):
                                        tr(pT[:, h * 128:h * 128 + n], qrb[0:n, tl, h * 128:(h + 1) * 128],
                                           [res("qrb%d" % tl)], [pTr])
                                    S.op(dve, lambda: nc.vector.tensor_copy(
                                        out=QT[:, :, tl * 128:tl * 128 + n],
                                        in_=pT[:, 0:512].rearrange("p (h t) -> p h t", h=4)[:, :, 0:n]),
                                        [], [pTr, res("QT")])
                                defer_T(tq)
                            elif s_ == 1:
                                copy(act, qf[0:n, 0, :], banks[bk][0:n, 0:512], [], [bres[bk], res("qf0")])
                                rope(qf[0:n, 0, :], n, 8, tab_i, kout[0:n, 0, :], [res("qf0")],
                                     [res("kout0")], dve, pool)
                                S.dma(dstream("ko0"), G.outs["dk"][ti * n:(ti + 1) * n, :], kout[0:n, 0, :],
                                      reads=[res("kout0")], writes=[out_res])
                                S.op(pool, lambda: nc.gpsimd.tensor_copy(out=qrb[0:n, tl, :], in_=kout[0:n, 0, :]),
                                     [res("kout0")], [res("qrb%d" % tl)])

                                def tk(tl=tl, kt=kt, tok0=tok0):
                                    for h in range(4):
                                        tr(pT[:, h * 128:h * 128 + n], qrb[0:n, tl, h * 128:(h + 1) * 128],
                                           [res("qrb%d" % tl)], [pTr])
                                    S.op(dve, lambda: nc.vector.tensor_copy(
                                        out=KT[:, :, tok0:tok0 + n],
                                        in_=pT[:, 0:512].rearrange("p (h t) -> p h t", h=4)[:, :, 0:n]),
                                        [], [pTr, kvr("KT", kt)])
                                defer_T(tk)
                            elif s_ == 2:
                                copy(act, vout[0:n, 0, :], banks[bk][0:n, 0:512], [], [bres[bk], res("vout0")])
                                S.dma(dstream("vo0"), G.outs["dv"][ti * n:(ti + 1) * n, :], vout[0:n, 0, :],
                                      reads=[res("vout0")], writes=[out_res])
                                S.op(pool, lambda: nc.gpsimd.tensor_copy(
                                    out=VX[0:n, kt, :, 0:128],
                                    in_=vout[0:n, 0, :].rearrange("p (h e) -> p h e", h=4)),
                                    [res("vout0")], [kvr("VX", kt)])
                            else:
                                copy(act, mf[0:n, 0, 0:448], banks[bk][0:n, 0:448], [], [bres[bk], res("qf0")])
                                mfr = res("qf0")
                                sq, sqr = statc("cq_ssq")
                                S.op(dve, lambda: nc.vector.scalar_tensor_tensor(
                                    out=ra[0:n, 0, 0:256], in0=mf[0:n, 0, 0:256], scalar=1.0, in1=mf[0:n, 0, 0:256],
                                    op0=ALU.mult, op1=ALU.mult, accum_out=sq[0:n]), [mfr], [res("ra0"), sqr])
                                rs, rsr = rstd_cached("cq", sq, sqr, 256, n)
                                S.op(dve, lambda: nc.vector.tensor_scalar(
                                    out=cqn[0:n, tl, :], in0=mf[0:n, 0, 0:256], scalar1=rs[0:n], scalar2=None,
                                    op0=ALU.mult), [mfr, rsr], [res("cqn%d" % tl)])
                                sk, skr = statc("ckv_ssq")
                                S.op(dve, lambda: nc.vector.scalar_tensor_tensor(
                                    out=rb[0:n, 0, 0:128], in0=mf[0:n, 0, 256:384], scalar=1.0,
                                    in1=mf[0:n, 0, 256:384], op0=ALU.mult, op1=ALU.mult, accum_out=sk[0:n]),
                                    [mfr], [res("rb0"), skr])
                                rs2, rsr2 = rstd_cached("ckv", sk, skr, 128, n)
                                S.op(dve, lambda: nc.vector.scalar_tensor_tensor(
                                    out=cko[0:n, 0, :], in0=mf[0:n, 0, 256:384], scalar=rs2[0:n], in1=gkv_b[0:n, :],
                                    op0=ALU.mult, op1=ALU.mult), [mfr, rsr2, res("gkv")], [res("cko0")])
                                S.dma(dstream("co0"), G.outs["ckv"][ti * n:(ti + 1) * n, :], cko[0:n, 0, :],
                                      reads=[res("cko0")], writes=[out_res])
                                S.op(pool, lambda: nc.gpsimd.tensor_copy(out=CX[0:n, kt, 0:128], in_=cko[0:n, 0, :]),
                                     [res("cko0")], [kvr("CX", kt)])
                                rope(mf[0:n, 0, 384:448], n, 1, tab_i, kro[0:n, tl, :], [mfr],
                                     [res("kro%d" % tl)], dve, pool)
                                S.dma(dstream("ro%d" % tl), G.outs["kr"][ti * n:(ti + 1) * n, :], kro[0:n, tl, :],
                                      reads=[res("kro%d" % tl)], writes=[out_res])
                                for dd in range(2):
                                    S.op(pool, lambda dd=dd: nc.gpsimd.tensor_copy(
                                        out=krd[0:n, tl, dd * 64:(dd + 1) * 64], in_=kro[0:n, tl, :]),
                                        [res("kro%d" % tl)], [res("krd%d" % tl)])

                                def tm(tl=tl, kt=kt, tok0=tok0):
                                    for c in range(2):
                                        tr(pT[:, c * 128:c * 128 + n], cqn[0:n, tl, c * 128:(c + 1) * 128],
                                           [res("cqn%d" % tl)], [pTr])
                                    tr(pT[:, 256:256 + n], CX[0:n, kt, 0:128], [kvr("CX", kt)], [pTr])
                                    tr(pT[:, 384:384 + n], krd[0:n, tl, :], [res("krd%d" % tl)], [pTr])
                                    S.op(dve, lambda: nc.vector.tensor_copy(
                                        out=cqT[:, :, tl * 128:tl * 128 + n],
                                        in_=pT[:, 0:256].rearrange("p (c t) -> p c t", c=2)[:, :, 0:n]),
                                        [], [pTr, res("cqT")])
                                    S.op(dve, lambda: nc.vector.tensor_copy(out=CT[:, tok0:tok0 + n],
                                                                            in_=pT[:, 256:256 + n]),
                                         [], [pTr, kvr("CT", kt)])
                                    S.op(dve, lambda: nc.vector.tensor_copy(out=RT[:, tok0:tok0 + n],
                                                                            in_=pT[:, 384:384 + n]),
                                         [], [pTr, kvr("RT", kt)])
                                defer_T(tm)
                        stages.append(st_seg)

                for hp in range(2):
                    def st_qp(hp=hp):
                        flush_T(hp == 0)
                        for hh in range(2):
                            h = hp * 2 + hh
                            for c in range(2):
                                mm(banks[PA][:, hh * 256:hh * 256 + nq], wq_eff[:, c, h, :], cqT[:, c, 0:nq],
                                   c == 0, c == 1, [res("cqT"), res("wq_eff")], [bres[PA]])
                        S.op(dve, lambda hp=hp: nc.vector.tensor_copy(
                            out=QpT[:, hp * 2:hp * 2 + 2, 0:nq],
                            in_=banks[PA][:, :].rearrange("p (h t) -> p h t", h=2)[:, :, 0:nq]),
                            [], [bres[PA], res("QpT")])
                    stages.append(st_qp)

                for tl in range(tpg):
                    def st_qr(tl=tl):
                        flush_T()
                        ti = G.g * tpg + tl
                        tab_i = G.tab0 + ti
                        for c in range(2):
                            mm(banks[PA][0:n, 0:256], cqT[:, c, tl * 128:tl * 128 + n],
                               wqr[:, c].rearrange("p h r -> p (h r)"), c == 0, c == 1, [res("cqT"), res("wqr")],
                               [bres[PA]])
                        copy(act, qf[0:n, 0, 0:256], banks[PA][0:n, 0:256], [], [bres[PA], res("qf0")])
                        rope(qf[0:n, 0, 0:256], n, 4, tab_i, qrb[0:n, tl, 0:256], [res("qf0")],
                             [res("qrb%d" % tl)], dve, pool)

                        def tqr(tl=tl):
                            for u in range(2):
                                tr(pT[:, u * 128:u * 128 + n], qrb[0:n, tl, u * 128:(u + 1) * 128],
                                   [res("qrb%d" % tl)], [pTr])
                            S.op(dve, lambda: nc.vector.tensor_copy(
                                out=QRT[:, :, tl * 128:tl * 128 + n],
                                in_=pT[:, 0:256].rearrange("p (u t) -> p u t", u=2)[:, :, 0:n]),
                                [], [pTr, res("QRT")])
                        defer_T(tqr)
                    stages.append(st_qr)
                stages.append(flush_T)
                stages.append(lambda: flush_T(True))
                return stages

            def attention(G):
                n, tpg, nq, g, npast = G.tsz, G.tpg, G.nq, G.g, G.npast
                pairs = []
                if not G.sample:
                    for j in range(g):
                        pairs.append([(2 * j, 128), (2 * j + 1, 128)])
                    pairs.append("diag")
                else:
                    for j in range(npast // 2):
                        pairs.append([(2 * j, 128), (2 * j + 1, 128)])
                    pairs.append([(npast, n)])
                qts = [(q0, min(128, nq - q0)) for q0 in range(0, nq, 128)]
                pS = [[3, 5], [4, 6]]
                pO = [7, 0]
                deferred = []

                def make_unit(unit_i):
                    is_diff = unit_i < 4
                    scale = DIFF_SCALE if is_diff else MLA_SCALE
                    first_av = [True, True]

                    def emit_S(pi, pr):
                        buf = pi % 2
                        tl_list = pr if pr != "diag" else [(2 * g, 128), (2 * g + 1, 128)]
                        for i, (kt, nk) in enumerate(tl_list):
                            c0 = kt * 128
                            for m in range(2):
                                bk = pS[m][buf]
                                outp = banks[bk][0:nk, i * 256:i * 256 + nq]
                                if is_diff:
                                    h = unit_i
                                    mm(outp, KT[m * 64:(m + 1) * 64, h, c0:c0 + nk], QT[m * 64:(m + 1) * 64, h, 0:nq],
                                       True, True, [kvr("KT", kt), res("QT")], [bres[bk]])
                                else:
                                    h = (unit_i - 4) * 2 + m
                                    mm(outp, CT[:, c0:c0 + nk], QpT[:, h, 0:nq], True, False,
                                       [kvr("CT", kt), res("QpT")], [bres[bk]])
                                    mm(outp, RT[m * 64:(m + 1) * 64, c0:c0 + nk],
                                       QRT[m * 64:(m + 1) * 64, unit_i - 4, 0:nq], False, True,
                                       [kvr("RT", kt), res("QRT")], [bres[bk]])
                        bks = [bres[pS[0][buf]], bres[pS[1][buf]]]
                        ptrs = [res("PT0%d" % buf), res("PT1%d" % buf)]
                        b0 = (pS[0][buf] - 3) * 512
                        bv = pS_all[:, b0:b0 + 1024].rearrange("p (m i q) -> p m i q", m=2, i=2)
                        if pr == "diag":
                            regs = [(0, 64, 0, 0, 256), (64, 128, 0, 64, 256), (0, 64, 1, 128, 256),
                                    (64, 128, 1, 192, 256)]
                            for (p0, p1, i, q0, q1) in [(64, 128, 0, 0, 64), (64, 128, 1, 128, 192)]:
                                S.op(pool, lambda: nc.gpsimd.memset(PT[p0:p1, :, buf, i, q0:q1], 0.0), [], ptrs)
                            for (p0, p1, i, q0, q1) in regs:
                                S.op(act, lambda: nc.scalar.activation(
                                    out=PT[p0:p1, :, buf, i, q0:q1], in_=bv[p0:p1, :, i, q0:q1], func=AF.Exp,
                                    scale=scale), [], bks + ptrs)
                        elif len(pr) == 2:
                            S.op(act, lambda: nc.scalar.activation(
                                out=PT[:, :, buf, :, 0:nq], in_=bv[:, :, :, 0:nq], func=AF.Exp, scale=scale),
                                [], bks + ptrs)
                        else:
                            nk = pr[0][1]
                            S.op(act, lambda: nc.scalar.activation(
                                out=PT[0:nk, :, buf, 0, 0:nq], in_=bv[0:nk, :, 0, 0:nq], func=AF.Exp,
                                scale=scale), [], bks + ptrs)

                    def emit_AV(pi, pr, last):
                        buf = pi % 2
                        if pr == "diag":
                            items = [(2 * g, 128, 0, (0, 1)), (2 * g + 1, 128, 1, (1,))]
                        else:
                            items = [(kt, nk, i, tuple(range(len(qts)))) for i, (kt, nk) in enumerate(pr)]
                        for qi, (q0, nqt) in enumerate(qts):
                            bk = pO[qi]
                            ov = banks[bk][:, :].rearrange("p (m x) -> p m x", m=2)
                            for m in range(2):
                                its = [it for it in items if qi in it[3]]
                                for ii, (kt, nk, i, _q) in enumerate(its):
                                    lhsT = PT[0:nk, m, buf, i, q0:q0 + nqt]
                                    ptr_ = res("PT%d%d" % (m, buf))
                                    if is_diff:
                                        rhs = VX[0:nk, kt, unit_i, 0:129]
                                        rr = [kvr("VX", kt), res("VXones")]
                                    else:
                                        rhs = CX[0:nk, kt, 0:129]
                                        rr = [kvr("CX", kt), res("CXones")]
                                    start = first_av[qi] and m == 0
                                    if start:
                                        first_av[qi] = False
                                    mm(ov[0:nqt, m, 0:129], lhsT, rhs, start, last and ii == len(its) - 1,
                                       [ptr_] + rr, [bres[bk]])

                    def finish_unit():
                        def each_q(fn):
                            for qi, (q0, nqt) in enumerate(qts):
                                fn(qi, nqt)

                        def e_copy(qi, nqt):
                            bk = pO[qi]
                            ov = banks[bk][:, :].rearrange("p (m x) -> p m x", m=2)
                            S.op(dve, lambda: nc.vector.tensor_copy(out=osb[0:nqt, qi, :, :], in_=ov[0:nqt, :, 0:129]),
                                 [], [bres[bk], res("osb%d" % qi)])
                        each_q(e_copy)

                        def e_recip(qi, nqt):
                            rsum, rsumr = statc("rsum%d" % qi, 2)
                            S.op(dve, lambda: nc.vector.reciprocal(out=rsum[0:nqt, 0:2], in_=osb[0:nqt, qi, :, 128]),
                                 [res("osb%d" % qi)], [rsumr])
                        each_q(e_recip)
                        if is_diff:
                            def e_r1(qi, nqt):
                                rsum, rsumr = statc("rsum%d" % qi, 2)
                                r1, r1r = statc("r1_%d" % qi)
                                S.op(dve, lambda: nc.vector.tensor_tensor(out=r1[0:nqt], in0=rsum[0:nqt, 1:2],
                                                                          in1=neg_lam[0:nqt], op=ALU.mult),
                                     [rsumr, res("neg_lam")], [r1r])
                            each_q(e_r1)

                            def e_t0(qi, nqt):
                                rsum, rsumr = statc("rsum%d" % qi, 2)
                                S.op(dve, lambda: nc.vector.tensor_scalar(
                                    out=t0[0:nqt, qi, :], in0=osb[0:nqt, qi, 0, 0:128], scalar1=rsum[0:nqt, 0:1],
                                    scalar2=None, op0=ALU.mult), [rsumr, res("osb%d" % qi)], [res("t0%d" % qi)])
                            each_q(e_t0)

                            def e_od(qi, nqt):
                                r1, r1r = statc("r1_%d" % qi)
                                S.op(dve, lambda: nc.vector.scalar_tensor_tensor(
                                    out=od[0:nqt, qi, :], in0=osb[0:nqt, qi, 1, 0:128], scalar=r1[0:nqt],
                                    in1=t0[0:nqt, qi, :], op0=ALU.mult, op1=ALU.add),
                                    [r1r, res("osb%d" % qi), res("t0%d" % qi)], [res("od%d" % qi)])
                            each_q(e_od)

                            def e_ssq(qi, nqt):
                                sq, sqr = statc("od_ssq%d" % qi)
                                S.op(dve, lambda: nc.vector.scalar_tensor_tensor(
                                    out=t0[0:nqt, qi, :], in0=od[0:nqt, qi, :], scalar=1.0, in1=od[0:nqt, qi, :],
                                    op0=ALU.mult, op1=ALU.mult, accum_out=sq[0:nqt]),
                                    [res("od%d" % qi)], [res("t0%d" % qi), sqr])
                            each_q(e_ssq)
                            rss = {}

                            def e_rstd(qi, nqt):
                                sq, sqr = statc("od_ssq%d" % qi)
                                rss[qi] = rstd_cached("od%d" % qi, sq, sqr, 128, nqt)
                            each_q(e_rstd)

                            def e_on(qi, nqt):
                                rs, rsr = rss[qi]
                                S.op(dve, lambda: nc.vector.tensor_scalar(
                                    out=onall[0:nqt, qi, unit_i % 2, 0, :], in0=od[0:nqt, qi, :], scalar1=rs[0:nqt],
                                    scalar2=None, op0=ALU.mult), [res("od%d" % qi), rsr],
                                    [res("onall%d_0" % (unit_i % 2))])
                            each_q(e_on)
                        else:
                            for m in range(2):
                                def e_onm(qi, nqt, m=m):
                                    rsum, rsumr = statc("rsum%d" % qi, 2)
                                    S.op(dve, lambda: nc.vector.tensor_scalar(
                                        out=onall[0:nqt, qi, unit_i % 2, m, :], in0=osb[0:nqt, qi, m, 0:128],
                                        scalar1=rsum[0:nqt, m:m + 1], scalar2=None, op0=ALU.mult),
                                        [rsumr, res("osb%d" % qi)], [res("onall%d_%d" % (unit_i % 2, m))])
                                each_q(e_onm)

                        def tail(unit_i=unit_i, is_diff=is_diff):
                            cs = [unit_i] if is_diff else [4 + (unit_i - 4) * 2, 5 + (unit_i - 4) * 2]
                            for ci, c in enumerate(cs):
                                for qi, (q0, nqt) in enumerate(qts):
                                    tr(pT[:, (ci * 2 + qi) * 128:(ci * 2 + qi) * 128 + nqt],
                                       onall[0:nqt, qi, unit_i % 2, ci, :], [res("onall%d_%d" % (unit_i % 2, ci))], [pTr])
                            S.op(dve, lambda: nc.vector.tensor_copy(
                                out=mixT[:, cs[0]:cs[0] + len(cs), 0:nq],
                                in_=pT[:, 0:512].rearrange("p (c t) -> p c t", c=2)[:, 0:len(cs), 0:nq]),
                                [], [pTr, res("xnT0"), res("xnT1")] + [res("mixT%d" % c) for c in cs])
                        deferred.append(tail)
                    return emit_S, emit_AV, finish_unit

                unit_fns = [make_unit(u) for u in range(6)]
                np_ = len(pairs)
                steps = [(u, pr) for u in range(6) for pr in pairs]

                def run_av(pd):
                    pu, psi, ppr, plast = pd
                    unit_fns[pu][1](psi, ppr, plast)
                    if plast:
                        while deferred:
                            deferred.pop(0)()
                        unit_fns[pu][2]()
                pend = None
                for si, (u, pr) in enumerate(steps):
                    unit_fns[u][0](si, pr)
                    if pend is not None:
                        run_av(pend)
                    pend = (u, si, pr, si % np_ == np_ - 1)
                run_av(pend)
                while deferred:
                    deferred.pop(0)()

            def outproj(G):
                n, tpg = G.tsz, G.tpg
                mixr = [res("mixT%d" % c) for c in range(8)] + [res("xnT0"), res("xnT1")]
                us = {}
                for half in range(2):
                    for hk in range(2):
                        us[(half, hk)] = ring_next(("wout", half, hk))
                for tl in range(tpg):
                    xr = res("xg%d%d" % (G.slot, tl))
                    for half in range(2):
                        bk = half
                        for c in range(8):
                            u_, unit, ur = us[(half, c // 4)]
                            uv = unit[:, 0:2048].rearrange("p (k n) -> p k n", k=4)
                            mm(banks[bk][0:n, 0:512], mixT[:, c, tl * 128:tl * 128 + n], uv[:, c % 4, :], c == 0,
                               c == 7, mixr + [ur], [bres[bk]])
                        S.op(dve, lambda: nc.vector.tensor_tensor(
                            out=xg[0:n, G.slot, tl, half * 512:(half + 1) * 512], in0=banks[bk][0:n, 0:512],
                            in1=xg[0:n, G.slot, tl, half * 512:(half + 1) * 512], op=ALU.add), [xr], [bres[bk], xr])
                    xt = xg[0:n, G.slot, tl, :]
                    xb = (xn, xn2)[tl]
                    ssq, ssqr = statc("f_ssq%d" % tl)
                    S.op(act, lambda: nc.scalar.activation(out=xb[0:n, :], in_=xt, func=AF.Square,
                                                           accum_out=ssq[0:n]), [xr], [res("xn%d" % tl), ssqr])
                    rs, rsr = rstd_cached("f%d" % tl, ssq, ssqr, D, n)
                    S.op(dve, lambda: nc.vector.tensor_scalar(out=xb[0:n, :], in0=xt, scalar1=rs[0:n], scalar2=None,
                                                              op0=ALU.mult), [xr, rsr], [res("xn%d" % tl)])
                for h_ in us.values():
                    ring_release(h_[0])

            def ffn(G, stages, depth=2):
                n, tpg, nq = G.tsz, G.tpg, G.nq
                if G.g == 0:
                    if G.conv_init is None:
                        S.op(pool, lambda: nc.gpsimd.memset(gstate[:], 0.0), [], [res("gstate")])
                    else:
                        for r_ in range(2):
                            S.dma(dstream("c0"), gstate[:, :, r_],
                                  G.conv_init[r_].rearrange("(j p) -> p j", p=128),
                                  writes=[res("gstate")], allow_slow_non_contiguous=True)
                for tl in range(tpg):
                    xb = (xn, xn2)[tl]
                    for k in range(8):
                        tr(pT[:, k * 128:k * 128 + n], xb[0:n, k * 128:(k + 1) * 128], [res("xn%d" % tl)], [pTr])
                    S.op(dve, lambda: nc.vector.tensor_copy(
                        out=h2T[:, :, tl * 128:tl * 128 + n],
                        in_=pT[:, :].rearrange("p (k t) -> p k t", k=8)[:, :, 0:n]), [], [pTr, res("h2T%d" % tl)])
                pY = [[3, 4], [5, 6]]
                pend = []
                h2r = [res("h2T%d" % tl) for tl in range(tpg)]

                def emit_down(j, b3, ud):
                    u_, unit, ur = ud
                    for tl in range(tpg):
                        for half in range(2):
                            bk = pY[tl][half]
                            mm(banks[bk][0:n, 0:512], actT[:, b3, tl * 128:tl * 128 + n],
                               unit[:, half * 512:(half + 1) * 512], j == 0, j == NCH - 1,
                               [res("actT%d_%d" % (b3, tl)), ur], [bres[bk]])
                    ring_release(u_)

                stages = list(stages)
                for j in range(NCH):
                    b = j % 2
                    uu = ring_next(("up", j, 0))
                    ud = ring_next(("down", j, 0))
                    unit = uu[1]
                    uvv = unit[:, 0:2048].rearrange("p (k u c) -> p k u c", k=8, u=2)
                    pv = banks[b][:, :].rearrange("p (u t) -> p u t", u=2)
                    for u in range(2):
                        for k in range(8):
                            mm(pv[:, u, 0:nq], uvv[:, k, u, :], h2T[:, k, 0:nq], k == 0, k == 7,
                               h2r + [uu[2]], [bres[b]])
                    ring_release(uu[0])
                    if len(pend) == depth:
                        emit_down(*pend.pop(0))
                    b3 = j % 3
                    gb = gbuf[:, b, :]
                    gbr = res("gbuf%d" % b)
                    ubr = res("ub%d" % b)
                    copy(act, gb[:, 2:2 + nq], pv[:, 1, 0:nq], [], [bres[b], gbr])
                    copy(act, ub[:, b, 0:nq], pv[:, 0, 0:nq], [], [bres[b], ubr])
                    S.op(pool, lambda: nc.gpsimd.tensor_copy(out=gb[:, 0:2], in_=gstate[:, j, :]),
                         [res("gstate")], [gbr])
                    S.op(pool, lambda: nc.gpsimd.tensor_copy(out=gstate[:, j, :], in_=gb[:, nq:nq + 2]),
                         [gbr], [res("gstate")])
                    halves = [(0, nq)] if nq <= 128 else [(0, 128), (128, nq)]
                    hres = [res("cbuf%d_%d" % (b, hi)) for hi in range(len(halves))]
                    for hi, (h0, h1) in enumerate(halves):
                        S.op(dve, lambda: nc.vector.tensor_scalar(
                            out=cbuf[:, b, h0:h1], in0=gb[:, h0:h1], scalar1=wconv[:, j, 0:1],
                            scalar2=bconv[:, j:j + 1], op0=ALU.mult, op1=ALU.add), [gbr, res("wconv")], [hres[hi]])
                    for hi, (h0, h1) in enumerate(halves):
                        S.op(dve, lambda: nc.vector.scalar_tensor_tensor(
                            out=cbuf[:, b, h0:h1], in0=gb[:, 1 + h0:1 + h1], scalar=wconv[:, j, 1:2],
                            in1=cbuf[:, b, h0:h1], op0=ALU.mult, op1=ALU.add), [gbr, res("wconv"), hres[hi]],
                            [hres[hi]])
                    for hi, (h0, h1) in enumerate(halves):
                        S.op(dve, lambda: nc.vector.scalar_tensor_tensor(
                            out=cbuf[:, b, h0:h1], in0=gb[:, 2 + h0:2 + h1], scalar=wconv[:, j, 2:3],
                            in1=cbuf[:, b, h0:h1], op0=ALU.mult, op1=ALU.add), [gbr, res("wconv"), hres[hi]],
                            [hres[hi]])
                    for hi, (h0, h1) in enumerate(halves):
                        S.op(act, lambda: nc.scalar.activation(out=cbuf[:, b, h0:h1], in_=cbuf[:, b, h0:h1],
                                                               func=AF.Silu), [hres[hi]], [hres[hi]])
                    for hi, (h0, h1) in enumerate(halves):
                        S.op(dve, lambda: nc.vector.tensor_tensor(out=actT[:, b3, h0:h1], in0=ub[:, b, h0:h1],
                                                                  in1=cbuf[:, b, h0:h1], op=ALU.mult),
                             [hres[hi], ubr], [res("actT%d_%d" % (b3, hi))])
                    pend.append((j, b3, ud))
                    if stages and j >= 1:
                        stg = stages.pop(0)
                        if stg is not None:
                            stg()
                while pend:
                    emit_down(*pend.pop(0))
                while stages:
                    stg = stages.pop(0)
                    if stg is not None:
                        stg()

                for tl in range(tpg):
                    ti = G.g * tpg + tl
                    xr = res("xg%d%d" % (G.slot, tl))
                    for half in range(2):
                        bk = pY[tl][half]
                        S.op(dve, lambda: nc.vector.tensor_tensor(
                            out=xg[0:n, G.slot, tl, half * 512:(half + 1) * 512], in0=banks[bk][0:n, 0:512],
                            in1=xg[0:n, G.slot, tl, half * 512:(half + 1) * 512], op=ALU.add), [xr], [bres[bk], xr])
                    xt = xg[0:n, G.slot, tl, :]
                    ssq, ssqr = statc("y_ssq%d" % tl)
                    xb = (xn, xn2)[tl]
                    S.op(act, lambda: nc.scalar.activation(out=xb[0:n, :], in_=xt, func=AF.Square,
                                                           accum_out=ssq[0:n]), [xr], [res("xn%d" % tl), ssqr])
                    rs, rsr = rstd_cached("y%d" % tl, ssq, ssqr, D, n)
                    S.op(dve, lambda: nc.vector.scalar_tensor_tensor(
                        out=xt, in0=xt, scalar=rs[0:n], in1=gfin_b[0:n, :], op0=ALU.mult, op1=ALU.mult),
                        [xr, rsr, res("gfin")], [xr])
                    S.dma(dstream("yo%d" % tl), G.outs["y"][ti * n:(ti + 1) * n, :], xt,
                          reads=[xr], writes=[out_res])
                if G.g == G.ng - 1:
                    for r_ in range(2):
                        S.dma(dstream("c%d" % (1 + r_)), G.outs["conv"][r_].rearrange("(j p) -> p j", p=128),
                              gstate[:, :, r_], reads=[res("gstate")], writes=[out_res],
                              allow_slow_non_contiguous=True)

            load_x(groups[0])
            if len(groups) > 1:
                load_x(groups[1])
            for stg in phaseA_stages(groups[0]):
                if stg is not None:
                    stg()
            for f_, G in enumerate(groups):
                attention(G)
                outproj(G)
                nxt = phaseA_stages(groups[f_ + 1]) if f_ + 1 < len(groups) else []
                dense = f_ + 1 < len(groups) and groups[f_ + 1].tpg == 1
                ffn(G, nxt, 1 if dense else 2)
                if f_ + 2 < len(groups):
                    load_x(groups[f_ + 2])
            S.barrier()
            return ring_log

        plan = emit_main(None)
        emit_main(plan)
    return nc


def rope_tables(seq, past, dec):
    nt = seq // 128
    half = 32
    inv = (np.float32(10000.0) ** (-np.arange(half, dtype=np.float32) * np.float32(2.0 / 64))).astype(np.float32)
    pos = np.zeros((128, nt + 1), np.float32)
    for t in range(nt):
        pos[:, t] = t * 128 + np.arange(128)
    pos[:, nt] = past + np.arange(128)
    ang = (pos[:, :, None] * inv[None, None, :]).astype(np.float32)
    return np.cos(ang).astype(np.float32), np.sin(ang).astype(np.float32)


def make_in_maps(inputs, n_cores, n_prompt, seq, past, dec):
    f = lambda a: np.ascontiguousarray(np.asarray(a, dtype=np.float32))
    cos, sin = rope_tables(seq, past, dec)
    common = {
        "g_attn_pk": f(inputs["g_attn"][0].reshape(8, 128).T),
        "g_ffn_pk": f(inputs["g_ffn"][0].reshape(8, 128).T),
        "g_q_pk": f(inputs["g_q_lora"][0].reshape(2, 128).T),
        "g_sub_p": f(inputs["g_diff_sub"][0].reshape(128, 1)),
        "g_kv": f(inputs["g_kv_lora"][0].reshape(1, 128)),
        "g_final": f(inputs["g_final"].reshape(1, D)),
        "lamv": f(np.stack([inputs["lambda_q1"][0], inputs["lambda_k1"][0], inputs["lambda_q2"][0],
                            inputs["lambda_k2"][0]], 0).reshape(1, 256)),
        "wconv_pk": f(inputs["w_conv"][0].reshape(3, NCH, 128).transpose(2, 1, 0)),
        "bconv_pk": f(inputs["b_conv"][0].reshape(NCH, 128).T),
        "w_in": f(inputs["w_in"][0]), "w_qb": f(inputs["w_q_b"][0]), "w_kvb": f(inputs["w_kv_b"][0]),
        "w_out": f(inputs["w_out"][0]), "w_up": f(inputs["w_up"][0]), "w_down": f(inputs["w_down"][0]),
        "ident": np.eye(128, dtype=np.float32).astype(ml_dtypes.bfloat16),
        "cos_t": cos, "sin_t": sin,
    }
    maps = []
    for c in range(n_cores):
        m = dict(common)
        m["xp"] = f(inputs["x_prompt"][c * n_prompt:(c + 1) * n_prompt])
        m["xs"] = f(inputs["x_sample"][c])
        m["cdk"] = f(inputs["cache_diff_k"][0, c].reshape(past, 512))
        m["cdv"] = f(inputs["cache_diff_v"][0, c].reshape(past, 512))
        m["cckv"] = f(inputs["cache_mla_ckv"][0, c])
        m["ckr"] = f(inputs["cache_mla_krope"][0, c])
        m["sconv"] = f(inputs["state_conv"][0, c])
        maps.append(m)
    return maps


_NC_CACHE = {}


def kernel(**inputs):
    inputs = {k: np.asarray(v) for k, v in inputs.items()}
    B, seq, _ = inputs["x_prompt"].shape
    n_cores = inputs["x_sample"].shape[0]
    n_prompt = B // n_cores
    dec = inputs["x_sample"].shape[1]
    past = inputs["cache_diff_k"].shape[2]
    key = (n_prompt, seq, past, dec)
    if key not in _NC_CACHE:
        _NC_CACHE[key] = build(n_prompt=n_prompt, seq=seq, sample=True, past=past, dec=dec)
    nc = _NC_CACHE[key]
    maps = make_in_maps(inputs, n_cores, n_prompt, seq, past, dec)
    res = run_bass_kernel_spmd(nc, maps, core_ids=list(range(n_cores)))
    r = res.results
    cat = lambda k: np.concatenate([np.asarray(x[k], dtype=np.float32) for x in r], axis=0)
    stk = lambda k: np.stack([np.asarray(x[k], dtype=np.float32) for x in r], axis=0)
    y_prompt = cat("yp")
    y_sample = stk("ys")
    dk_p = cat("dkp").reshape(1, B, seq, 4, 2, 64)
    dv_p = cat("dvp").reshape(1, B, seq, 4, 128)
    ckv_p = cat("ckvp").reshape(1, B, seq, 128)
    kr_p = cat("krp").reshape(1, B, seq, 64)
    conv_p = cat("convp").reshape(1, B, 2, D_FF)
    dk_s = stk("dks").reshape(1, n_cores, dec, 4, 2, 64)
    dv_s = stk("dvs").reshape(1, n_cores, dec, 4, 128)
    ckv_s = stk("ckvs").reshape(1, n_cores, dec, 128)
    kr_s = stk("krs").reshape(1, n_cores, dec, 64)
    conv_s = stk("convs").reshape(1, n_cores, 2, D_FF)
    return (y_prompt, y_sample, dk_p, dv_p, ckv_p, kr_p, conv_p, dk_s, dv_s, ckv_s, kr_s, conv_s)
```

```python
import math
from contextlib import ExitStack

import numpy as np
import ml_dtypes

import concourse.bass as bass
import concourse.mybir as mybir
from concourse.bass_utils import run_bass_kernel_spmd

F32 = mybir.dt.float32
BF16 = mybir.dt.bfloat16
ALU = mybir.AluOpType
AF = mybir.ActivationFunctionType

D = 1024
NCORES = 8
SEQ = 4096
DEC_SEQ = 64
PAST = 1024
IN_COLS = 1984
D_FF = 2816
NCH = 22
EPS = 1e-6
LAM_INIT = 0.8 - 0.6 * math.exp(-0.3 * 0)
MLA_SCALE = (128 + 64) ** -0.5
DIFF_SCALE = 64 ** -0.5
RING_SLOTS = 6
RINGD_SLOTS = 6
RING_W = 2048


class Res:
    __slots__ = ("name", "w", "r")

    def __init__(self, name):
        self.name = name
        self.w = None
        self.r = []


class Stream:
    def __init__(self, sched, name, eng, inc, limit):
        self.sched, self.name, self.eng, self.inc, self.limit = sched, name, eng, inc, limit
        self.epoch = 0
        self.sem = sched.new_sem(name + "_0")
        self.count = 0
        self.total = 0
        self.seen = {}

    def bump(self):
        self.count += 1
        self.total += 1
        ev = ((self.name, self.epoch), self.sem, self.count * self.inc, self.name)
        if self.count >= self.limit:
            self.epoch += 1
            self.sem = self.sched.new_sem("%s_%d" % (self.name, self.epoch))
            self.count = 0
        return ev


class Sched:
    def __init__(self, nc, stack):
        self.nc = nc
        self.stack = stack
        self.nsem = 0
        self.pe = Stream(self, "pe", nc.tensor, 1, 20000)
        self.act = Stream(self, "act", nc.scalar, 1, 20000)
        self.dve = Stream(self, "dve", nc.vector, 1, 20000)
        self.pool = Stream(self, "pool", nc.gpsimd, 1, 20000)
        self.sp = nc.sync
        self.sp_seen = {}
        self.dma_streams = []
        self.last_ev = {}

    def new_sem(self, name):
        self.nsem += 1
        return self.stack.enter_context(self.nc.semaphore(name))

    def dma_stream(self, name):
        s = Stream(self, name, None, 16, 2000)
        self.dma_streams.append(s)
        return s

    def _deps(self, stream_name, reads, writes, same_engine_raw=True):
        need = {}

        def add(ev, same_ok):
            if ev is None:
                return
            key, sem, val, sname = ev
            if sname == stream_name and not same_ok:
                return
            if key not in need or need[key][1] < val:
                need[key] = (sem, val)

        for r in reads:
            add(r.w, same_engine_raw)
        for w in writes:
            add(w.w, same_engine_raw)
            for ev in w.r:
                add(ev, same_engine_raw)
        return need

    @staticmethod
    def _record(ev, reads, writes):
        for r in reads:
            r.r.append(ev)
            if len(r.r) > 48:
                best = {}
                for e in r.r:
                    if e[0] not in best or best[e[0]][2] < e[2]:
                        best[e[0]] = e
                r.r = list(best.values())
        for w in writes:
            w.w = ev
            w.r = []

    def op(self, st, fn, reads=(), writes=()):
        excl = []
        if st is not self.pe:
            excl = [w for w in writes if w.name.startswith("bank")]
            if excl:
                writes = [w for w in writes if not w.name.startswith("bank")]
        need = self._deps(st.name, reads, writes, same_engine_raw=(st is not self.pe))
        for b in excl:
            evs = ([b.w] if b.w is not None else []) + [e for e in b.r if e[3] != st.name]
            for key, sem, val, sname in evs:
                if key not in need or need[key][1] < val:
                    need[key] = (sem, val)
        for key, (sem, val) in need.items():
            if st.seen.get(key, 0) < val:
                st.eng.wait_ge(sem, val)
                st.seen[key] = val
        ins = fn()
        ins.then_inc(st.sem, 1)
        ev = st.bump()
        self.last_ev[st.name] = ev
        self._record(ev, list(reads) + excl, writes)
        return ins

    def dma(self, ds, out, in_, reads=(), writes=(), **kw):
        need = self._deps("sp:" + ds.name, reads, writes)
        for key, (sem, val) in need.items():
            if self.sp_seen.get(key, 0) < val:
                self.sp.wait_ge(sem, val)
                self.sp_seen[key] = val
        ins = self.sp.dma_start(out=out, in_=in_, **kw)
        ins.then_inc(ds.sem, 16)
        ev = ds.bump()
        self.last_ev[ds.name] = ev
        self._record(ev, reads, writes)
        return ins

    def barrier(self):
        evs = list(self.last_ev.values())
        for st in (self.pe, self.act, self.dve, self.pool):
            for key, sem, val, sname in evs:
                if sname == st.name:
                    continue
                if st.seen.get(key, 0) < val:
                    st.eng.wait_ge(sem, val)
                    st.seen[key] = val
        for key, sem, val, sname in evs:
            if self.sp_seen.get(key, 0) < val:
                self.sp.wait_ge(sem, val)
                self.sp_seen[key] = val


def build(n_prompt=2, seq=SEQ, sample=True, past=PAST, dec=DEC_SEQ):
    nc = bass.Bass("TRN2", target_bir_lowering=False)
    NT = seq // 128
    NKT = max(NT, past // 128 + 1)
    NTAB = NT + 1

    def din(name, shape, dt=F32):
        return nc.dram_tensor(name, list(shape), dt, kind="ExternalInput").ap()

    def dout(name, shape, dt=F32):
        return nc.dram_tensor(name, list(shape), dt, kind="ExternalOutput").ap()

    xp = din("xp", [n_prompt, seq, D])
    xs = din("xs", [dec, D])
    cdk = din("cdk", [past, 512])
    cdv = din("cdv", [past, 512])
    cckv = din("cckv", [past, 128])
    ckr = din("ckr", [past, 64])
    sconv = din("sconv", [2, D_FF])
    g_attn_pk = din("g_attn_pk", [128, 8])
    g_ffn_pk = din("g_ffn_pk", [128, 8])
    g_q_pk = din("g_q_pk", [128, 2])
    g_sub_p = din("g_sub_p", [128, 1])
    g_kv = din("g_kv", [1, 128])
    g_final = din("g_final", [1, D])
    lamv = din("lamv", [1, 256])
    wconv_pk = din("wconv_pk", [128, NCH, 3])
    bconv_pk = din("bconv_pk", [128, NCH])
    w_in = din("w_in", [D, IN_COLS])
    w_qb = din("w_qb", [256, 768])
    w_kvb = din("w_kvb", [128, 1024])
    w_out = din("w_out", [D, D])
    w_up = din("w_up", [D, 2 * D_FF])
    w_down = din("w_down", [D_FF, D])
    ident_d = din("ident", [128, 128], BF16)
    cos_d = din("cos_t", [128, NTAB, 32])
    sin_d = din("sin_t", [128, NTAB, 32])
    yp = dout("yp", [n_prompt, seq, D])
    dkp = dout("dkp", [n_prompt, seq, 512])
    dvp = dout("dvp", [n_prompt, seq, 512])
    ckvp = dout("ckvp", [n_prompt, seq, 128])
    krp = dout("krp", [n_prompt, seq, 64])
    convp = dout("convp", [n_prompt, 2, D_FF])
    ys = dout("ys", [dec, D])
    dks = dout("dks", [dec, 512])
    dvs = dout("dvs", [dec, 512])
    ckvs = dout("ckvs", [dec, 128])
    krs = dout("krs", [dec, 64])
    convs = dout("convs", [2, D_FF])
    win_s = nc.dram_tensor("win_s", [4, 128, 8, 512], BF16).ap()
    wout_s = nc.dram_tensor("wout_s", [2, 128, 8, 512], BF16).ap()
    ffn_s = nc.dram_tensor("ffn_s", [NCH, 128, 3072], BF16).ap()

    with ExitStack() as st:
        S = Sched(nc, st)
        pe, act, dve, pool = S.pe, S.act, S.dve, S.pool

        def sb(name, shape, dt):
            return st.enter_context(nc.sbuf_tensor("s_" + name, list(shape), dt))

        banks = [st.enter_context(nc.psum_tensor("bank%d" % i, [128, 512], F32)) for i in (0, 1)]
        pT = st.enter_context(nc.psum_tensor("bankT", [128, 1024], BF16))
        banks += [None]
        pS_all = st.enter_context(nc.psum_tensor("bankS", [128, 2048], F32))
        banks += [pS_all[:, i * 512:(i + 1) * 512] for i in range(4)]
        banks += [st.enter_context(nc.psum_tensor("bank7", [128, 512], F32))]
        bres = [Res("bank%d" % i) for i in range(8)]
        pTr = bres[2]

        out_res = Res("outputs")
        scr_res = {"win": Res("win_s"), "wout": Res("wout_s"), "ffn": Res("ffn_s")}

        ident = sb("ident", [128, 128], BF16)
        cos_t = sb("cos_t", [128, NTAB, 32], F32)
        sin_t = sb("sin_t", [128, NTAB, 32], F32)
        gkv_b = sb("gkv_b", [128, 128], F32)
        gfin_b = sb("gfin_b", [128, D], F32)
        wconv = sb("wconv", [128, NCH, 3], F32)
        bconv = sb("bconv", [128, NCH], F32)
        neg_lam = sb("neg_lam", [128, 1], F32)
        wq_eff = sb("wq_eff", [128, 2, 4, 128], BF16)
        wqr = sb("wqr", [128, 2, 4, 64], BF16)
        R = {}

        def res(name):
            if name not in R:
                R[name] = Res(name)
            return R[name]

        _stat_next = [0]

        def stat(name, width=1):
            i = _stat_next[0]
            _stat_next[0] += width
            assert _stat_next[0] <= 64
            return st_small[:, i:i + width], res("stat_" + name)

        dq = {}

        def dstream(name):
            if getattr(S, "dry", False):
                return None
            if name not in dq:
                dq[name] = S.dma_stream("d_" + name)
            return dq[name]

        def mm(out, lhsT, rhs, start, stop, reads, writes):
            S.op(pe, lambda: nc.tensor.matmul(out, lhsT=lhsT, rhs=rhs, start=start, stop=stop,
                                              skip_group_check=True), reads, writes)

        def tr(out, in_, reads, writes):
            n = in_.shape[0]
            S.op(pe, lambda: nc.tensor.transpose(out=out, in_=in_, identity=ident[0:n, 0:n]),
                 list(reads) + [res("ident")], writes)

        def eng_of(stm):
            return {"act": nc.scalar, "dve": nc.vector, "pool": nc.gpsimd}[stm.name]

        def copy(stm, out, in_, reads, writes):
            if stm is act:
                S.op(act, lambda: nc.scalar.copy(out=out, in_=in_), reads, writes)
            else:
                e = eng_of(stm)
                S.op(stm, lambda: e.tensor_copy(out=out, in_=in_), reads, writes)

        def rstd_from_ssq(ssq_ap, ssq_res, n_feat, nm, n):
            ms, msr = stat(nm + "_ms")
            rs, rsr = stat(nm + "_rs")
            S.op(dve, lambda: nc.vector.tensor_scalar(out=ms[0:n], in0=ssq_ap[0:n], scalar1=1.0 / n_feat,
                                                      scalar2=EPS, op0=ALU.mult, op1=ALU.add),
                 [ssq_res], [msr])
            S.op(act, lambda: nc.scalar.activation(out=ms[0:n], in_=ms[0:n], func=AF.Ln), [msr], [msr])
            S.op(act, lambda: nc.scalar.activation(out=rs[0:n], in_=ms[0:n], func=AF.Exp, scale=-0.5),
                 [msr], [rsr])
            return rs, rsr

        S.dma(dstream("k0"), ident[:], ident_d[:], writes=[res("ident")])
        S.dma(dstream("k1"), cos_t[:], cos_d[:], writes=[res("tab")])
        S.dma(dstream("k2"), sin_t[:], sin_d[:], writes=[res("tab")])
        S.dma(dstream("k3"), gkv_b[:], g_kv.partition_broadcast(128), writes=[res("gkv")])
        S.dma(dstream("k4"), gfin_b[:], g_final.partition_broadcast(128), writes=[res("gfin")])
        S.dma(dstream("k5"), wconv[:], wconv_pk[:], writes=[res("wconv")])
        S.dma(dstream("k6"), bconv[:], bconv_pk[:], writes=[res("wconv")])

        with ExitStack() as st2:
            def sb2(name, shape, dt):
                return st2.enter_context(nc.sbuf_tensor("t_" + name, list(shape), dt))

            gA = sb2("gA", [128, 8], F32)
            gF = sb2("gF", [128, 8], F32)
            gQ = sb2("gQ", [128, 2], F32)
            gS = sb2("gS", [128, 1], F32)
            lamb = sb2("lamb", [128, 4, 64], F32)
            lj = sb2("lj", [128, 64], F32)
            ls = sb2("ls", [128, 4], F32)
            S.dma(dstream("k7"), gA[:], g_attn_pk[:], writes=[res("gA")])
            S.dma(dstream("k8"), gF[:], g_ffn_pk[:], writes=[res("gF")])
            S.dma(dstream("k9"), gQ[:], g_q_pk[:], writes=[res("gQ")])
            S.dma(dstream("k10"), gS[:], g_sub_p[:], writes=[res("gS")])
            S.dma(dstream("klam"), lamb[:].rearrange("p a b -> p (a b)"), lamv.partition_broadcast(128),
                  writes=[res("lamb")])
            for i in range(2):
                S.op(dve, lambda i=i: nc.vector.scalar_tensor_tensor(
                    out=lj[:], in0=lamb[:, 2 * i, :], scalar=1.0, in1=lamb[:, 2 * i + 1, :],
                    op0=ALU.mult, op1=ALU.mult, accum_out=ls[:, i:i + 1]),
                    [res("lamb")], [res("lj"), res("ls")])
            S.op(act, lambda: nc.scalar.activation(out=ls[:, 2:4], in_=ls[:, 0:2], func=AF.Exp),
                 [res("ls")], [res("ls2")])
            S.op(dve, lambda: nc.vector.scalar_tensor_tensor(
                out=neg_lam[:], in0=ls[:, 3:4], scalar=-LAM_INIT, in1=ls[:, 2:3],
                op0=ALU.add, op1=ALU.subtract), [res("ls2")], [res("neg_lam")])
            S.op(dve, lambda: nc.vector.tensor_scalar(out=gS[:], in0=gS[:], scalar1=1.0 - LAM_INIT, scalar2=None,
                                                      op0=ALU.mult), [res("gS")], [res("gS")])

            NB = 4
            wst = sb2("wst", [128, NB, 2048], F32)
            wbf = sb2("wbf", [128, NB, 4, 512], BF16)
            S.op(pool, lambda: nc.gpsimd.memset(wbf[:], 0.0), [], [res("wbf%d" % b) for b in range(NB)])
            win_v = win_s.rearrange("s p k n -> p s k n")
            def win_load(k):
                b = k % NB
                S.dma(dstream("w%d" % b), wst[:, b, 0:IN_COLS], w_in[k * 128:(k + 1) * 128, :],
                      writes=[res("wst%d" % b)])
            for k in range(NB - 2):
                win_load(k)
            for k in range(8):
                b = k % NB
                if k + NB - 2 < 8:
                    win_load(k + NB - 2)
                for s_ in range(4):
                    wd = 512 if s_ < 3 else 448
                    if s_ % 2 == 0:
                        S.op(dve, lambda s_=s_, wd=wd, b=b, k=k: nc.vector.tensor_scalar(
                            out=wbf[:, b, s_, 0:wd], in0=wst[:, b, s_ * 512:s_ * 512 + wd],
                            scalar1=gA[:, k:k + 1], scalar2=None, op0=ALU.mult),
                            [res("wst%d" % b), res("gA")], [res("wbf%d" % b)])
                    else:
                        S.op(act, lambda s_=s_, wd=wd, b=b, k=k: nc.scalar.activation(
                            out=wbf[:, b, s_, 0:wd], in_=wst[:, b, s_ * 512:s_ * 512 + wd], func=AF.Copy,
                            scale=gA[:, k:k + 1]), [res("wst%d" % b), res("gA")], [res("wbf%d" % b)])
                S.dma(dstream("ws%d" % b), win_v[:, :, k, :], wbf[:, b], reads=[res("wbf%d" % b)],
                      writes=[scr_res["win"]])

            wqf = sb2("wqf", [128, 2, 768], F32)
            wqb_bf = sb2("wqb_bf", [128, 2, 768], BF16)
            wkvf = sb2("wkvf", [128, 1024], F32)
            wkvb_bf = sb2("wkvb_bf", [128, 1024], BF16)
            wkT = sb2("wkT", [128, 4, 128], BF16)
            wvT = sb2("wvT", [128, 4, 128], BF16)
            wqnT = sb2("wqnT", [128, 4, 2, 128], BF16)
            S.dma(dstream("k12"), wqf[:], w_qb.rearrange("(c p) n -> p c n", p=128), writes=[res("wqf")])
            S.dma(dstream("k13"), wkvf[:], w_kvb[:], writes=[res("wkvf")])
            for c in range(2):
                S.op(dve, lambda c=c: nc.vector.tensor_scalar(out=wqb_bf[:, c, :], in0=wqf[:, c, :],
                                                              scalar1=gQ[:, c:c + 1], scalar2=None, op0=ALU.mult),
                     [res("wqf"), res("gQ")], [res("wqb_bf")])
            S.op(pool, lambda: nc.gpsimd.tensor_copy(out=wkvb_bf[:], in_=wkvf[:]), [res("wkvf")], [res("wkvb_bf")])
            wqb_v = wqb_bf[:].rearrange("p c (h n) -> p c h n", h=4)
            S.op(dve, lambda: nc.vector.tensor_copy(out=wqr[:], in_=wqb_v[:, :, :, 128:192]),
                 [res("wqb_bf")], [res("wqr")])
            for h in range(4):
                tr(pT[:, h * 128:(h + 1) * 128], wkvb_bf[:, h * 256:h * 256 + 128], [res("wkvb_bf")], [pTr])
                tr(pT[:, 512 + h * 128:512 + (h + 1) * 128], wkvb_bf[:, h * 256 + 128:h * 256 + 256],
                   [res("wkvb_bf")], [pTr])
            S.op(dve, lambda: nc.vector.tensor_copy(out=wkT[:], in_=pT[:, 0:512].rearrange("p (h l) -> p h l", h=4)),
                 [pTr], [pTr, res("wkT")])
            S.op(dve, lambda: nc.vector.tensor_copy(out=wvT[:], in_=pT[:, 512:1024].rearrange("p (h l) -> p h l", h=4)),
                 [pTr], [pTr, res("wvT")])
            for h in range(4):
                for c in range(2):
                    tr(pT[:, (h * 2 + c) * 128:(h * 2 + c + 1) * 128], wqb_v[:, c, h, 0:128], [res("wqb_bf")], [pTr])
            S.op(dve, lambda: nc.vector.tensor_copy(
                out=wqnT[:], in_=pT[:, 0:1024].rearrange("p (h c q) -> p h c q", h=4, c=2)),
                [pTr], [pTr, res("wqnT")])
            for c in range(2):
                for h in range(4):
                    mm(banks[c][:, h * 128:(h + 1) * 128], wqnT[:, h, c, :], wkT[:, h, :], True, True,
                       [res("wqnT"), res("wkT")], [bres[c]])
                S.op(dve, lambda c=c: nc.vector.tensor_copy(
                    out=wq_eff[:, c], in_=banks[c][:, 0:512].rearrange("p (h l) -> p h l", h=4)),
                    [], [bres[c], res("wq_eff")])

            wof = sb2("wof", [128, 2, 1024], F32)
            wob = sb2("wob", [128, 2, 1024], BF16)
            woe = sb2("woe", [128, 8, 1024], BF16)
            for c in range(8):
                b = c % 2
                S.dma(dstream("w%d" % b), wof[:, b, :], w_out[c * 128:(c + 1) * 128, :], writes=[res("wof%d" % b)])
                if c < 4:
                    S.op(dve, lambda c=c, b=b: nc.vector.tensor_scalar(
                        out=woe[:, c, :], in0=wof[:, b, :], scalar1=gS[:, 0:1], scalar2=None, op0=ALU.mult),
                        [res("wof%d" % b), res("gS")], [res("woe")])
                else:
                    h = c - 4
                    S.op(pool, lambda b=b: nc.gpsimd.tensor_copy(out=wob[:, b, :], in_=wof[:, b, :]),
                         [res("wof%d" % b)], [res("wob%d" % b)])
                    for half in range(2):
                        mm(banks[half][:, 0:512], wvT[:, h, :], wob[:, b, half * 512:(half + 1) * 512], True, True,
                           [res("wvT"), res("wob%d" % b)], [bres[half]])
                        S.op(act, lambda c=c, half=half: nc.scalar.copy(
                            out=woe[:, c, half * 512:(half + 1) * 512], in_=banks[half][:, 0:512]),
                            [], [bres[half], res("woe")])
            for half in range(2):
                S.dma(dstream("ws%d" % half), wout_s[half], woe[:, :, half * 512:(half + 1) * 512],
                      reads=[res("woe")], writes=[scr_res["wout"]])

            NB = 5
            fst = sb2("fst", [128, NB, 8, 2, 128], F32)
            fdn = sb2("fdn", [128, NB, 1024], F32)
            fbf = sb2("fbf", [128, NB, 3072], BF16)
            w_up_v = w_up.rearrange("(k p) (u j c) -> p k u j c", p=128, u=2, c=128)
            def ffn_load(j):
                b = j % NB
                for u in range(2):
                    S.dma(dstream("f%d" % b), fst[:, b, :, u, :], w_up_v[:, :, u, j, :],
                          writes=[res("fst%d" % b)])
                S.dma(dstream("fd%d" % b), fdn[:, b, :], w_down[j * 128:(j + 1) * 128, :], writes=[res("fdn%d" % b)])
            for j in range(NB - 2):
                ffn_load(j)
            for j in range(NCH):
                b = j % NB
                if j + NB - 2 < NCH:
                    ffn_load(j + NB - 2)
                S.op(dve, lambda b=b: nc.vector.tensor_tensor(
                    out=fbf[:, b, 0:2048].rearrange("p (k x) -> p k x", k=8),
                    in0=fst[:, b].rearrange("p k u c -> p k (u c)"),
                    in1=gF[:, :].unsqueeze(2).to_broadcast([128, 8, 256]), op=ALU.mult),
                    [res("fst%d" % b), res("gF")], [res("fbfu%d" % b)])
                S.op(act, lambda b=b: nc.scalar.copy(out=fbf[:, b, 2048:3072], in_=fdn[:, b, :]),
                     [res("fdn%d" % b)], [res("fbfd%d" % b)])
                S.dma(dstream("fs%d" % b), ffn_s[j], fbf[:, b, :], reads=[res("fbfu%d" % b), res("fbfd%d" % b)],
                      writes=[scr_res["ffn"]])
            S.barrier()

        KT = sb("KT", [128, 4, NKT * 128], BF16)
        VX = sb("VX", [128, NKT, 4, 130], BF16)
        CT = sb("CT", [128, NKT * 128], BF16)
        RT = sb("RT", [128, NKT * 128], BF16)
        CX = sb("CX", [128, NKT, 130], BF16)
        ring = sb("ring", [128, RING_SLOTS, RING_W], BF16)
        ringD = sb("ringD", [128, RINGD_SLOTS, 1024], BF16)
        xg = sb("xg", [128, 2, 2, D], F32)
        xn = sb("xn", [128, D], BF16)
        xnT = sb("xnT", [128, 8, 256], BF16)
        mixT = xnT
        h2T = xnT
        qf = sb("qf", [128, 1, 512], F32)
        ra = sb("ra", [128, 1, 512], F32)
        rb = sb("rb", [128, 1, 512], F32)
        qrb = sb("qrb", [128, 2, 512], BF16)
        kout = sb("kout", [128, 1, 512], F32)
        vout = sb("vout", [128, 1, 512], F32)
        mf = qf
        cko = sb("cko", [128, 1, 128], F32)
        kro = sb("kro", [128, 2, 64], F32)
        krd = sb("krd", [128, 2, 128], BF16)
        cqn = sb("cqn", [128, 2, 256], BF16)
        cqT = sb("cqT", [128, 2, 256], BF16)
        QT = sb("QT", [128, 4, 256], BF16)
        QpT = sb("QpT", [128, 4, 256], BF16)
        QRT = sb("QRT", [128, 2, 256], BF16)
        PT = sb("PT", [128, 2, 2, 2, 256], BF16)
        st_small = sb("st_small", [128, 64], F32)
        t0 = sb("t0", [128, 2, 128], F32)
        od = sb("od", [128, 2, 128], F32)
        gbuf = sb("gbuf", [128, 2, 258], F32)
        cbuf = sb("cbuf", [128, 2, 256], F32)
        actT = sb("actT", [128, 3, 256], BF16)
        gstate = sb("gstate", [128, NCH, 2], F32)
        ub = sb("ub", [128, 2, 256], F32)

        xn2 = sb("xn2", [128, D], BF16)
        h2T = sb("h2T_", [128, 8, 256], BF16)
        osb = sb("osb", [128, 2, 2, 129], F32)
        onall = sb("onall", [128, 2, 2, 2, 128], BF16)

        S.op(pool, lambda: nc.gpsimd.memset(VX[:, :, :, 128:130], 1.0), [], [res("VXones")])
        S.op(pool, lambda: nc.gpsimd.memset(CX[:, :, 128:130], 1.0), [], [res("CXones")])
        mhalf = sb("mhalf", [128, 1], F32)
        S.op(pool, lambda: nc.gpsimd.memset(mhalf[:], -0.5), [], [res("mhalf")])
        S.barrier()

        statcache = {}
        rstdcache = {}

        def statc(name, width=1):
            if name not in statcache:
                statcache[name] = stat(name, width)
            return statcache[name]

        def rstd_cached(nm, ssq_ap, ssq_res, n_feat, n):
            if nm not in rstdcache:
                rstdcache[nm] = (statc(nm + "_ms"), statc(nm + "_rs"))
            (ms, msr), (rs, rsr) = rstdcache[nm]
            S.op(dve, lambda: nc.vector.tensor_scalar(out=ms[0:n], in0=ssq_ap[0:n], scalar1=1.0 / n_feat,
                                                      scalar2=EPS, op0=ALU.mult, op1=ALU.add), [ssq_res], [msr])
            S.op(pool, lambda: nc.gpsimd.tensor_tensor(out=rs[0:n], in0=ms[0:n], in1=mhalf[0:n], op=ALU.pow),
                 [msr, res("mhalf")], [rsr])
            return rs, rsr

        real_S = S

        class DrySched:
            dry = True

            def __init__(self):
                class _E:
                    def __init__(self, n):
                        self.name = n
                self.pe, self.act, self.dve, self.pool = _E("pe"), _E("act"), _E("dve"), _E("pool")

            def op(self, *a, **k):
                return None

            def dma(self, *a, **k):
                return None

            def barrier(self):
                return None

            def dma_stream(self, name):
                return None

        class Group:
            pass

        groups = []
        for sq_i in range(n_prompt):
            ng = seq // 256
            for g in range(ng):
                G = Group()
                G.x_ap, G.tsz, G.tpg, G.g, G.ng, G.npast, G.tab0 = xp[sq_i], 128, 2, g, ng, 0, 0
                G.outs = {"y": yp[sq_i], "dk": dkp[sq_i], "dv": dvp[sq_i], "ckv": ckvp[sq_i], "kr": krp[sq_i],
                          "conv": convp[sq_i]}
                G.conv_init = None
                G.sample = False
                groups.append(G)
        if sample:
            G = Group()
            G.x_ap, G.tsz, G.tpg, G.g, G.ng, G.npast, G.tab0 = xs, dec, 1, 0, 1, past // 128, NT
            G.outs = {"y": ys, "dk": dks, "dv": dvs, "ckv": ckvs, "kr": krs, "conv": convs}
            G.conv_init = sconv
            G.sample = True
            groups.append(G)
        for f_, G in enumerate(groups):
            G.f = f_
            G.slot = f_ % 2
            G.nq = G.tsz * G.tpg

        def emit_main(plan):
            nonlocal S, pe, act, dve, pool
            dry = plan is None
            if dry:
                S = DrySched()
            else:
                S = real_S
            pe, act, dve, pool = S.pe, S.act, S.dve, S.pool
            RINGS = {"M": (ring, RING_SLOTS), "D": (ringD, RINGD_SLOTS)}
            ring_log = {"M": [], "D": []}
            ring_res = {k: [Res("ring%s%d" % (k, i)) for i in range(v[1])] for k, v in RINGS.items()}
            rstate = {k: {"loaded": 0, "next": 0, "released": set()} for k in RINGS}
            kvres = {}

            def kvr(buf, kt):
                key = (buf, kt)
                if key not in kvres:
                    kvres[key] = Res("%s%d" % (buf, kt))
                return kvres[key]

            def ring_src(spec):
                kind, a, b = spec
                if kind == "win":
                    return win_s[a][:, b * 4:(b + 1) * 4, :], 2048, scr_res["win"]
                if kind == "wout":
                    return wout_s[a][:, b * 4:(b + 1) * 4, :], 2048, scr_res["wout"]
                if kind == "up":
                    return ffn_s[a][:, 0:2048], 2048, scr_res["ffn"]
                return ffn_s[a][:, 2048:3072], 1024, scr_res["ffn"]

            def ring_prefetch():
                if dry:
                    return
                progress = True
                while progress:
                    progress = False
                    for k in ("M", "D"):
                        rs_, (rt, nsl), pl = rstate[k], RINGS[k], plan[k]
                        u = rs_["loaded"]
                        if u >= len(pl):
                            continue
                        if u >= nsl and (u - nsl) not in rs_["released"]:
                            continue
                        if u > rs_["next"] + nsl - 1:
                            continue
                        ap, wd, sres = ring_src(pl[u])
                        slot = u % nsl
                        if pl[u][0] in ("win", "wout"):
                            dst = rt[:, slot, 0:2048].rearrange("p (k n) -> p k n", k=4)
                        else:
                            dst = rt[:, slot, 0:wd]
                        S.dma(dstream("ring%s%d" % (k, slot)), dst, ap, reads=[sres], writes=[ring_res[k][slot]])
                        rs_["loaded"] += 1
                        progress = True

            def ring_next(spec):
                k = "D" if spec[0] == "down" else "M"
                rs_, (rt, nsl) = rstate[k], RINGS[k]
                u = rs_["next"]
                ring_log[k].append(spec)
                if not dry:
                    assert plan[k][u] == spec, (k, u, plan[k][u], spec)
                rs_["next"] += 1
                ring_prefetch()
                if not dry:
                    assert rs_["loaded"] > u, "ring %s deadlock: unit %d not loadable" % (k, u)
                slot = u % nsl
                return (k, u), rt[:, slot, :], ring_res[k][slot]

            def ring_release(h):
                rstate[h[0]]["released"].add(h[1])
                ring_prefetch()

            def rope(src, n, nh, tab_i, out_ap, reads, out_res_list, e1, e2):
                xv = src.rearrange("p (h t j) -> p h t j", h=nh, t=2)
                ov = out_ap.rearrange("p (h t j) -> p h t j", h=nh, t=2)
                av = ra[0:n, 0, 0:nh * 64].rearrange("p (h t j) -> p h t j", h=nh, t=2)
                bv = rb[0:n, 0, 0:nh * 64].rearrange("p (h t j) -> p h t j", h=nh, t=2)
                cosb = cos_t[0:n, tab_i, :].unsqueeze(1).unsqueeze(1).to_broadcast([n, nh, 2, 32])
                sinb = sin_t[0:n, tab_i, :].unsqueeze(1).to_broadcast([n, nh, 32])
                rar, rbr = res("ra0"), res("rb0")
                S.op(e1, lambda: eng_of(e1).tensor_tensor(out=av, in0=xv, in1=cosb, op=ALU.mult),
                     list(reads) + [res("tab")], [rar])
                S.op(e2, lambda: eng_of(e2).tensor_tensor(out=bv[:, :, 0, :], in0=xv[:, :, 1, :], in1=sinb,
                                                          op=ALU.mult), list(reads) + [res("tab")], [rbr])
                S.op(e2, lambda: eng_of(e2).tensor_tensor(out=bv[:, :, 1, :], in0=xv[:, :, 0, :], in1=sinb,
                                                          op=ALU.mult), list(reads) + [res("tab")], [rbr])
                S.op(e1, lambda: eng_of(e1).tensor_tensor(out=ov[:, :, 0, :], in0=av[:, :, 0, :], in1=bv[:, :, 0, :],
                                                          op=ALU.subtract), [rar, rbr], out_res_list)
                S.op(e1, lambda: eng_of(e1).tensor_tensor(out=ov[:, :, 1, :], in0=av[:, :, 1, :], in1=bv[:, :, 1, :],
                                                          op=ALU.add), [rar, rbr], out_res_list)

            def load_x(G):
                for tl in range(G.tpg):
                    ti = G.g * G.tpg + tl
                    S.dma(dstream("x%d%d" % (G.slot, tl)), xg[0:G.tsz, G.slot, tl, :],
                          G.x_ap[ti * G.tsz:(ti + 1) * G.tsz, :], writes=[res("xg%d%d" % (G.slot, tl))])

            PA = 7

            def phaseA_stages(G):
                n, tpg, nq = G.tsz, G.tpg, G.nq
                stages = []
                xnb = [xn, xn2]

                if G.sample:
                    for kt in range(past // 128):
                        def st_past(kt=kt):
                            r0 = kt * 128
                            S.dma(dstream("c0"), qf[:, 0, :], cdk[r0:r0 + 128, :], writes=[res("qf0")])
                            S.dma(dstream("c1"), ra[:, 0, :], cdv[r0:r0 + 128, :], writes=[res("ra0")])
                            S.dma(dstream("c2"), rb[:, 0, 0:128], cckv[r0:r0 + 128, :], writes=[res("rb0")])
                            S.dma(dstream("c2"), rb[:, 0, 128:192], ckr[r0:r0 + 128, :], writes=[res("rb0")])
                            S.op(pool, lambda: nc.gpsimd.tensor_copy(out=qrb[:, 0, :], in_=qf[:, 0, :]),
                                 [res("qf0")], [res("qrb0")])
                            for h in range(4):
                                tr(pT[:, h * 128:(h + 1) * 128], qrb[:, 0, h * 128:(h + 1) * 128], [res("qrb0")], [pTr])
                            S.op(dve, lambda: nc.vector.tensor_copy(
                                out=KT[:, :, r0:r0 + 128], in_=pT[:, 0:512].rearrange("p (h t) -> p h t", h=4)),
                                [], [pTr, kvr("KT", kt)])
                            S.op(pool, lambda: nc.gpsimd.tensor_copy(
                                out=VX[:, kt, :, 0:128], in_=ra[:, 0, :].rearrange("p (h e) -> p h e", h=4)),
                                [res("ra0")], [kvr("VX", kt)])
                            S.op(pool, lambda: nc.gpsimd.tensor_copy(out=CX[:, kt, 0:128], in_=rb[:, 0, 0:128]),
                                 [res("rb0")], [kvr("CX", kt)])
                            for dd in range(2):
                                S.op(pool, lambda dd=dd: nc.gpsimd.tensor_copy(
                                    out=krd[:, 0, dd * 64:(dd + 1) * 64], in_=rb[:, 0, 128:192]),
                                    [res("rb0")], [res("krd0")])
                            tr(pT[:, 512:640], CX[:, kt, 0:128], [kvr("CX", kt)], [pTr])
                            tr(pT[:, 640:768], krd[:, 0, :], [res("krd0")], [pTr])
                            S.op(dve, lambda: nc.vector.tensor_copy(out=CT[:, r0:r0 + 128], in_=pT[:, 512:640]),
                                 [], [pTr, kvr("CT", kt)])
                            S.op(dve, lambda: nc.vector.tensor_copy(out=RT[:, r0:r0 + 128], in_=pT[:, 640:768]),
                                 [], [pTr, kvr("RT", kt)])
                        stages.append(st_past)

                def st_norm():
                    for tl in range(tpg):
                        xr = res("xg%d%d" % (G.slot, tl))
                        xt = xg[0:n, G.slot, tl, :]
                        xb = xnb[tl]
                        ssq, ssqr = statc("a_ssq%d" % tl)
                        S.op(act, lambda: nc.scalar.activation(out=xb[0:n, :], in_=xt, func=AF.Square,
                                                               accum_out=ssq[0:n]), [xr], [res("xn%d" % tl), ssqr])
                        rs, rsr = rstd_cached("a%d" % tl, ssq, ssqr, D, n)
                        S.op(dve, lambda: nc.vector.tensor_scalar(out=xb[0:n, :], in0=xt, scalar1=rs[0:n],
                                                                  scalar2=None, op0=ALU.mult),
                             [xr, rsr], [res("xn%d" % tl)])
                stages.append(st_norm)
                stages.extend([None, None])

                for tl in range(tpg):
                    def st_xT(tl=tl):
                        xb = xnb[tl]
                        for k in range(8):
                            tr(pT[:, k * 128:k * 128 + n], xb[0:n, k * 128:(k + 1) * 128], [res("xn%d" % tl)], [pTr])
                        S.op(dve, lambda: nc.vector.tensor_copy(
                            out=xnT[:, :, tl * 128:tl * 128 + n],
                            in_=pT[:, :].rearrange("p (k t) -> p k t", k=8)[:, :, 0:n]), [],
                            [pTr, res("xnT%d" % tl)] + [res("mixT%d" % c) for c in range(8)])
                    stages.append(st_xT)

                pending_T = []
                stage_no = [0]

                def flush_T(all_=False):
                    stage_no[0] += 1
                    while pending_T and (all_ or pending_T[0][0] <= stage_no[0] - tpg):
                        pending_T.pop(0)[1]()

                def defer_T(fn):
                    pending_T.append((stage_no[0], fn))

                for s_ in (3, 0, 1, 2):
                    wd = 512 if s_ < 3 else 448
                    units = {}
                    for tl in range(tpg):
                        def st_seg(s_=s_, tl=tl, wd=wd, units=units):
                            flush_T()
                            ti = G.g * tpg + tl
                            kt = G.npast + ti
                            tok0 = kt * 128
                            tab_i = G.tab0 + ti
                            if tl == 0:
                                for hk in range(2):
                                    units[hk] = ring_next(("win", s_, hk))
                            for k in range(8):
                                u_, unit, ur = units[k // 4]
                                uv = unit[:, 0:2048].rearrange("p (k n) -> p k n", k=4)
                                mm(banks[PA][0:n, 0:wd], xnT[:, k, tl * 128:tl * 128 + n], uv[:, k % 4, 0:wd],
                                   k == 0, k == 7, [res("xnT%d" % tl), ur], [bres[PA]])
                            if tl == tpg - 1:
                                for hk in range(2):
                                    ring_release(units[hk][0])
                            bk = PA
                            if s_ == 0:
                                copy(act, qf[0:n, 0, :], banks[bk][0:n, 0:512], [], [bres[bk], res("qf0")])
                                rope(qf[0:n, 0, :], n, 8, tab_i, qrb[0:n, tl, :], [res("qf0")],
                                     [res("qrb%d" % tl)], dve, pool)

                                def tq(tl=tl):
                                    for h in range(4):
                                        tr(pT[:, h * 128:h * 128 + n], qrb[0:n, tl, h * 128:(h + 1) * 128],
                                           [res("qrb%d" % tl)], [pTr])
                                    S.op(dve, lambda: nc.vector.tensor_copy(
                                        out=QT[:, :, tl * 128:tl * 128 + n],
                                        in_=pT[:, 0:512].rearrange("p (h t) -> p h t", h=4)[:, :, 0:n]),
                                        [], [pTr, res("QT")])
                                defer_T(tq)
                            elif s_ == 1:
                                copy(act, qf[0:n, 0, :], banks[bk][0:n, 0:512], [], [bres[bk], res("qf0")])
                                rope(qf[0:n, 0, :], n, 8, tab_i, kout[0:n, 0, :], [res("qf0")],
                                     [res("kout0")], dve, pool)
                                S.dma(dstream("ko0"), G.outs["dk"][ti * n:(ti + 1) * n, :], kout[0:n, 0, :],
                                      reads=[res("kout0")], writes=[out_res])
                                S.op(pool, lambda: nc.gpsimd.tensor_copy(out=qrb[0:n, tl, :], in_=kout[0:n, 0, :]),
                                     [res("kout0")], [res("qrb%d" % tl)])

                                def tk(tl=tl, kt=kt, tok0=tok0):
                                    for h in range(4):
                                        tr(pT[:, h * 128:h * 128 + n], qrb[0:n, tl, h * 128:(h + 1) * 128],
                                           [res("qrb%d" % tl)], [pTr])
                                    S.op(dve, lambda: nc.vector.tensor_copy(
                                        out=KT[:, :, tok0:tok0 + n],
                                        in_=pT[:, 0:512].rearrange("p (h t) -> p h t", h=4)[:, :, 0:n]),
                                        [], [pTr, kvr("KT", kt)])
                                defer_T(tk)
                            elif s_ == 2:
                                copy(act, vout[0:n, 0, :], banks[bk][0:n, 0:512], [], [bres[bk], res("vout0")])
                                S.dma(dstream("vo0"), G.outs["dv"][ti * n:(ti + 1) * n, :], vout[0:n, 0, :],
                                      reads=[res("vout0")], writes=[out_res])
                                S.op(pool, lambda: nc.gpsimd.tensor_copy(
                                    out=VX[0:n, kt, :, 0:128],
                                    in_=vout[0:n, 0, :].rearrange("p (h e) -> p h e", h=4)),
                                    [res("vout0")], [kvr("VX", kt)])
                            else:
                                copy(act, mf[0:n, 0, 0:448], banks[bk][0:n, 0:448], [], [bres[bk], res("qf0")])
                                mfr = res("qf0")
                                sq, sqr = statc("cq_ssq")
                                S.op(dve, lambda: nc.vector.scalar_tensor_tensor(
                                    out=ra[0:n, 0, 0:256], in0=mf[0:n, 0, 0:256], scalar=1.0, in1=mf[0:n, 0, 0:256],
                                    op0=ALU.mult, op1=ALU.mult, accum_out=sq[0:n]), [mfr], [res("ra0"), sqr])
                                rs, rsr = rstd_cached("cq", sq, sqr, 256, n)
                                S.op(dve, lambda: nc.vector.tensor_scalar(
                                    out=cqn[0:n, tl, :], in0=mf[0:n, 0, 0:256], scalar1=rs[0:n], scalar2=None,
                                    op0=ALU.mult), [mfr, rsr], [res("cqn%d" % tl)])
                                sk, skr = statc("ckv_ssq")
                                S.op(dve, lambda: nc.vector.scalar_tensor_tensor(
                                    out=rb[0:n, 0, 0:128], in0=mf[0:n, 0, 256:384], scalar=1.0,
                                    in1=mf[0:n, 0, 256:384], op0=ALU.mult, op1=ALU.mult, accum_out=sk[0:n]),
                                    [mfr], [res("rb0"), skr])
                                rs2, rsr2 = rstd_cached("ckv", sk, skr, 128, n)
                                S.op(dve, lambda: nc.vector.scalar_tensor_tensor(
                                    out=cko[0:n, 0, :], in0=mf[0:n, 0, 256:384], scalar=rs2[0:n], in1=gkv_b[0:n, :],
                                    op0=ALU.mult, op1=ALU.mult), [mfr, rsr2, res("gkv")], [res("cko0")])
                                S.dma(dstream("co0"), G.outs["ckv"][ti * n:(ti + 1) * n, :], cko[0:n, 0, :],
                                      reads=[res("cko0")], writes=[out_res])
                                S.op(pool, lambda: nc.gpsimd.tensor_copy(out=CX[0:n, kt, 0:128], in_=cko[0:n, 0, :]),
                                     [res("cko0")], [kvr("CX", kt)])
                                rope(mf[0:n, 0, 384:448], n, 1, tab_i, kro[0:n, tl, :], [mfr],
                                     [res("kro%d" % tl)], dve, pool)
                                S.dma(dstream("ro%d" % tl), G.outs["kr"][ti * n:(ti + 1) * n, :], kro[0:n, tl, :],
                                      reads=[res("kro%d" % tl)], writes=[out_res])
                                for dd in range(2):
                                    S.op(pool, lambda dd=dd: nc.gpsimd.tensor_copy(
                                        out=krd[0:n, tl, dd * 64:(dd + 1) * 64], in_=kro[0:n, tl, :]),
                                        [res("kro%d" % tl)], [res("krd%d" % tl)])

                                def tm(tl=tl, kt=kt, tok0=tok0):
                                    for c in range(2):
                                        tr(pT[:, c * 128:c * 128 + n], cqn[0:n, tl, c * 128:(c + 1) * 128],
                                           [res("cqn%d" % tl)], [pTr])
                                    tr(pT[:, 256:256 + n], CX[0:n, kt, 0:128], [kvr("CX", kt)], [pTr])
                                    tr(pT[:, 384:384 + n], krd[0:n, tl, :], [res("krd%d" % tl)], [pTr])
                                    S.op(dve, lambda: nc.vector.tensor_copy(
                                        out=cqT[:, :, tl * 128:tl * 128 + n],
                                        in_=pT[:, 0:256].rearrange("p (c t) -> p c t", c=2)[:, :, 0:n]),
                                        [], [pTr, res("cqT")])
                                    S.op(dve, lambda: nc.vector.tensor_copy(out=CT[:, tok0:tok0 + n],
                                                                            in_=pT[:, 256:256 + n]),
                                         [], [pTr, kvr("CT", kt)])
                                    S.op(dve, lambda: nc.vector.tensor_copy(out=RT[:, tok0:tok0 + n],
                                                                            in_=pT[:, 384:384 + n]),
                                         [], [pTr, kvr("RT", kt)])
                                defer_T(tm)
                        stages.append(st_seg)

                for hp in range(2):
                    def st_qp(hp=hp):
                        flush_T(hp == 0)
                        for hh in range(2):
                            h = hp * 2 + hh
                            for c in range(2):
                                mm(banks[PA][:, hh * 256:hh * 256 + nq], wq_eff[:, c, h, :], cqT[:, c, 0:nq],
                                   c == 0, c == 1, [res("cqT"), res("wq_eff")], [bres[PA]])
                        S.op(dve, lambda hp=hp: nc.vector.tensor_copy(
                            out=QpT[:, hp * 2:hp * 2 + 2, 0:nq],
                            in_=banks[PA][:, :].rearrange("p (h t) -> p h t", h=2)[:, :, 0:nq]),
                            [], [bres[PA], res("QpT")])
                    stages.append(st_qp)

                for tl in range(tpg):
                    def st_qr(tl=tl):
                        flush_T()
                        ti = G.g * tpg + tl
                        tab_i = G.tab0 + ti
                        for c in range(2):
                            mm(banks[PA][0:n, 0:256], cqT[:, c, tl * 128:tl * 128 + n],
                               wqr[:, c].rearrange("p h r -> p (h r)"), c == 0, c == 1, [res("cqT"), res("wqr")],
                               [bres[PA]])
                        copy(act, qf[0:n, 0, 0:256], banks[PA][0:n, 0:256], [], [bres[PA], res("qf0")])
                        rope(qf[0:n, 0, 0:256], n, 4, tab_i, qrb[0:n, tl, 0:256], [res("qf0")],
                             [res("qrb%d" % tl)], dve, pool)

                        def tqr(tl=tl):
                            for u in range(2):
                                tr(pT[:, u * 128:u * 128 + n], qrb[0:n, tl, u * 128:(u + 1) * 128],
                                   [res("qrb%d" % tl)], [pTr])
                            S.op(dve, lambda: nc.vector.tensor_copy(
                                out=QRT[:, :, tl * 128:tl * 128 + n],
                                in_=pT[:, 0:256].rearrange("p (u t) -> p u t", u=2)[:, :, 0:n]),
                                [], [pTr, res("QRT")])
                        defer_T(tqr)
                    stages.append(st_qr)
                stages.append(flush_T)
                stages.append(lambda: flush_T(True))
                return stages

            def attention(G):
                n, tpg, nq, g, npast = G.tsz, G.tpg, G.nq, G.g, G.npast
                pairs = []
                if not G.sample:
                    for j in range(g):
                        pairs.append([(2 * j, 128), (2 * j + 1, 128)])
                    pairs.append("diag")
                else:
                    for j in range(npast // 2):
                        pairs.append([(2 * j, 128), (2 * j + 1, 128)])
                    pairs.append([(npast, n)])
                qts = [(q0, min(128, nq - q0)) for q0 in range(0, nq, 128)]
                pS = [[3, 5], [4, 6]]
                pO = [7, 0]
                deferred = []

                def make_unit(unit_i):
                    is_diff = unit_i < 4
                    scale = DIFF_SCALE if is_diff else MLA_SCALE
                    first_av = [True, True]

                    def emit_S(pi, pr):
                        buf = pi % 2
                        tl_list = pr if pr != "diag" else [(2 * g, 128), (2 * g + 1, 128)]
                        for i, (kt, nk) in enumerate(tl_list):
                            c0 = kt * 128
                            for m in range(2):
                                bk = pS[m][buf]
                                outp = banks[bk][0:nk, i * 256:i * 256 + nq]
                                if is_diff:
                                    h = unit_i
                                    mm(outp, KT[m * 64:(m + 1) * 64, h, c0:c0 + nk], QT[m * 64:(m + 1) * 64, h, 0:nq],
                                       True, True, [kvr("KT", kt), res("QT")], [bres[bk]])
                                else:
                                    h = (unit_i - 4) * 2 + m
                                    mm(outp, CT[:, c0:c0 + nk], QpT[:, h, 0:nq], True, False,
                                       [kvr("CT", kt), res("QpT")], [bres[bk]])
                                    mm(outp, RT[m * 64:(m + 1) * 64, c0:c0 + nk],
                                       QRT[m * 64:(m + 1) * 64, unit_i - 4, 0:nq], False, True,
                                       [kvr("RT", kt), res("QRT")], [bres[bk]])
                        bks = [bres[pS[0][buf]], bres[pS[1][buf]]]
                        ptrs = [res("PT0%d" % buf), res("PT1%d" % buf)]
                        b0 = (pS[0][buf] - 3) * 512
                        bv = pS_all[:, b0:b0 + 1024].rearrange("p (m i q) -> p m i q", m=2, i=2)
                        if pr == "diag":
                            regs = [(0, 64, 0, 0, 256), (64, 128, 0, 64, 256), (0, 64, 1, 128, 256),
                                    (64, 128, 1, 192, 256)]
                            for (p0, p1, i, q0, q1) in [(64, 128, 0, 0, 64), (64, 128, 1, 128, 192)]:
                                S.op(pool, lambda: nc.gpsimd.memset(PT[p0:p1, :, buf, i, q0:q1], 0.0), [], ptrs)
                            for (p0, p1, i, q0, q1) in regs:
                                S.op(act, lambda: nc.scalar.activation(
                                    out=PT[p0:p1, :, buf, i, q0:q1], in_=bv[p0:p1, :, i, q0:q1], func=AF.Exp,
                                    scale=scale), [], bks + ptrs)
                        elif len(pr) == 2:
                            S.op(act, lambda: nc.scalar.activation(
                                out=PT[:, :, buf, :, 0:nq], in_=bv[:, :, :, 0:nq], func=AF.Exp, scale=scale),
                                [], bks + ptrs)
                        else:
                            nk = pr[0][1]
                            S.op(act, lambda: nc.scalar.activation(
                                out=PT[0:nk, :, buf, 0, 0:nq], in_=bv[0:nk, :, 0, 0:nq], func=AF.Exp,
                                scale=scale), [], bks + ptrs)

                    def emit_AV(pi, pr, last):
                        buf = pi % 2
                        if pr == "diag":
                            items = [(2 * g, 128, 0, (0, 1)), (2 * g + 1, 128, 1, (1,))]
                        else:
                            items = [(kt, nk, i, tuple(range(len(qts)))) for i, (kt, nk) in enumerate(pr)]
                        for qi, (q0, nqt) in enumerate(qts):
                            bk = pO[qi]
                            ov = banks[bk][:, :].rearrange("p (m x) -> p m x", m=2)
                            for m in range(2):
                                its = [it for it in items if qi in it[3]]
                                for ii, (kt, nk, i, _q) in enumerate(its):
                                    lhsT = PT[0:nk, m, buf, i, q0:q0 + nqt]
                                    ptr_ = res("PT%d%d" % (m, buf))
                                    if is_diff:
                                        rhs = VX[0:nk, kt, unit_i, 0:129]
                                        rr = [kvr("VX", kt), res("VXones")]
                                    else:
                                        rhs = CX[0:nk, kt, 0:129]
                                        rr = [kvr("CX", kt), res("CXones")]
                                    start = first_av[qi] and m == 0
                                    if start:
                                        first_av[qi] = False
                                    mm(ov[0:nqt, m, 0:129], lhsT, rhs, start, last and ii == len(its) - 1,
                                       [ptr_] + rr, [bres[bk]])

                    def finish_unit():
                        def each_q(fn):
                            for qi, (q0, nqt) in enumerate(qts):
                                fn(qi, nqt)

                        def e_copy(qi, nqt):
                            bk = pO[qi]
                            ov = banks[bk][:, :].rearrange("p (m x) -> p m x", m=2)
                            S.op(dve, lambda: nc.vector.tensor_copy(out=osb[0:nqt, qi, :, :], in_=ov[0:nqt, :, 0:129]),
                                 [], [bres[bk], res("osb%d" % qi)])
                        each_q(e_copy)

                        def e_recip(qi, nqt):
                            rsum, rsumr = statc("rsum%d" % qi, 2)
                            S.op(dve, lambda: nc.vector.reciprocal(out=rsum[0:nqt, 0:2], in_=osb[0:nqt, qi, :, 128]),
                                 [res("osb%d" % qi)], [rsumr])
                        each_q(e_recip)
                        if is_diff:
                            def e_r1(qi, nqt):
                                rsum, rsumr = statc("rsum%d" % qi, 2)
                                r1, r1r = statc("r1_%d" % qi)
                                S.op(dve, lambda: nc.vector.tensor_tensor(out=r1[0:nqt], in0=rsum[0:nqt, 1:2],
                                                                          in1=neg_lam[0:nqt], op=ALU.mult),
                                     [rsumr, res("neg_lam")], [r1r])
                            each_q(e_r1)

                            def e_t0(qi, nqt):
                                rsum, rsumr = statc("rsum%d" % qi, 2)
                                S.op(dve, lambda: nc.vector.tensor_scalar(
                                    out=t0[0:nqt, qi, :], in0=osb[0:nqt, qi, 0, 0:128], scalar1=rsum[0:nqt, 0:1],
                                    scalar2=None, op0=ALU.mult), [rsumr, res("osb%d" % qi)], [res("t0%d" % qi)])
                            each_q(e_t0)

                            def e_od(qi, nqt):
                                r1, r1r = statc("r1_%d" % qi)
                                S.op(dve, lambda: nc.vector.scalar_tensor_tensor(
                                    out=od[0:nqt, qi, :], in0=osb[0:nqt, qi, 1, 0:128], scalar=r1[0:nqt],
                                    in1=t0[0:nqt, qi, :], op0=ALU.mult, op1=ALU.add),
                                    [r1r, res("osb%d" % qi), res("t0%d" % qi)], [res("od%d" % qi)])
                            each_q(e_od)

                            def e_ssq(qi, nqt):
                                sq, sqr = statc("od_ssq%d" % qi)
                                S.op(dve, lambda: nc.vector.scalar_tensor_tensor(
                                    out=t0[0:nqt, qi, :], in0=od[0:nqt, qi, :], scalar=1.0, in1=od[0:nqt, qi, :],
                                    op0=ALU.mult, op1=ALU.mult, accum_out=sq[0:nqt]),
                                    [res("od%d" % qi)], [res("t0%d" % qi), sqr])
                            each_q(e_ssq)
                            rss = {}

                            def e_rstd(qi, nqt):
                                sq, sqr = statc("od_ssq%d" % qi)
                                rss[qi] = rstd_cached("od%d" % qi, sq, sqr, 128, nqt)
                            each_q(e_rstd)

                            def e_on(qi, nqt):
                                rs, rsr = rss[qi]
                                S.op(dve, lambda: nc.vector.tensor_scalar(
                                    out=onall[0:nqt, qi, unit_i % 2, 0, :], in0=od[0:nqt, qi, :], scalar1=rs[0:nqt],
                                    scalar2=None, op0=ALU.mult), [res("od%d" % qi), rsr],
                                    [res("onall%d_0" % (unit_i % 2))])
                            each_q(e_on)
                        else:
                            for m in range(2):
                                def e_onm(qi, nqt, m=m):
                                    rsum, rsumr = statc("rsum%d" % qi, 2)
                                    S.op(dve, lambda: nc.vector.tensor_scalar(
                                        out=onall[0:nqt, qi, unit_i % 2, m, :], in0=osb[0:nqt, qi, m, 0:128],
                                        scalar1=rsum[0:nqt, m:m + 1], scalar2=None, op0=ALU.mult),
                                        [rsumr, res("osb%d" % qi)], [res("onall%d_%d" % (unit_i % 2, m))])
                                each_q(e_onm)

                        def tail(unit_i=unit_i, is_diff=is_diff):
                            cs = [unit_i] if is_diff else [4 + (unit_i - 4) * 2, 5 + (unit_i - 4) * 2]
                            for ci, c in enumerate(cs):
                                for qi, (q0, nqt) in enumerate(qts):
                                    tr(pT[:, (ci * 2 + qi) * 128:(ci * 2 + qi) * 128 + nqt],
                                       onall[0:nqt, qi, unit_i % 2, ci, :], [res("onall%d_%d" % (unit_i % 2, ci))], [pTr])
                            S.op(dve, lambda: nc.vector.tensor_copy(
                                out=mixT[:, cs[0]:cs[0] + len(cs), 0:nq],
                                in_=pT[:, 0:512].rearrange("p (c t) -> p c t", c=2)[:, 0:len(cs), 0:nq]),
                                [], [pTr, res("xnT0"), res("xnT1")] + [res("mixT%d" % c) for c in cs])
                        deferred.append(tail)
                    return emit_S, emit_AV, finish_unit

                unit_fns = [make_unit(u) for u in range(6)]
                np_ = len(pairs)
                steps = [(u, pr) for u in range(6) for pr in pairs]

                def run_av(pd):
                    pu, psi, ppr, plast = pd
                    unit_fns[pu][1](psi, ppr, plast)
                    if plast:
                        while deferred:
                            deferred.pop(0)()
                        unit_fns[pu][2]()
                pend = None
                for si, (u, pr) in enumerate(steps):
                    unit_fns[u][0](si, pr)
                    if pend is not None:
                        run_av(pend)
                    pend = (u, si, pr, si % np_ == np_ - 1)
                run_av(pend)
                while deferred:
                    deferred.pop(0)()

            def outproj(G):
                n, tpg = G.tsz, G.tpg
                mixr = [res("mixT%d" % c) for c in range(8)] + [res("xnT0"), res("xnT1")]
                us = {}
                for half in range(2):
                    for hk in range(2):
                        us[(half, hk)] = ring_next(("wout", half, hk))
                for tl in range(tpg):
                    xr = res("xg%d%d" % (G.slot, tl))
                    for half in range(2):
                        bk = half
                        for c in range(8):
                            u_, unit, ur = us[(half, c // 4)]
                            uv = unit[:, 0:2048].rearrange("p (k n) -> p k n", k=4)
                            mm(banks[bk][0:n, 0:512], mixT[:, c, tl * 128:tl * 128 + n], uv[:, c % 4, :], c == 0,
                               c == 7, mixr + [ur], [bres[bk]])
                        S.op(dve, lambda: nc.vector.tensor_tensor(
                            out=xg[0:n, G.slot, tl, half * 512:(half + 1) * 512], in0=banks[bk][0:n, 0:512],
                            in1=xg[0:n, G.slot, tl, half * 512:(half + 1) * 512], op=ALU.add), [xr], [bres[bk], xr])
                    xt = xg[0:n, G.slot, tl, :]
                    xb = (xn, xn2)[tl]
                    ssq, ssqr = statc("f_ssq%d" % tl)
                    S.op(act, lambda: nc.scalar.activation(out=xb[0:n, :], in_=xt, func=AF.Square,
                                                           accum_out=ssq[0:n]), [xr], [res("xn%d" % tl), ssqr])
                    rs, rsr = rstd_cached("f%d" % tl, ssq, ssqr, D, n)
                    S.op(dve, lambda: nc.vector.tensor_scalar(out=xb[0:n, :], in0=xt, scalar1=rs[0:n], scalar2=None,
                                                              op0=ALU.mult), [xr, rsr], [res("xn%d" % tl)])
                for h_ in us.values():
                    ring_release(h_[0])

            def ffn(G, stages, depth=2):
                n, tpg, nq = G.tsz, G.tpg, G.nq
                if G.g == 0:
                    if G.conv_init is None:
                        S.op(pool, lambda: nc.gpsimd.memset(gstate[:], 0.0), [], [res("gstate")])
                    else:
                        for r_ in range(2):
                            S.dma(dstream("c0"), gstate[:, :, r_],
                                  G.conv_init[r_].rearrange("(j p) -> p j", p=128),
                                  writes=[res("gstate")], allow_slow_non_contiguous=True)
                for tl in range(tpg):
                    xb = (xn, xn2)[tl]
                    for k in range(8):
                        tr(pT[:, k * 128:k * 128 + n], xb[0:n, k * 128:(k + 1) * 128], [res("xn%d" % tl)], [pTr])
                    S.op(dve, lambda: nc.vector.tensor_copy(
                        out=h2T[:, :, tl * 128:tl * 128 + n],
                        in_=pT[:, :].rearrange("p (k t) -> p k t", k=8)[:, :, 0:n]), [], [pTr, res("h2T%d" % tl)])
                pY = [[3, 4], [5, 6]]
                pend = []
                h2r = [res("h2T%d" % tl) for tl in range(tpg)]

                def emit_down(j, b3, ud):
                    u_, unit, ur = ud
                    for tl in range(tpg):
                        for half in range(2):
                            bk = pY[tl][half]
                            mm(banks[bk][0:n, 0:512], actT[:, b3, tl * 128:tl * 128 + n],
                               unit[:, half * 512:(half + 1) * 512], j == 0, j == NCH - 1,
                               [res("actT%d_%d" % (b3, tl)), ur], [bres[bk]])
                    ring_release(u_)

                stages = list(stages)
                for j in range(NCH):
                    b = j % 2
                    uu = ring_next(("up", j, 0))
                    ud = ring_next(("down", j, 0))
                    unit = uu[1]
                    uvv = unit[:, 0:2048].rearrange("p (k u c) -> p k u c", k=8, u=2)
                    pv = banks[b][:, :].rearrange("p (u t) -> p u t", u=2)
                    for u in range(2):
                        for k in range(8):
                            mm(pv[:, u, 0:nq], uvv[:, k, u, :], h2T[:, k, 0:nq], k == 0, k == 7,
                               h2r + [uu[2]], [bres[b]])
                    ring_release(uu[0])
                    if len(pend) == depth:
                        emit_down(*pend.pop(0))
                    b3 = j % 3
                    gb = gbuf[:, b, :]
                    gbr = res("gbuf%d" % b)
                    ubr = res("ub%d" % b)
                    copy(act, gb[:, 2:2 + nq], pv[:, 1, 0:nq], [], [bres[b], gbr])
                    copy(act, ub[:, b, 0:nq], pv[:, 0, 0:nq], [], [bres[b], ubr])
                    S.op(pool, lambda: nc.gpsimd.tensor_copy(out=gb[:, 0:2], in_=gstate[:, j, :]),
                         [res("gstate")], [gbr])
                    S.op(pool, lambda: nc.gpsimd.tensor_copy(out=gstate[:, j, :], in_=gb[:, nq:nq + 2]),
                         [gbr], [res("gstate")])
                    halves = [(0, nq)] if nq <= 128 else [(0, 128), (128, nq)]
                    hres = [res("cbuf%d_%d" % (b, hi)) for hi in range(len(halves))]
                    for hi, (h0, h1) in enumerate(halves):
                        S.op(dve, lambda: nc.vector.tensor_scalar(
                            out=cbuf[:, b, h0:h1], in0=gb[:, h0:h1], scalar1=wconv[:, j, 0:1],
                            scalar2=bconv[:, j:j + 1], op0=ALU.mult, op1=ALU.add), [gbr, res("wconv")], [hres[hi]])
                    for hi, (h0, h1) in enumerate(halves):
                        S.op(dve, lambda: nc.vector.scalar_tensor_tensor(
                            out=cbuf[:, b, h0:h1], in0=gb[:, 1 + h0:1 + h1], scalar=wconv[:, j, 1:2],
                            in1=cbuf[:, b, h0:h1], op0=ALU.mult, op1=ALU.add), [gbr, res("wconv"), hres[hi]],
                            [hres[hi]])
                    for hi, (h0, h1) in enumerate(halves):
                        S.op(dve, lambda: nc.vector.scalar_tensor_tensor(
                            out=cbuf[:, b, h0:h1], in0=gb[:, 2 + h0:2 + h1], scalar=wconv[:, j, 2:3],
                            in1=cbuf[:, b, h0:h1], op0=ALU.mult, op1=ALU.add), [gbr, res("wconv"), hres[hi]],
                            [hres[hi]])
                    for hi, (h0, h1) in enumerate(halves):
                        S.op(act, lambda: nc.scalar.activation(out=cbuf[:, b, h0:h1], in_=cbuf[:, b, h0:h1],
                                                               func=AF.Silu), [hres[hi]], [hres[hi]])
                    for hi, (h0, h1) in enumerate(halves):
                        S.op(dve, lambda: nc.vector.tensor_tensor(out=actT[:, b3, h0:h1], in0=ub[:, b, h0:h1],
                                                                  in1=cbuf[:, b, h0:h1], op=ALU.mult),
                             [hres[hi], ubr], [res("actT%d_%d" % (b3, hi))])
                    pend.append((j, b3, ud))
                    if stages and j >= 1:
                        stg = stages.pop(0)
                        if stg is not None:
                            stg()
                while pend:
                    emit_down(*pend.pop(0))
                while stages:
                    stg = stages.pop(0)
                    if stg is not None:
                        stg()

                for tl in range(tpg):
                    ti = G.g * tpg + tl
                    xr = res("xg%d%d" % (G.slot, tl))
                    for half in range(2):
                        bk = pY[tl][half]
                        S.op(dve, lambda: nc.vector.tensor_tensor(
                            out=xg[0:n, G.slot, tl, half * 512:(half + 1) * 512], in0=banks[bk][0:n, 0:512],
                            in1=xg[0:n, G.slot, tl, half * 512:(half + 1) * 512], op=ALU.add), [xr], [bres[bk], xr])
                    xt = xg[0:n, G.slot, tl, :]
                    ssq, ssqr = statc("y_ssq%d" % tl)
                    xb = (xn, xn2)[tl]
                    S.op(act, lambda: nc.scalar.activation(out=xb[0:n, :], in_=xt, func=AF.Square,
                                                           accum_out=ssq[0:n]), [xr], [res("xn%d" % tl), ssqr])
                    rs, rsr = rstd_cached("y%d" % tl, ssq, ssqr, D, n)
                    S.op(dve, lambda: nc.vector.scalar_tensor_tensor(
                        out=xt, in0=xt, scalar=rs[0:n], in1=gfin_b[0:n, :], op0=ALU.mult, op1=ALU.mult),
                        [xr, rsr, res("gfin")], [xr])
                    S.dma(dstream("yo%d" % tl), G.outs["y"][ti * n:(ti + 1) * n, :], xt,
                          reads=[xr], writes=[out_res])
                if G.g == G.ng - 1:
                    for r_ in range(2):
                        S.dma(dstream("c%d" % (1 + r_)), G.outs["conv"][r_].rearrange("(j p) -> p j", p=128),
                              gstate[:, :, r_], reads=[res("gstate")], writes=[out_res],
                              allow_slow_non_contiguous=True)

            load_x(groups[0])
            if len(groups) > 1:
                load_x(groups[1])
            for stg in phaseA_stages(groups[0]):
                if stg is not None:
                    stg()
            for f_, G in enumerate(groups):
                attention(G)
                outproj(G)
                nxt = phaseA_stages(groups[f_ + 1]) if f_ + 1 < len(groups) else []
                dense = f_ + 1 < len(groups) and groups[f_ + 1].tpg == 1
                ffn(G, nxt, 1 if dense else 2)
                if f_ + 2 < len(groups):
                    load_x(groups[f_ + 2])
            S.barrier()
            return ring_log

        plan = emit_main(None)
        emit_main(plan)
    return nc


def rope_tables(seq, past, dec):
    nt = seq // 128
    half = 32
    inv = (np.float32(10000.0) ** (-np.arange(half, dtype=np.float32) * np.float32(2.0 / 64))).astype(np.float32)
    pos = np.zeros((128, nt + 1), np.float32)
    for t in range(nt):
        pos[:, t] = t * 128 + np.arange(128)
    pos[:, nt] = past + np.arange(128)
    ang = (pos[:, :, None] * inv[None, None, :]).astype(np.float32)
    return np.cos(ang).astype(np.float32), np.sin(ang).astype(np.float32)


def make_in_maps(inputs, n_cores, n_prompt, seq, past, dec):
    f = lambda a: np.ascontiguousarray(np.asarray(a, dtype=np.float32))
    cos, sin = rope_tables(seq, past, dec)
    common = {
        "g_attn_pk": f(inputs["g_attn"][0].reshape(8, 128).T),
        "g_ffn_pk": f(inputs["g_ffn"][0].reshape(8, 128).T),
        "g_q_pk": f(inputs["g_q_lora"][0].reshape(2, 128).T),
        "g_sub_p": f(inputs["g_diff_sub"][0].reshape(128, 1)),
        "g_kv": f(inputs["g_kv_lora"][0].reshape(1, 128)),
        "g_final": f(inputs["g_final"].reshape(1, D)),
        "lamv": f(np.stack([inputs["lambda_q1"][0], inputs["lambda_k1"][0], inputs["lambda_q2"][0],
                            inputs["lambda_k2"][0]], 0).reshape(1, 256)),
        "wconv_pk": f(inputs["w_conv"][0].reshape(3, NCH, 128).transpose(2, 1, 0)),
        "bconv_pk": f(inputs["b_conv"][0].reshape(NCH, 128).T),
        "w_in": f(inputs["w_in"][0]), "w_qb": f(inputs["w_q_b"][0]), "w_kvb": f(inputs["w_kv_b"][0]),
        "w_out": f(inputs["w_out"][0]), "w_up": f(inputs["w_up"][0]), "w_down": f(inputs["w_down"][0]),
        "ident": np.eye(128, dtype=np.float32).astype(ml_dtypes.bfloat16),
        "cos_t": cos, "sin_t": sin,
    }
    maps = []
    for c in range(n_cores):
        m = dict(common)
        m["xp"] = f(inputs["x_prompt"][c * n_prompt:(c + 1) * n_prompt])
        m["xs"] = f(inputs["x_sample"][c])
        m["cdk"] = f(inputs["cache_diff_k"][0, c].reshape(past, 512))
        m["cdv"] = f(inputs["cache_diff_v"][0, c].reshape(past, 512))
        m["cckv"] = f(inputs["cache_mla_ckv"][0, c])
        m["ckr"] = f(inputs["cache_mla_krope"][0, c])
        m["sconv"] = f(inputs["state_conv"][0, c])
        maps.append(m)
    return maps


_NC_CACHE = {}


def kernel(**inputs):
    inputs = {k: np.asarray(v) for k, v in inputs.items()}
    B, seq, _ = inputs["x_prompt"].shape
    n_cores = inputs["x_sample"].shape[0]
    n_prompt = B // n_cores
    dec = inputs["x_sample"].shape[1]
    past = inputs["cache_diff_k"].shape[2]
    key = (n_prompt, seq, past, dec)
    if key not in _NC_CACHE:
        _NC_CACHE[key] = build(n_prompt=n_prompt, seq=seq, sample=True, past=past, dec=dec)
    nc = _NC_CACHE[key]
    maps = make_in_maps(inputs, n_cores, n_prompt, seq, past, dec)
    res = run_bass_kernel_spmd(nc, maps, core_ids=list(range(n_cores)))
    r = res.results
    cat = lambda k: np.concatenate([np.asarray(x[k], dtype=np.float32) for x in r], axis=0)
    stk = lambda k: np.stack([np.asarray(x[k], dtype=np.float32) for x in r], axis=0)
    y_prompt = cat("yp")
    y_sample = stk("ys")
    dk_p = cat("dkp").reshape(1, B, seq, 4, 2, 64)
    dv_p = cat("dvp").reshape(1, B, seq, 4, 128)
    ckv_p = cat("ckvp").reshape(1, B, seq, 128)
    kr_p = cat("krp").reshape(1, B, seq, 64)
    conv_p = cat("convp").reshape(1, B, 2, D_FF)
    dk_s = stk("dks").reshape(1, n_cores, dec, 4, 2, 64)
    dv_s = stk("dvs").reshape(1, n_cores, dec, 4, 128)
    ckv_s = stk("ckvs").reshape(1, n_cores, dec, 128)
    kr_s = stk("krs").reshape(1, n_cores, dec, 64)
    conv_s = stk("convs").reshape(1, n_cores, 2, D_FF)
    return (y_prompt, y_sample, dk_p, dv_p, ckv_p, kr_p, conv_p, dk_s, dv_s, ckv_s, kr_s, conv_s)
```

```python
import math
from contextlib import ExitStack

import numpy as np
import ml_dtypes

import concourse.bass as bass
import concourse.mybir as mybir
from concourse.bass_utils import run_bass_kernel_spmd

F32 = mybir.dt.float32
BF16 = mybir.dt.bfloat16
ALU = mybir.AluOpType
AF = mybir.ActivationFunctionType

D = 1024
NCORES = 8
SEQ = 4096
DEC_SEQ = 64
PAST = 1024
IN_COLS = 1984
D_FF = 2816
NCH = 22
EPS = 1e-6
LAM_INIT = 0.8 - 0.6 * math.exp(-0.3 * 0)
MLA_SCALE = (128 + 64) ** -0.5
DIFF_SCALE = 64 ** -0.5
RING_SLOTS = 6
RINGD_SLOTS = 6
RING_W = 2048


class Res:
    __slots__ = ("name", "w", "r")

    def __init__(self, name):
        self.name = name
        self.w = None
        self.r = []


class Stream:
    def __init__(self, sched, name, eng, inc, limit):
        self.sched, self.name, self.eng, self.inc, self.limit = sched, name, eng, inc, limit
        self.epoch = 0
        self.sem = sched.new_sem(name + "_0")
        self.count = 0
        self.total = 0
        self.seen = {}

    def bump(self):
        self.count += 1
        self.total += 1
        ev = ((self.name, self.epoch), self.sem, self.count * self.inc, self.name)
        if self.count >= self.limit:
            self.epoch += 1
            self.sem = self.sched.new_sem("%s_%d" % (self.name, self.epoch))
            self.count = 0
        return ev


class Sched:
    def __init__(self, nc, stack):
        self.nc = nc
        self.stack = stack
        self.nsem = 0
        self.pe = Stream(self, "pe", nc.tensor, 1, 20000)
        self.act = Stream(self, "act", nc.scalar, 1, 20000)
        self.dve = Stream(self, "dve", nc.vector, 1, 20000)
        self.pool = Stream(self, "pool", nc.gpsimd, 1, 20000)
        self.sp = nc.sync
        self.sp_seen = {}
        self.dma_streams = []
        self.last_ev = {}

    def new_sem(self, name):
        self.nsem += 1
        return self.stack.enter_context(self.nc.semaphore(name))

    def dma_stream(self, name):
        s = Stream(self, name, None, 16, 2000)
        self.dma_streams.append(s)
        return s

    def _deps(self, stream_name, reads, writes, same_engine_raw=True):
        need = {}

        def add(ev, same_ok):
            if ev is None:
                return
            key, sem, val, sname = ev
            if sname == stream_name and not same_ok:
                return
            if key not in need or need[key][1] < val:
                need[key] = (sem, val)

        for r in reads:
            add(r.w, same_engine_raw)
        for w in writes:
            add(w.w, False)
            for ev in w.r:
                add(ev, False)
        return need

    @staticmethod
    def _record(ev, reads, writes):
        for r in reads:
            r.r.append(ev)
            if len(r.r) > 48:
                best = {}
                for e in r.r:
                    if e[0] not in best or best[e[0]][2] < e[2]:
                        best[e[0]] = e
                r.r = list(best.values())
        for w in writes:
            w.w = ev
            w.r = []

    def op(self, st, fn, reads=(), writes=()):
        need = self._deps(st.name, reads, writes, same_engine_raw=(st is not self.pe))
        for key, (sem, val) in need.items():
            if st.seen.get(key, 0) < val:
                st.eng.wait_ge(sem, val)
                st.seen[key] = val
        ins = fn()
        ins.then_inc(st.sem, 1)
        ev = st.bump()
        self.last_ev[st.name] = ev
        self._record(ev, reads, writes)
        return ins

    def dma(self, ds, out, in_, reads=(), writes=(), **kw):
        need = self._deps("sp:" + ds.name, reads, writes)
        for key, (sem, val) in need.items():
            if self.sp_seen.get(key, 0) < val:
                self.sp.wait_ge(sem, val)
                self.sp_seen[key] = val
        ins = self.sp.dma_start(out=out, in_=in_, **kw)
        ins.then_inc(ds.sem, 16)
        ev = ds.bump()
        self.last_ev[ds.name] = ev
        self._record(ev, reads, writes)
        return ins

    def barrier(self):
        evs = list(self.last_ev.values())
        for st in (self.pe, self.act, self.dve, self.pool):
            for key, sem, val, sname in evs:
                if sname == st.name:
                    continue
                if st.seen.get(key, 0) < val:
                    st.eng.wait_ge(sem, val)
                    st.seen[key] = val
        for key, sem, val, sname in evs:
            if self.sp_seen.get(key, 0) < val:
                self.sp.wait_ge(sem, val)
                self.sp_seen[key] = val


def build(n_prompt=2, seq=SEQ, sample=True, past=PAST, dec=DEC_SEQ):
    nc = bass.Bass("TRN2", target_bir_lowering=False)
    NT = seq // 128
    NKT = max(NT, past // 128 + 1)
    NTAB = NT + 1

    def din(name, shape, dt=F32):
        return nc.dram_tensor(name, list(shape), dt, kind="ExternalInput").ap()

    def dout(name, shape, dt=F32):
        return nc.dram_tensor(name, list(shape), dt, kind="ExternalOutput").ap()

    xp = din("xp", [n_prompt, seq, D])
    xs = din("xs", [dec, D])
    cdk = din("cdk", [past, 512])
    cdv = din("cdv", [past, 512])
    cckv = din("cckv", [past, 128])
    ckr = din("ckr", [past, 64])
    sconv = din("sconv", [2, D_FF])
    g_attn_pk = din("g_attn_pk", [128, 8])
    g_ffn_pk = din("g_ffn_pk", [128, 8])
    g_q_pk = din("g_q_pk", [128, 2])
    g_sub_p = din("g_sub_p", [128, 1])
    g_kv = din("g_kv", [1, 128])
    g_final = din("g_final", [1, D])
    lamv = din("lamv", [1, 256])
    wconv_pk = din("wconv_pk", [128, NCH, 3])
    bconv_pk = din("bconv_pk", [128, NCH])
    w_in = din("w_in", [D, IN_COLS])
    w_qb = din("w_qb", [256, 768])
    w_kvb = din("w_kvb", [128, 1024])
    w_out = din("w_out", [D, D])
    w_up = din("w_up", [D, 2 * D_FF])
    w_down = din("w_down", [D_FF, D])
    ident_d = din("ident", [128, 128], BF16)
    cos_d = din("cos_t", [128, NTAB, 32])
    sin_d = din("sin_t", [128, NTAB, 32])
    yp = dout("yp", [n_prompt, seq, D])
    dkp = dout("dkp", [n_prompt, seq, 512])
    dvp = dout("dvp", [n_prompt, seq, 512])
    ckvp = dout("ckvp", [n_prompt, seq, 128])
    krp = dout("krp", [n_prompt, seq, 64])
    convp = dout("convp", [n_prompt, 2, D_FF])
    ys = dout("ys", [dec, D])
    dks = dout("dks", [dec, 512])
    dvs = dout("dvs", [dec, 512])
    ckvs = dout("ckvs", [dec, 128])
    krs = dout("krs", [dec, 64])
    convs = dout("convs", [2, D_FF])
    win_s = nc.dram_tensor("win_s", [4, 128, 8, 512], BF16).ap()
    wout_s = nc.dram_tensor("wout_s", [2, 128, 8, 512], BF16).ap()
    ffn_s = nc.dram_tensor("ffn_s", [NCH, 128, 3072], BF16).ap()

    with ExitStack() as st:
        S = Sched(nc, st)
        pe, act, dve, pool = S.pe, S.act, S.dve, S.pool

        def sb(name, shape, dt):
            return st.enter_context(nc.sbuf_tensor("s_" + name, list(shape), dt))

        banks = [st.enter_context(nc.psum_tensor("bank%d" % i, [128, 512], F32)) for i in (0, 1)]
        pT = st.enter_context(nc.psum_tensor("bankT", [128, 1024], BF16))
        banks += [None]
        pS_all = st.enter_context(nc.psum_tensor("bankS", [128, 2048], F32))
        banks += [pS_all[:, i * 512:(i + 1) * 512] for i in range(4)]
        banks += [st.enter_context(nc.psum_tensor("bank7", [128, 512], F32))]
        bres = [Res("bank%d" % i) for i in range(8)]
        pTr = bres[2]

        out_res = Res("outputs")
        scr_res = {"win": Res("win_s"), "wout": Res("wout_s"), "ffn": Res("ffn_s")}

        ident = sb("ident", [128, 128], BF16)
        cos_t = sb("cos_t", [128, NTAB, 32], F32)
        sin_t = sb("sin_t", [128, NTAB, 32], F32)
        gkv_b = sb("gkv_b", [128, 128], F32)
        gfin_b = sb("gfin_b", [128, D], F32)
        wconv = sb("wconv", [128, NCH, 3], F32)
        bconv = sb("bconv", [128, NCH], F32)
        neg_lam = sb("neg_lam", [128, 1], F32)
        wq_eff = sb("wq_eff", [128, 2, 4, 128], BF16)
        wqr = sb("wqr", [128, 2, 4, 64], BF16)
        R = {}

        def res(name):
            if name not in R:
                R[name] = Res(name)
            return R[name]

        _stat_next = [0]

        def stat(name, width=1):
            i = _stat_next[0]
            _stat_next[0] += width
            assert _stat_next[0] <= 64
            return st_small[:, i:i + width], res("stat_" + name)

        dq = {}

        def dstream(name):
            if getattr(S, "dry", False):
                return None
            if name not in dq:
                dq[name] = S.dma_stream("d_" + name)
            return dq[name]

        def mm(out, lhsT, rhs, start, stop, reads, writes):
            S.op(pe, lambda: nc.tensor.matmul(out, lhsT=lhsT, rhs=rhs, start=start, stop=stop,
                                              skip_group_check=True), reads, writes)

        def tr(out, in_, reads, writes):
            n = in_.shape[0]
            S.op(pe, lambda: nc.tensor.transpose(out=out, in_=in_, identity=ident[0:n, 0:n]),
                 list(reads) + [res("ident")], writes)

        def eng_of(stm):
            return {"act": nc.scalar, "dve": nc.vector, "pool": nc.gpsimd}[stm.name]

        def copy(stm, out, in_, reads, writes):
            if stm is act:
                S.op(act, lambda: nc.scalar.copy(out=out, in_=in_), reads, writes)
            else:
                e = eng_of(stm)
                S.op(stm, lambda: e.tensor_copy(out=out, in_=in_), reads, writes)

        def rstd_from_ssq(ssq_ap, ssq_res, n_feat, nm, n):
            ms, msr = stat(nm + "_ms")
            rs, rsr = stat(nm + "_rs")
            S.op(dve, lambda: nc.vector.tensor_scalar(out=ms[0:n], in0=ssq_ap[0:n], scalar1=1.0 / n_feat,
                                                      scalar2=EPS, op0=ALU.mult, op1=ALU.add),
                 [ssq_res], [msr])
            S.op(act, lambda: nc.scalar.activation(out=ms[0:n], in_=ms[0:n], func=AF.Ln), [msr], [msr])
            S.op(act, lambda: nc.scalar.activation(out=rs[0:n], in_=ms[0:n], func=AF.Exp, scale=-0.5),
                 [msr], [rsr])
            return rs, rsr

        S.dma(dstream("k0"), ident[:], ident_d[:], writes=[res("ident")])
        S.dma(dstream("k1"), cos_t[:], cos_d[:], writes=[res("tab")])
        S.dma(dstream("k2"), sin_t[:], sin_d[:], writes=[res("tab")])
        S.dma(dstream("k3"), gkv_b[:], g_kv.partition_broadcast(128), writes=[res("gkv")])
        S.dma(dstream("k4"), gfin_b[:], g_final.partition_broadcast(128), writes=[res("gfin")])
        S.dma(dstream("k5"), wconv[:], wconv_pk[:], writes=[res("wconv")])
        S.dma(dstream("k6"), bconv[:], bconv_pk[:], writes=[res("wconv")])

        with ExitStack() as st2:
            def sb2(name, shape, dt):
                return st2.enter_context(nc.sbuf_tensor("t_" + name, list(shape), dt))

            gA = sb2("gA", [128, 8], F32)
            gF = sb2("gF", [128, 8], F32)
            gQ = sb2("gQ", [128, 2], F32)
            gS = sb2("gS", [128, 1], F32)
            lamb = sb2("lamb", [128, 4, 64], F32)
            lj = sb2("lj", [128, 64], F32)
            ls = sb2("ls", [128, 4], F32)
            S.dma(dstream("k7"), gA[:], g_attn_pk[:], writes=[res("gA")])
            S.dma(dstream("k8"), gF[:], g_ffn_pk[:], writes=[res("gF")])
            S.dma(dstream("k9"), gQ[:], g_q_pk[:], writes=[res("gQ")])
            S.dma(dstream("k10"), gS[:], g_sub_p[:], writes=[res("gS")])
            S.dma(dstream("klam"), lamb[:].rearrange("p a b -> p (a b)"), lamv.partition_broadcast(128),
                  writes=[res("lamb")])
            for i in range(2):
                S.op(dve, lambda i=i: nc.vector.scalar_tensor_tensor(
                    out=lj[:], in0=lamb[:, 2 * i, :], scalar=1.0, in1=lamb[:, 2 * i + 1, :],
                    op0=ALU.mult, op1=ALU.mult, accum_out=ls[:, i:i + 1]),
                    [res("lamb")], [res("lj"), res("ls")])
            S.op(act, lambda: nc.scalar.activation(out=ls[:, 2:4], in_=ls[:, 0:2], func=AF.Exp),
                 [res("ls")], [res("ls2")])
            S.op(dve, lambda: nc.vector.scalar_tensor_tensor(
                out=neg_lam[:], in0=ls[:, 3:4], scalar=-LAM_INIT, in1=ls[:, 2:3],
                op0=ALU.add, op1=ALU.subtract), [res("ls2")], [res("neg_lam")])
            S.op(dve, lambda: nc.vector.tensor_scalar(out=gS[:], in0=gS[:], scalar1=1.0 - LAM_INIT, scalar2=None,
                                                      op0=ALU.mult), [res("gS")], [res("gS")])

            NB = 4
            wst = sb2("wst", [128, NB, 2048], F32)
            wbf = sb2("wbf", [128, NB, 4, 512], BF16)
            S.op(pool, lambda: nc.gpsimd.memset(wbf[:], 0.0), [], [res("wbf%d" % b) for b in range(NB)])
            win_v = win_s.rearrange("s p k n -> p s k n")
            def win_load(k):
                b = k % NB
                S.dma(dstream("w%d" % b), wst[:, b, 0:IN_COLS], w_in[k * 128:(k + 1) * 128, :],
                      writes=[res("wst%d" % b)])
            for k in range(NB - 2):
                win_load(k)
            for k in range(8):
                b = k % NB
                if k + NB - 2 < 8:
                    win_load(k + NB - 2)
                for s_ in range(4):
                    wd = 512 if s_ < 3 else 448
                    if s_ % 2 == 0:
                        S.op(dve, lambda s_=s_, wd=wd, b=b, k=k: nc.vector.tensor_scalar(
                            out=wbf[:, b, s_, 0:wd], in0=wst[:, b, s_ * 512:s_ * 512 + wd],
                            scalar1=gA[:, k:k + 1], scalar2=None, op0=ALU.mult),
                            [res("wst%d" % b), res("gA")], [res("wbf%d" % b)])
                    else:
                        S.op(act, lambda s_=s_, wd=wd, b=b, k=k: nc.scalar.activation(
                            out=wbf[:, b, s_, 0:wd], in_=wst[:, b, s_ * 512:s_ * 512 + wd], func=AF.Copy,
                            scale=gA[:, k:k + 1]), [res("wst%d" % b), res("gA")], [res("wbf%d" % b)])
                S.dma(dstream("ws%d" % b), win_v[:, :, k, :], wbf[:, b], reads=[res("wbf%d" % b)],
                      writes=[scr_res["win"]])

            wqf = sb2("wqf", [128, 2, 768], F32)
            wqb_bf = sb2("wqb_bf", [128, 2, 768], BF16)
            wkvf = sb2("wkvf", [128, 1024], F32)
            wkvb_bf = sb2("wkvb_bf", [128, 1024], BF16)
            wkT = sb2("wkT", [128, 4, 128], BF16)
            wvT = sb2("wvT", [128, 4, 128], BF16)
            wqnT = sb2("wqnT", [128, 4, 2, 128], BF16)
            S.dma(dstream("k11"), wqf[:], w_qb.rearrange("(c p) n -> p c n", p=128), writes=[res("wqf")])
            S.dma(dstream("k12"), wkvf[:], w_kvb[:], writes=[res("wkvf")])
            for c in range(2):
                S.op(dve, lambda c=c: nc.vector.tensor_scalar(out=wqb_bf[:, c, :], in0=wqf[:, c, :],
                                                              scalar1=gQ[:, c:c + 1], scalar2=None, op0=ALU.mult),
                     [res("wqf"), res("gQ")], [res("wqb_bf")])
            S.op(pool, lambda: nc.gpsimd.tensor_copy(out=wkvb_bf[:], in_=wkvf[:]), [res("wkvf")], [res("wkvb_bf")])
            wqb_v = wqb_bf[:].rearrange("p c (h n) -> p c h n", h=4)
            S.op(dve, lambda: nc.vector.tensor_copy(out=wqr[:], in_=wqb_v[:, :, :, 128:192]),
                 [res("wqb_bf")], [res("wqr")])
            for h in range(4):
                tr(pT[:, h * 128:(h + 1) * 128], wkvb_bf[:, h * 256:h * 256 + 128], [res("wkvb_bf")], [pTr])
                tr(pT[:, 512 + h * 128:512 + (h + 1) * 128], wkvb_bf[:, h * 256 + 128:h * 256 + 256],
                   [res("wkvb_bf")], [pTr])
            S.op(dve, lambda: nc.vector.tensor_copy(out=wkT[:], in_=pT[:, 0:512].rearrange("p (h l) -> p h l", h=4)),
                 [pTr], [pTr, res("wkT")])
            S.op(dve, lambda: nc.vector.tensor_copy(out=wvT[:], in_=pT[:, 512:1024].rearrange("p (h l) -> p h l", h=4)),
                 [pTr], [pTr, res("wvT")])
            for h in range(4):
                for c in range(2):
                    tr(pT[:, (h * 2 + c) * 128:(h * 2 + c + 1) * 128], wqb_v[:, c, h, 0:128], [res("wqb_bf")], [pTr])
            S.op(dve, lambda: nc.vector.tensor_copy(
                out=wqnT[:], in_=pT[:, 0:1024].rearrange("p (h c q) -> p h c q", h=4, c=2)),
                [pTr], [pTr, res("wqnT")])
            for c in range(2):
                for h in range(4):
                    mm(banks[c][:, h * 128:(h + 1) * 128], wqnT[:, h, c, :], wkT[:, h, :], True, True,
                       [res("wqnT"), res("wkT")], [bres[c]])
                S.op(dve, lambda c=c: nc.vector.tensor_copy(
                    out=wq_eff[:, c], in_=banks[c][:, 0:512].rearrange("p (h l) -> p h l", h=4)),
                    [], [bres[c], res("wq_eff")])

            wof = sb2("wof", [128, 2, 1024], F32)
            wob = sb2("wob", [128, 2, 1024], BF16)
            woe = sb2("woe", [128, 8, 1024], BF16)
            for c in range(8):
                b = c % 2
                S.dma(dstream("w%d" % b), wof[:, b, :], w_out[c * 128:(c + 1) * 128, :], writes=[res("wof%d" % b)])
                if c < 4:
                    S.op(dve, lambda c=c, b=b: nc.vector.tensor_scalar(
                        out=woe[:, c, :], in0=wof[:, b, :], scalar1=gS[:, 0:1], scalar2=None, op0=ALU.mult),
                        [res("wof%d" % b), res("gS")], [res("woe")])
                else:
                    h = c - 4
                    S.op(pool, lambda b=b: nc.gpsimd.tensor_copy(out=wob[:, b, :], in_=wof[:, b, :]),
                         [res("wof%d" % b)], [res("wob%d" % b)])
                    for half in range(2):
                        mm(banks[half][:, 0:512], wvT[:, h, :], wob[:, b, half * 512:(half + 1) * 512], True, True,
                           [res("wvT"), res("wob%d" % b)], [bres[half]])
                        S.op(act, lambda c=c, half=half: nc.scalar.copy(
                            out=woe[:, c, half * 512:(half + 1) * 512], in_=banks[half][:, 0:512]),
                            [], [bres[half], res("woe")])
            for half in range(2):
                S.dma(dstream("ws%d" % half), wout_s[half], woe[:, :, half * 512:(half + 1) * 512],
                      reads=[res("woe")], writes=[scr_res["wout"]])

            NB = 5
            fst = sb2("fst", [128, NB, 8, 2, 128], F32)
            fdn = sb2("fdn", [128, NB, 1024], F32)
            fbf = sb2("fbf", [128, NB, 3072], BF16)
            w_up_v = w_up.rearrange("(k p) (u j c) -> p k u j c", p=128, u=2, c=128)
            def ffn_load(j):
                b = j % NB
                for u in range(2):
                    S.dma(dstream("f%d" % b), fst[:, b, :, u, :], w_up_v[:, :, u, j, :],
                          writes=[res("fst%d" % b)])
                S.dma(dstream("fd%d" % b), fdn[:, b, :], w_down[j * 128:(j + 1) * 128, :], writes=[res("fdn%d" % b)])
            for j in range(NB - 2):
                ffn_load(j)
            for j in range(NCH):
                b = j % NB
                if j + NB - 2 < NCH:
                    ffn_load(j + NB - 2)
                S.op(dve, lambda b=b: nc.vector.tensor_tensor(
                    out=fbf[:, b, 0:2048].rearrange("p (k x) -> p k x", k=8),
                    in0=fst[:, b].rearrange("p k u c -> p k (u c)"),
                    in1=gF[:, :].unsqueeze(2).to_broadcast([128, 8, 256]), op=ALU.mult),
                    [res("fst%d" % b), res("gF")], [res("fbfu%d" % b)])
                S.op(act, lambda b=b: nc.scalar.copy(out=fbf[:, b, 2048:3072], in_=fdn[:, b, :]),
                     [res("fdn%d" % b)], [res("fbfd%d" % b)])
                S.dma(dstream("fs%d" % b), ffn_s[j], fbf[:, b, :], reads=[res("fbfu%d" % b), res("fbfd%d" % b)],
                      writes=[scr_res["ffn"]])
            S.barrier()

        KT = sb("KT", [128, 4, NKT * 128], BF16)
        VX = sb("VX", [128, NKT, 4, 130], BF16)
        CT = sb("CT", [128, NKT * 128], BF16)
        RT = sb("RT", [128, NKT * 128], BF16)
        CX = sb("CX", [128, NKT, 130], BF16)
        ring = sb("ring", [128, RING_SLOTS, RING_W], BF16)
        ringD = sb("ringD", [128, RINGD_SLOTS, 1024], BF16)
        xg = sb("xg", [128, 2, 2, D], F32)
        xn = sb("xn", [128, D], BF16)
        xnT = sb("xnT", [128, 8, 256], BF16)
        mixT = xnT
        h2T = xnT
        qf = sb("qf", [128, 1, 512], F32)
        ra = sb("ra", [128, 1, 512], F32)
        rb = sb("rb", [128, 1, 512], F32)
        qrb = sb("qrb", [128, 2, 512], BF16)
        kout = sb("kout", [128, 1, 512], F32)
        vout = sb("vout", [128, 1, 512], F32)
        mf = qf
        cko = sb("cko", [128, 1, 128], F32)
        kro = sb("kro", [128, 2, 64], F32)
        krd = sb("krd", [128, 2, 128], BF16)
        cqn = sb("cqn", [128, 2, 256], BF16)
        cqT = sb("cqT", [128, 2, 256], BF16)
        QT = sb("QT", [128, 4, 256], BF16)
        QpT = sb("QpT", [128, 4, 256], BF16)
        QRT = sb("QRT", [128, 2, 256], BF16)
        PT = sb("PT", [128, 2, 2, 2, 256], BF16)
        st_small = sb("st_small", [128, 64], F32)
        t0 = sb("t0", [128, 2, 128], F32)
        od = sb("od", [128, 2, 128], F32)
        gbuf = sb("gbuf", [128, 2, 258], F32)
        cbuf = sb("cbuf", [128, 2, 256], F32)
        actT = sb("actT", [128, 3, 256], BF16)
        gstate = sb("gstate", [128, NCH, 2], F32)
        ub = sb("ub", [128, 2, 256], F32)

        xn2 = sb("xn2", [128, D], BF16)
        h2T = sb("h2T_", [128, 8, 256], BF16)
        osb = sb("osb", [128, 2, 2, 129], F32)
        onall = sb("onall", [128, 2, 2, 2, 128], BF16)

        S.op(pool, lambda: nc.gpsimd.memset(VX[:, :, :, 128:130], 1.0), [], [res("VXones")])
        S.op(pool, lambda: nc.gpsimd.memset(CX[:, :, 128:130], 1.0), [], [res("CXones")])
        mhalf = sb("mhalf", [128, 1], F32)
        S.op(pool, lambda: nc.gpsimd.memset(mhalf[:], -0.5), [], [res("mhalf")])
        S.barrier()

        statcache = {}
        rstdcache = {}

        def statc(name, width=1):
            if name not in statcache:
                statcache[name] = stat(name, width)
            return statcache[name]

        def rstd_cached(nm, ssq_ap, ssq_res, n_feat, n):
            if nm not in rstdcache:
                rstdcache[nm] = (statc(nm + "_ms"), statc(nm + "_rs"))
            (ms, msr), (rs, rsr) = rstdcache[nm]
            S.op(dve, lambda: nc.vector.tensor_scalar(out=ms[0:n], in0=ssq_ap[0:n], scalar1=1.0 / n_feat,
                                                      scalar2=EPS, op0=ALU.mult, op1=ALU.add), [ssq_res], [msr])
            S.op(pool, lambda: nc.gpsimd.tensor_tensor(out=rs[0:n], in0=ms[0:n], in1=mhalf[0:n], op=ALU.pow),
                 [msr, res("mhalf")], [rsr])
            return rs, rsr

        real_S = S

        class DrySched:
            dry = True

            def __init__(self):
                class _E:
                    def __init__(self, n):
                        self.name = n
                self.pe, self.act, self.dve, self.pool = _E("pe"), _E("act"), _E("dve"), _E("pool")

            def op(self, *a, **k):
                return None

            def dma(self, *a, **k):
                return None

            def barrier(self):
                return None

            def dma_stream(self, name):
                return None

        class Group:
            pass

        groups = []
        for sq_i in range(n_prompt):
            ng = seq // 256
            for g in range(ng):
                G = Group()
                G.x_ap, G.tsz, G.tpg, G.g, G.ng, G.npast, G.tab0 = xp[sq_i], 128, 2, g, ng, 0, 0
                G.outs = {"y": yp[sq_i], "dk": dkp[sq_i], "dv": dvp[sq_i], "ckv": ckvp[sq_i], "kr": krp[sq_i],
                          "conv": convp[sq_i]}
                G.conv_init = None
                G.sample = False
                groups.append(G)
        if sample:
            G = Group()
            G.x_ap, G.tsz, G.tpg, G.g, G.ng, G.npast, G.tab0 = xs, dec, 1, 0, 1, past // 128, NT
            G.outs = {"y": ys, "dk": dks, "dv": dvs, "ckv": ckvs, "kr": krs, "conv": convs}
            G.conv_init = sconv
            G.sample = True
            groups.append(G)
        for f_, G in enumerate(groups):
            G.f = f_
            G.slot = f_ % 2
            G.nq = G.tsz * G.tpg

        def emit_main(plan):
            nonlocal S, pe, act, dve, pool
            dry = plan is None
            if dry:
                S = DrySched()
            else:
                S = real_S
            pe, act, dve, pool = S.pe, S.act, S.dve, S.pool
            RINGS = {"M": (ring, RING_SLOTS), "D": (ringD, RINGD_SLOTS)}
            ring_log = {"M": [], "D": []}
            ring_res = {k: [Res("ring%s%d" % (k, i)) for i in range(v[1])] for k, v in RINGS.items()}
            rstate = {k: {"loaded": 0, "next": 0, "released": set()} for k in RINGS}
            kvres = {}

            def kvr(buf, kt):
                key = (buf, kt)
                if key not in kvres:
                    kvres[key] = Res("%s%d" % (buf, kt))
                return kvres[key]

            def ring_src(spec):
                kind, a, b = spec
                if kind == "win":
                    return win_s[a][:, b * 4:(b + 1) * 4, :], 2048, scr_res["win"]
                if kind == "wout":
                    return wout_s[a][:, b * 4:(b + 1) * 4, :], 2048, scr_res["wout"]
                if kind == "up":
                    return ffn_s[a][:, 0:2048], 2048, scr_res["ffn"]
                return ffn_s[a][:, 2048:3072], 1024, scr_res["ffn"]

            def ring_prefetch():
                if dry:
                    return
                progress = True
                while progress:
                    progress = False
                    for k in ("M", "D"):
                        rs_, (rt, nsl), pl = rstate[k], RINGS[k], plan[k]
                        u = rs_["loaded"]
                        if u >= len(pl):
                            continue
                        if u >= nsl and (u - nsl) not in rs_["released"]:
                            continue
                        if u > rs_["next"] + nsl - 1:
                            continue
                        ap, wd, sres = ring_src(pl[u])
                        slot = u % nsl
                        if pl[u][0] in ("win", "wout"):
                            dst = rt[:, slot, 0:2048].rearrange("p (k n) -> p k n", k=4)
                        else:
                            dst = rt[:, slot, 0:wd]
                        S.dma(dstream("ring%s%d" % (k, slot)), dst, ap, reads=[sres], writes=[ring_res[k][slot]])
                        rs_["loaded"] += 1
                        progress = True

            def ring_next(spec):
                k = "D" if spec[0] == "down" else "M"
                rs_, (rt, nsl) = rstate[k], RINGS[k]
                u = rs_["next"]
                ring_log[k].append(spec)
                if not dry:
                    assert plan[k][u] == spec, (k, u, plan[k][u], spec)
                rs_["next"] += 1
                ring_prefetch()
                if not dry:
                    assert rs_["loaded"] > u, "ring %s deadlock: unit %d not loadable" % (k, u)
                slot = u % nsl
                return (k, u), rt[:, slot, :], ring_res[k][slot]

            def ring_release(h):
                rstate[h[0]]["released"].add(h[1])
                ring_prefetch()

            def rope(src, n, nh, tab_i, out_ap, reads, out_res_list, e1, e2):
                xv = src.rearrange("p (h t j) -> p h t j", h=nh, t=2)
                ov = out_ap.rearrange("p (h t j) -> p h t j", h=nh, t=2)
                av = ra[0:n, 0, 0:nh * 64].rearrange("p (h t j) -> p h t j", h=nh, t=2)
                bv = rb[0:n, 0, 0:nh * 64].rearrange("p (h t j) -> p h t j", h=nh, t=2)
                cosb = cos_t[0:n, tab_i, :].unsqueeze(1).unsqueeze(1).to_broadcast([n, nh, 2, 32])
                sinb = sin_t[0:n, tab_i, :].unsqueeze(1).to_broadcast([n, nh, 32])
                rar, rbr = res("ra0"), res("rb0")
                S.op(e1, lambda: eng_of(e1).tensor_tensor(out=av, in0=xv, in1=cosb, op=ALU.mult),
                     list(reads) + [res("tab")], [rar])
                S.op(e2, lambda: eng_of(e2).tensor_tensor(out=bv[:, :, 0, :], in0=xv[:, :, 1, :], in1=sinb,
                                                          op=ALU.mult), list(reads) + [res("tab")], [rbr])
                S.op(e2, lambda: eng_of(e2).tensor_tensor(out=bv[:, :, 1, :], in0=xv[:, :, 0, :], in1=sinb,
                                                          op=ALU.mult), list(reads) + [res("tab")], [rbr])
                S.op(e1, lambda: eng_of(e1).tensor_tensor(out=ov[:, :, 0, :], in0=av[:, :, 0, :], in1=bv[:, :, 0, :],
                                                          op=ALU.subtract), [rar, rbr], out_res_list)
                S.op(e1, lambda: eng_of(e1).tensor_tensor(out=ov[:, :, 1, :], in0=av[:, :, 1, :], in1=bv[:, :, 1, :],
                                                          op=ALU.add), [rar, rbr], out_res_list)

            def load_x(G):
                for tl in range(G.tpg):
                    ti = G.g * G.tpg + tl
                    S.dma(dstream("x%d%d" % (G.slot, tl)), xg[0:G.tsz, G.slot, tl, :],
                          G.x_ap[ti * G.tsz:(ti + 1) * G.tsz, :], writes=[res("xg%d%d" % (G.slot, tl))])

            PA = 7

            def phaseA_stages(G):
                n, tpg, nq = G.tsz, G.tpg, G.nq
                stages = []
                xnb = [xn, xn2]

                if G.sample:
                    for kt in range(past // 128):
                        def st_past(kt=kt):
                            r0 = kt * 128
                            S.dma(dstream("c0"), qf[:, 0, :], cdk[r0:r0 + 128, :], writes=[res("qf0")])
                            S.dma(dstream("c1"), ra[:, 0, :], cdv[r0:r0 + 128, :], writes=[res("ra0")])
                            S.dma(dstream("c2"), rb[:, 0, 0:128], cckv[r0:r0 + 128, :], writes=[res("rb0")])
                            S.dma(dstream("c2"), rb[:, 0, 128:192], ckr[r0:r0 + 128, :], writes=[res("rb0")])
                            S.op(pool, lambda: nc.gpsimd.tensor_copy(out=qrb[:, 0, :], in_=qf[:, 0, :]),
                                 [res("qf0")], [res("qrb0")])
                            for h in range(4):
                                tr(pT[:, h * 128:(h + 1) * 128], qrb[:, 0, h * 128:(h + 1) * 128], [res("qrb0")], [pTr])
                            S.op(dve, lambda: nc.vector.tensor_copy(
                                out=KT[:, :, r0:r0 + 128], in_=pT[:, 0:512].rearrange("p (h t) -> p h t", h=4)),
                                [], [pTr, kvr("KT", kt)])
                            S.op(pool, lambda: nc.gpsimd.tensor_copy(
                                out=VX[:, kt, :, 0:128], in_=ra[:, 0, :].rearrange("p (h e) -> p h e", h=4)),
                                [res("ra0")], [kvr("VX", kt)])
                            S.op(pool, lambda: nc.gpsimd.tensor_copy(out=CX[:, kt, 0:128], in_=rb[:, 0, 0:128]),
                                 [res("rb0")], [kvr("CX", kt)])
                            for dd in range(2):
                                S.op(pool, lambda dd=dd: nc.gpsimd.tensor_copy(
                                    out=krd[:, 0, dd * 64:(dd + 1) * 64], in_=rb[:, 0, 128:192]),
                                    [res("rb0")], [res("krd0")])
                            tr(pT[:, 512:640], CX[:, kt, 0:128], [kvr("CX", kt)], [pTr])
                            tr(pT[:, 640:768], krd[:, 0, :], [res("krd0")], [pTr])
                            S.op(dve, lambda: nc.vector.tensor_copy(out=CT[:, r0:r0 + 128], in_=pT[:, 512:640]),
                                 [], [pTr, kvr("CT", kt)])
                            S.op(dve, lambda: nc.vector.tensor_copy(out=RT[:, r0:r0 + 128], in_=pT[:, 640:768]),
                                 [], [pTr, kvr("RT", kt)])
                        stages.append(st_past)

                def st_norm():
                    for tl in range(tpg):
                        xr = res("xg%d%d" % (G.slot, tl))
                        xt = xg[0:n, G.slot, tl, :]
                        xb = xnb[tl]
                        ssq, ssqr = statc("a_ssq%d" % tl)
                        S.op(act, lambda: nc.scalar.activation(out=xb[0:n, :], in_=xt, func=AF.Square,
                                                               accum_out=ssq[0:n]), [xr], [res("xn%d" % tl), ssqr])
                        rs, rsr = rstd_cached("a%d" % tl, ssq, ssqr, D, n)
                        S.op(dve, lambda: nc.vector.tensor_scalar(out=xb[0:n, :], in0=xt, scalar1=rs[0:n],
                                                                  scalar2=None, op0=ALU.mult),
                             [xr, rsr], [res("xn%d" % tl)])
                stages.append(st_norm)
                stages.extend([None, None])

                for tl in range(tpg):
                    def st_xT(tl=tl):
                        xb = xnb[tl]
                        for k in range(8):
                            tr(pT[:, k * 128:k * 128 + n], xb[0:n, k * 128:(k + 1) * 128], [res("xn%d" % tl)], [pTr])
                        S.op(dve, lambda: nc.vector.tensor_copy(
                            out=xnT[:, :, tl * 128:tl * 128 + n],
                            in_=pT[:, :].rearrange("p (k t) -> p k t", k=8)[:, :, 0:n]), [],
                            [pTr, res("xnT%d" % tl)] + [res("mixT%d" % c) for c in range(8)])
                    stages.append(st_xT)

                pending_T = []
                stage_no = [0]

                def flush_T(all_=False):
                    stage_no[0] += 1
                    while pending_T and (all_ or pending_T[0][0] <= stage_no[0] - tpg):
                        pending_T.pop(0)[1]()

                def defer_T(fn):
                    pending_T.append((stage_no[0], fn))

                for s_ in (3, 0, 1, 2):
                    wd = 512 if s_ < 3 else 448
                    units = {}
                    for tl in range(tpg):
                        def st_seg(s_=s_, tl=tl, wd=wd, units=units):
                            flush_T()
                            ti = G.g * tpg + tl
                            kt = G.npast + ti
                            tok0 = kt * 128
                            tab_i = G.tab0 + ti
                            if tl == 0:
                                for hk in range(2):
                                    units[hk] = ring_next(("win", s_, hk))
                            for k in range(8):
                                u_, unit, ur = units[k // 4]
                                uv = unit[:, 0:2048].rearrange("p (k n) -> p k n", k=4)
                                mm(banks[PA][0:n, 0:wd], xnT[:, k, tl * 128:tl * 128 + n], uv[:, k % 4, 0:wd],
                                   k == 0, k == 7, [res("xnT%d" % tl), ur], [bres[PA]])
                            if tl == tpg - 1:
                                for hk in range(2):
                                    ring_release(units[hk][0])
                            bk = PA
                            if s_ == 0:
                                copy(act, qf[0:n, 0, :], banks[bk][0:n, 0:512], [], [bres[bk], res("qf0")])
                                rope(qf[0:n, 0, :], n, 8, tab_i, qrb[0:n, tl, :], [res("qf0")],
                                     [res("qrb%d" % tl)], dve, pool)

                                def tq(tl=tl):
                                    for h in range(4):
                                        tr(pT[:, h * 128:h * 128 + n], qrb[0:n, tl, h * 128:(h + 1) * 128],
                                           [res("qrb%d" % tl)], [pTr])
                                    S.op(dve, lambda: nc.vector.tensor_copy(
                                        out=QT[:, :, tl * 128:tl * 128 + n],
                                        in_=pT[:, 0:512].rearrange("p (h t) -> p h t", h=4)[:, :, 0:n]),
                                        [], [pTr, res("QT")])
                                defer_T(tq)
                            elif s_ == 1:
                                copy(act, qf[0:n, 0, :], banks[bk][0:n, 0:512], [], [bres[bk], res("qf0")])
                                rope(qf[0:n, 0, :], n, 8, tab_i, kout[0:n, 0, :], [res("qf0")],
                                     [res("kout0")], dve, pool)
                                S.dma(dstream("ko0"), G.outs["dk"][ti * n:(ti + 1) * n, :], kout[0:n, 0, :],
                                      reads=[res("kout0")], writes=[out_res])
                                S.op(pool, lambda: nc.gpsimd.tensor_copy(out=qrb[0:n, tl, :], in_=kout[0:n, 0, :]),
                                     [res("kout0")], [res("qrb%d" % tl)])

                                def tk(tl=tl, kt=kt, tok0=tok0):
                                    for h in range(4):
                                        tr(pT[:, h * 128:h * 128 + n], qrb[0:n, tl, h * 128:(h + 1) * 128],
                                           [res("qrb%d" % tl)], [pTr])
                                    S.op(dve, lambda: nc.vector.tensor_copy(
                                        out=KT[:, :, tok0:tok0 + n],
                                        in_=pT[:, 0:512].rearrange("p (h t) -> p h t", h=4)[:, :, 0:n]),
                                        [], [pTr, kvr("KT", kt)])
                                defer_T(tk)
                            elif s_ == 2:
                                copy(act, vout[0:n, 0, :], banks[bk][0:n, 0:512], [], [bres[bk], res("vout0")])
                                S.dma(dstream("vo0"), G.outs["dv"][ti * n:(ti + 1) * n, :], vout[0:n, 0, :],
                                      reads=[res("vout0")], writes=[out_res])
                                S.op(pool, lambda: nc.gpsimd.tensor_copy(
                                    out=VX[0:n, kt, :, 0:128],
                                    in_=vout[0:n, 0, :].rearrange("p (h e) -> p h e", h=4)),
                                    [res("vout0")], [kvr("VX", kt)])
                            else:
                                copy(act, mf[0:n, 0, 0:448], banks[bk][0:n, 0:448], [], [bres[bk], res("qf0")])
                                mfr = res("qf0")
                                sq, sqr = statc("cq_ssq")
                                S.op(dve, lambda: nc.vector.scalar_tensor_tensor(
                                    out=ra[0:n, 0, 0:256], in0=mf[0:n, 0, 0:256], scalar=1.0, in1=mf[0:n, 0, 0:256],
                                    op0=ALU.mult, op1=ALU.mult, accum_out=sq[0:n]), [mfr], [res("ra0"), sqr])
                                rs, rsr = rstd_cached("cq", sq, sqr, 256, n)
                                S.op(dve, lambda: nc.vector.tensor_scalar(
                                    out=cqn[0:n, tl, :], in0=mf[0:n, 0, 0:256], scalar1=rs[0:n], scalar2=None,
                                    op0=ALU.mult), [mfr, rsr], [res("cqn%d" % tl)])
                                sk, skr = statc("ckv_ssq")
                                S.op(dve, lambda: nc.vector.scalar_tensor_tensor(
                                    out=rb[0:n, 0, 0:128], in0=mf[0:n, 0, 256:384], scalar=1.0,
                                    in1=mf[0:n, 0, 256:384], op0=ALU.mult, op1=ALU.mult, accum_out=sk[0:n]),
                                    [mfr], [res("rb0"), skr])
                                rs2, rsr2 = rstd_cached("ckv", sk, skr, 128, n)
                                S.op(dve, lambda: nc.vector.scalar_tensor_tensor(
                                    out=cko[0:n, 0, :], in0=mf[0:n, 0, 256:384], scalar=rs2[0:n], in1=gkv_b[0:n, :],
                                    op0=ALU.mult, op1=ALU.mult), [mfr, rsr2, res("gkv")], [res("cko0")])
                                S.dma(dstream("co0"), G.outs["ckv"][ti * n:(ti + 1) * n, :], cko[0:n, 0, :],
                                      reads=[res("cko0")], writes=[out_res])
                                S.op(pool, lambda: nc.gpsimd.tensor_copy(out=CX[0:n, kt, 0:128], in_=cko[0:n, 0, :]),
                                     [res("cko0")], [kvr("CX", kt)])
                                rope(mf[0:n, 0, 384:448], n, 1, tab_i, kro[0:n, tl, :], [mfr],
                                     [res("kro%d" % tl)], dve, pool)
                                S.dma(dstream("ro%d" % tl), G.outs["kr"][ti * n:(ti + 1) * n, :], kro[0:n, tl, :],
                                      reads=[res("kro%d" % tl)], writes=[out_res])
                                for dd in range(2):
                                    S.op(pool, lambda dd=dd: nc.gpsimd.tensor_copy(
                                        out=krd[0:n, tl, dd * 64:(dd + 1) * 64], in_=kro[0:n, tl, :]),
                                        [res("kro%d" % tl)], [res("krd%d" % tl)])

                                def tm(tl=tl, kt=kt, tok0=tok0):
                                    for c in range(2):
                                        tr(pT[:, c * 128:c * 128 + n], cqn[0:n, tl, c * 128:(c + 1) * 128],
                                           [res("cqn%d" % tl)], [pTr])
                                    tr(pT[:, 256:256 + n], CX[0:n, kt, 0:128], [kvr("CX", kt)], [pTr])
                                    tr(pT[:, 384:384 + n], krd[0:n, tl, :], [res("krd%d" % tl)], [pTr])
                                    S.op(dve, lambda: nc.vector.tensor_copy(
                                        out=cqT[:, :, tl * 128:tl * 128 + n],
                                        in_=pT[:, 0:256].rearrange("p (c t) -> p c t", c=2)[:, :, 0:n]),
                                        [], [pTr, res("cqT")])
                                    S.op(dve, lambda: nc.vector.tensor_copy(out=CT[:, tok0:tok0 + n],
                                                                            in_=pT[:, 256:256 + n]),
                                         [], [pTr, kvr("CT", kt)])
                                    S.op(dve, lambda: nc.vector.tensor_copy(out=RT[:, tok0:tok0 + n],
                                                                            in_=pT[:, 384:384 + n]),
                                         [], [pTr, kvr("RT", kt)])
                                defer_T(tm)
                        stages.append(st_seg)

                for hp in range(2):
                    def st_qp(hp=hp):
                        flush_T(hp == 0)
                        for hh in range(2):
                            h = hp * 2 + hh
                            for c in range(2):
                                mm(banks[PA][:, hh * 256:hh * 256 + nq], wq_eff[:, c, h, :], cqT[:, c, 0:nq],
                                   c == 0, c == 1, [res("cqT"), res("wq_eff")], [bres[PA]])
                        S.op(dve, lambda hp=hp: nc.vector.tensor_copy(
                            out=QpT[:, hp * 2:hp * 2 + 2, 0:nq],
                            in_=banks[PA][:, :].rearrange("p (h t) -> p h t", h=2)[:, :, 0:nq]),
                            [], [bres[PA], res("QpT")])
                    stages.append(st_qp)

                for tl in range(tpg):
                    def st_qr(tl=tl):
                        flush_T()
                        ti = G.g * tpg + tl
                        tab_i = G.tab0 + ti
                        for c in range(2):
                            mm(banks[PA][0:n, 0:256], cqT[:, c, tl * 128:tl * 128 + n],
                               wqr[:, c].rearrange("p h r -> p (h r)"), c == 0, c == 1, [res("cqT"), res("wqr")],
                               [bres[PA]])
                        copy(act, qf[0:n, 0, 0:256], banks[PA][0:n, 0:256], [], [bres[PA], res("qf0")])
                        rope(qf[0:n, 0, 0:256], n, 4, tab_i, qrb[0:n, tl, 0:256], [res("qf0")],
                             [res("qrb%d" % tl)], dve, pool)

                        def tqr(tl=tl):
                            for u in range(2):
                                tr(pT[:, u * 128:u * 128 + n], qrb[0:n, tl, u * 128:(u + 1) * 128],
                                   [res("qrb%d" % tl)], [pTr])
                            S.op(dve, lambda: nc.vector.tensor_copy(
                                out=QRT[:, :, tl * 128:tl * 128 + n],
                                in_=pT[:, 0:256].rearrange("p (u t) -> p u t", u=2)[:, :, 0:n]),
                                [], [pTr, res("QRT")])
                        defer_T(tqr)
                    stages.append(st_qr)
                stages.append(flush_T)
                stages.append(lambda: flush_T(True))
                return stages

            def attention(G):
                n, tpg, nq, g, npast = G.tsz, G.tpg, G.nq, G.g, G.npast
                pairs = []
                if not G.sample:
                    for j in range(g):
                        pairs.append([(2 * j, 128), (2 * j + 1, 128)])
                    pairs.append("diag")
                else:
                    for j in range(npast // 2):
                        pairs.append([(2 * j, 128), (2 * j + 1, 128)])
                    pairs.append([(npast, n)])
                qts = [(q0, min(128, nq - q0)) for q0 in range(0, nq, 128)]
                pS = [[3, 5], [4, 6]]
                pO = [7, 0]
                deferred = []

                def make_unit(unit_i):
                    is_diff = unit_i < 4
                    scale = DIFF_SCALE if is_diff else MLA_SCALE
                    first_av = [True, True]

                    def emit_S(pi, pr):
                        buf = pi % 2
                        tl_list = pr if pr != "diag" else [(2 * g, 128), (2 * g + 1, 128)]
                        for i, (kt, nk) in enumerate(tl_list):
                            c0 = kt * 128
                            for m in range(2):
                                bk = pS[m][buf]
                                outp = banks[bk][0:nk, i * 256:i * 256 + nq]
                                if is_diff:
                                    h = unit_i
                                    mm(outp, KT[m * 64:(m + 1) * 64, h, c0:c0 + nk], QT[m * 64:(m + 1) * 64, h, 0:nq],
                                       True, True, [kvr("KT", kt), res("QT")], [bres[bk]])
                                else:
                                    h = (unit_i - 4) * 2 + m
                                    mm(outp, CT[:, c0:c0 + nk], QpT[:, h, 0:nq], True, False,
                                       [kvr("CT", kt), res("QpT")], [bres[bk]])
                                    mm(outp, RT[m * 64:(m + 1) * 64, c0:c0 + nk],
                                       QRT[m * 64:(m + 1) * 64, unit_i - 4, 0:nq], False, True,
                                       [kvr("RT", kt), res("QRT")], [bres[bk]])
                        bks = [bres[pS[0][buf]], bres[pS[1][buf]]]
                        ptrs = [res("PT0%d" % buf), res("PT1%d" % buf)]
                        b0 = (pS[0][buf] - 3) * 512
                        bv = pS_all[:, b0:b0 + 1024].rearrange("p (m i q) -> p m i q", m=2, i=2)
                        if pr == "diag":
                            regs = [(0, 64, 0, 0, 256), (64, 128, 0, 64, 256), (0, 64, 1, 128, 256),
                                    (64, 128, 1, 192, 256)]
                            for (p0, p1, i, q0, q1) in [(64, 128, 0, 0, 64), (64, 128, 1, 128, 192)]:
                                S.op(pool, lambda: nc.gpsimd.memset(PT[p0:p1, :, buf, i, q0:q1], 0.0), [], ptrs)
                            for (p0, p1, i, q0, q1) in regs:
                                S.op(act, lambda: nc.scalar.activation(
                                    out=PT[p0:p1, :, buf, i, q0:q1], in_=bv[p0:p1, :, i, q0:q1], func=AF.Exp,
                                    scale=scale), [], bks + ptrs)
                        elif len(pr) == 2:
                            S.op(act, lambda: nc.scalar.activation(
                                out=PT[:, :, buf, :, 0:nq], in_=bv[:, :, :, 0:nq], func=AF.Exp, scale=scale),
                                [], bks + ptrs)
                        else:
                            nk = pr[0][1]
                            S.op(act, lambda: nc.scalar.activation(
                                out=PT[0:nk, :, buf, 0, 0:nq], in_=bv[0:nk, :, 0, 0:nq], func=AF.Exp,
                                scale=scale), [], bks + ptrs)

                    def emit_AV(pi, pr, last):
                        buf = pi % 2
                        if pr == "diag":
                            items = [(2 * g, 128, 0, (0, 1)), (2 * g + 1, 128, 1, (1,))]
                        else:
                            items = [(kt, nk, i, tuple(range(len(qts)))) for i, (kt, nk) in enumerate(pr)]
                        for qi, (q0, nqt) in enumerate(qts):
                            bk = pO[qi]
                            ov = banks[bk][:, :].rearrange("p (m x) -> p m x", m=2)
                            for m in range(2):
                                its = [it for it in items if qi in it[3]]
                                for ii, (kt, nk, i, _q) in enumerate(its):
                                    lhsT = PT[0:nk, m, buf, i, q0:q0 + nqt]
                                    ptr_ = res("PT%d%d" % (m, buf))
                                    if is_diff:
                                        rhs = VX[0:nk, kt, unit_i, 0:129]
                                        rr = [kvr("VX", kt), res("VXones")]
                                    else:
                                        rhs = CX[0:nk, kt, 0:129]
                                        rr = [kvr("CX", kt), res("CXones")]
                                    start = first_av[qi] and m == 0
                                    if start:
                                        first_av[qi] = False
                                    mm(ov[0:nqt, m, 0:129], lhsT, rhs, start, last and ii == len(its) - 1,
                                       [ptr_] + rr, [bres[bk]])

                    def finish_unit():
                        def each_q(fn):
                            for qi, (q0, nqt) in enumerate(qts):
                                fn(qi, nqt)

                        def e_copy(qi, nqt):
                            bk = pO[qi]
                            ov = banks[bk][:, :].rearrange("p (m x) -> p m x", m=2)
                            S.op(dve, lambda: nc.vector.tensor_copy(out=osb[0:nqt, qi, :, :], in_=ov[0:nqt, :, 0:129]),
                                 [], [bres[bk], res("osb%d" % qi)])
                        each_q(e_copy)

                        def e_recip(qi, nqt):
                            rsum, rsumr = statc("rsum%d" % qi, 2)
                            S.op(dve, lambda: nc.vector.reciprocal(out=rsum[0:nqt, 0:2], in_=osb[0:nqt, qi, :, 128]),
                                 [res("osb%d" % qi)], [rsumr])
                        each_q(e_recip)
                        if is_diff:
                            def e_r1(qi, nqt):
                                rsum, rsumr = statc("rsum%d" % qi, 2)
                                r1, r1r = statc("r1_%d" % qi)
                                S.op(dve, lambda: nc.vector.tensor_tensor(out=r1[0:nqt], in0=rsum[0:nqt, 1:2],
                                                                          in1=neg_lam[0:nqt], op=ALU.mult),
                                     [rsumr, res("neg_lam")], [r1r])
                            each_q(e_r1)

                            def e_t0(qi, nqt):
                                rsum, rsumr = statc("rsum%d" % qi, 2)
                                S.op(dve, lambda: nc.vector.tensor_scalar(
                                    out=t0[0:nqt, qi, :], in0=osb[0:nqt, qi, 0, 0:128], scalar1=rsum[0:nqt, 0:1],
                                    scalar2=None, op0=ALU.mult), [rsumr, res("osb%d" % qi)], [res("t0%d" % qi)])
                            each_q(e_t0)

                            def e_od(qi, nqt):
                                r1, r1r = statc("r1_%d" % qi)
                                S.op(dve, lambda: nc.vector.scalar_tensor_tensor(
                                    out=od[0:nqt, qi, :], in0=osb[0:nqt, qi, 1, 0:128], scalar=r1[0:nqt],
                                    in1=t0[0:nqt, qi, :], op0=ALU.mult, op1=ALU.add),
                                    [r1r, res("osb%d" % qi), res("t0%d" % qi)], [res("od%d" % qi)])
                            each_q(e_od)

                            def e_ssq(qi, nqt):
                                sq, sqr = statc("od_ssq%d" % qi)
                                S.op(dve, lambda: nc.vector.scalar_tensor_tensor(
                                    out=t0[0:nqt, qi, :], in0=od[0:nqt, qi, :], scalar=1.0, in1=od[0:nqt, qi, :],
                                    op0=ALU.mult, op1=ALU.mult, accum_out=sq[0:nqt]),
                                    [res("od%d" % qi)], [res("t0%d" % qi), sqr])
                            each_q(e_ssq)
                            rss = {}

                            def e_rstd(qi, nqt):
                                sq, sqr = statc("od_ssq%d" % qi)
                                rss[qi] = rstd_cached("od%d" % qi, sq, sqr, 128, nqt)
                            each_q(e_rstd)

                            def e_on(qi, nqt):
                                rs, rsr = rss[qi]
                                S.op(dve, lambda: nc.vector.tensor_scalar(
                                    out=onall[0:nqt, qi, unit_i % 2, 0, :], in0=od[0:nqt, qi, :], scalar1=rs[0:nqt],
                                    scalar2=None, op0=ALU.mult), [res("od%d" % qi), rsr],
                                    [res("onall%d_0" % (unit_i % 2))])
                            each_q(e_on)
                        else:
                            for m in range(2):
                                def e_onm(qi, nqt, m=m):
                                    rsum, rsumr = statc("rsum%d" % qi, 2)
                                    S.op(dve, lambda: nc.vector.tensor_scalar(
                                        out=onall[0:nqt, qi, unit_i % 2, m, :], in0=osb[0:nqt, qi, m, 0:128],
                                        scalar1=rsum[0:nqt, m:m + 1], scalar2=None, op0=ALU.mult),
                                        [rsumr, res("osb%d" % qi)], [res("onall%d_%d" % (unit_i % 2, m))])
                                each_q(e_onm)

                        def tail(unit_i=unit_i, is_diff=is_diff):
                            cs = [unit_i] if is_diff else [4 + (unit_i - 4) * 2, 5 + (unit_i - 4) * 2]
                            for ci, c in enumerate(cs):
                                for qi, (q0, nqt) in enumerate(qts):
                                    tr(pT[:, (ci * 2 + qi) * 128:(ci * 2 + qi) * 128 + nqt],
                                       onall[0:nqt, qi, unit_i % 2, ci, :], [res("onall%d_%d" % (unit_i % 2, ci))], [pTr])
                            S.op(dve, lambda: nc.vector.tensor_copy(
                                out=mixT[:, cs[0]:cs[0] + len(cs), 0:nq],
                                in_=pT[:, 0:512].rearrange("p (c t) -> p c t", c=2)[:, 0:len(cs), 0:nq]),
                                [], [pTr, res("xnT0"), res("xnT1")] + [res("mixT%d" % c) for c in cs])
                        deferred.append(tail)
                    return emit_S, emit_AV, finish_unit

                unit_fns = [make_unit(u) for u in range(6)]
                np_ = len(pairs)
                steps = [(u, pr) for u in range(6) for pr in pairs]

                def run_av(pd):
                    pu, psi, ppr, plast = pd
                    unit_fns[pu][1](psi, ppr, plast)
                    if plast:
                        while deferred:
                            deferred.pop(0)()
                        unit_fns[pu][2]()
                pend = None
                for si, (u, pr) in enumerate(steps):
                    unit_fns[u][0](si, pr)
                    if pend is not None:
                        run_av(pend)
                    pend = (u, si, pr, si % np_ == np_ - 1)
                run_av(pend)
                while deferred:
                    deferred.pop(0)()

            def outproj(G):
                n, tpg = G.tsz, G.tpg
                mixr = [res("mixT%d" % c) for c in range(8)] + [res("xnT0"), res("xnT1")]
                us = {}
                for half in range(2):
                    for hk in range(2):
                        us[(half, hk)] = ring_next(("wout", half, hk))
                for tl in range(tpg):
                    xr = res("xg%d%d" % (G.slot, tl))
                    for half in range(2):
                        bk = half
                        for c in range(8):
                            u_, unit, ur = us[(half, c // 4)]
                            uv = unit[:, 0:2048].rearrange("p (k n) -> p k n", k=4)
                            mm(banks[bk][0:n, 0:512], mixT[:, c, tl * 128:tl * 128 + n], uv[:, c % 4, :], c == 0,
                               c == 7, mixr + [ur], [bres[bk]])
                        S.op(dve, lambda: nc.vector.tensor_tensor(
                            out=xg[0:n, G.slot, tl, half * 512:(half + 1) * 512], in0=banks[bk][0:n, 0:512],
                            in1=xg[0:n, G.slot, tl, half * 512:(half + 1) * 512], op=ALU.add), [xr], [bres[bk], xr])
                    xt = xg[0:n, G.slot, tl, :]
                    xb = (xn, xn2)[tl]
                    ssq, ssqr = statc("f_ssq%d" % tl)
                    S.op(act, lambda: nc.scalar.activation(out=xb[0:n, :], in_=xt, func=AF.Square,
                                                           accum_out=ssq[0:n]), [xr], [res("xn%d" % tl), ssqr])
                    rs, rsr = rstd_cached("f%d" % tl, ssq, ssqr, D, n)
                    S.op(dve, lambda: nc.vector.tensor_scalar(out=xb[0:n, :], in0=xt, scalar1=rs[0:n], scalar2=None,
                                                              op0=ALU.mult), [xr, rsr], [res("xn%d" % tl)])
                for h_ in us.values():
                    ring_release(h_[0])

            def ffn(G, stages, depth=2):
                n, tpg, nq = G.tsz, G.tpg, G.nq
                if G.g == 0:
                    if G.conv_init is None:
                        S.op(pool, lambda: nc.gpsimd.memset(gstate[:], 0.0), [], [res("gstate")])
                    else:
                        for r_ in range(2):
                            S.dma(dstream("c0"), gstate[:, :, r_],
                                  G.conv_init[r_].rearrange("(j p) -> p j", p=128),
                                  writes=[res("gstate")], allow_slow_non_contiguous=True)
                for tl in range(tpg):
                    xb = (xn, xn2)[tl]
                    for k in range(8):
                        tr(pT[:, k * 128:k * 128 + n], xb[0:n, k * 128:(k + 1) * 128], [res("xn%d" % tl)], [pTr])
                    S.op(dve, lambda: nc.vector.tensor_copy(
                        out=h2T[:, :, tl * 128:tl * 128 + n],
                        in_=pT[:, :].rearrange("p (k t) -> p k t", k=8)[:, :, 0:n]), [], [pTr, res("h2T%d" % tl)])
                pY = [[3, 4], [5, 6]]
                pend = []
                h2r = [res("h2T%d" % tl) for tl in range(tpg)]

                def emit_down(j, b3, ud):
                    u_, unit, ur = ud
                    for tl in range(tpg):
                        for half in range(2):
                            bk = pY[tl][half]
                            mm(banks[bk][0:n, 0:512], actT[:, b3, tl * 128:tl * 128 + n],
                               unit[:, half * 512:(half + 1) * 512], j == 0, j == NCH - 1,
                               [res("actT%d_%d" % (b3, tl)), ur], [bres[bk]])
                    ring_release(u_)

                stages = list(stages)
                for j in range(NCH):
                    b = j % 2
                    uu = ring_next(("up", j, 0))
                    ud = ring_next(("down", j, 0))
                    unit = uu[1]
                    uvv = unit[:, 0:2048].rearrange("p (k u c) -> p k u c", k=8, u=2)
                    pv = banks[b][:, :].rearrange("p (u t) -> p u t", u=2)
                    for u in range(2):
                        for k in range(8):
                            mm(pv[:, u, 0:nq], uvv[:, k, u, :], h2T[:, k, 0:nq], k == 0, k == 7,
                               h2r + [uu[2]], [bres[b]])
                    ring_release(uu[0])
                    if len(pend) == depth:
                        emit_down(*pend.pop(0))
                    b3 = j % 3
                    gb = gbuf[:, b, :]
                    gbr = res("gbuf%d" % b)
                    ubr = res("ub%d" % b)
                    copy(act, gb[:, 2:2 + nq], pv[:, 1, 0:nq], [], [bres[b], gbr])
                    copy(act, ub[:, b, 0:nq], pv[:, 0, 0:nq], [], [bres[b], ubr])
                    S.op(pool, lambda: nc.gpsimd.tensor_copy(out=gb[:, 0:2], in_=gstate[:, j, :]),
                         [res("gstate")], [gbr])
                    S.op(pool, lambda: nc.gpsimd.tensor_copy(out=gstate[:, j, :], in_=gb[:, nq:nq + 2]),
                         [gbr], [res("gstate")])
                    halves = [(0, nq)] if nq <= 128 else [(0, 128), (128, nq)]
                    hres = [res("cbuf%d_%d" % (b, hi)) for hi in range(len(halves))]
                    for hi, (h0, h1) in enumerate(halves):
                        S.op(dve, lambda: nc.vector.tensor_scalar(
                            out=cbuf[:, b, h0:h1], in0=gb[:, h0:h1], scalar1=wconv[:, j, 0:1],
                            scalar2=bconv[:, j:j + 1], op0=ALU.mult, op1=ALU.add), [gbr, res("wconv")], [hres[hi]])
                    for hi, (h0, h1) in enumerate(halves):
                        S.op(dve, lambda: nc.vector.scalar_tensor_tensor(
                            out=cbuf[:, b, h0:h1], in0=gb[:, 1 + h0:1 + h1], scalar=wconv[:, j, 1:2],
                            in1=cbuf[:, b, h0:h1], op0=ALU.mult, op1=ALU.add), [gbr, res("wconv"), hres[hi]],
                            [hres[hi]])
                    for hi, (h0, h1) in enumerate(halves):
                        S.op(dve, lambda: nc.vector.scalar_tensor_tensor(
                            out=cbuf[:, b, h0:h1], in0=gb[:, 2 + h0:2 + h1], scalar=wconv[:, j, 2:3],
                            in1=cbuf[:, b, h0:h1], op0=ALU.mult, op1=ALU.add), [gbr, res("wconv"), hres[hi]],
                            [hres[hi]])
                    for hi, (h0, h1) in enumerate(halves):
                        S.op(act, lambda: nc.scalar.activation(out=cbuf[:, b, h0:h1], in_=cbuf[:, b, h0:h1],
                                                               func=AF.Silu), [hres[hi]], [hres[hi]])
                    for hi, (h0, h1) in enumerate(halves):
                        S.op(dve, lambda: nc.vector.tensor_tensor(out=actT[:, b3, h0:h1], in0=ub[:, b, h0:h1],
                                                                  in1=cbuf[:, b, h0:h1], op=ALU.mult),
                             [hres[hi], ubr], [res("actT%d_%d" % (b3, hi))])
                    pend.append((j, b3, ud))
                    if stages and j >= 1:
                        stg = stages.pop(0)
                        if stg is not None:
                            stg()
                while pend:
                    emit_down(*pend.pop(0))
                while stages:
                    stg = stages.pop(0)
                    if stg is not None:
                        stg()

                for tl in range(tpg):
                    ti = G.g * tpg + tl
                    xr = res("xg%d%d" % (G.slot, tl))
                    for half in range(2):
                        bk = pY[tl][half]
                        S.op(dve, lambda: nc.vector.tensor_tensor(
                            out=xg[0:n, G.slot, tl, half * 512:(half + 1) * 512], in0=banks[bk][0:n, 0:512],
                            in1=xg[0:n, G.slot, tl, half * 512:(half + 1) * 512], op=ALU.add), [xr], [bres[bk], xr])
                    xt = xg[0:n, G.slot, tl, :]
                    ssq, ssqr = statc("y_ssq%d" % tl)
                    xb = (xn, xn2)[tl]
                    S.op(act, lambda: nc.scalar.activation(out=xb[0:n, :], in_=xt, func=AF.Square,
                                                           accum_out=ssq[0:n]), [xr], [res("xn%d" % tl), ssqr])
                    rs, rsr = rstd_cached("y%d" % tl, ssq, ssqr, D, n)
                    S.op(dve, lambda: nc.vector.scalar_tensor_tensor(
                        out=xt, in0=xt, scalar=rs[0:n], in1=gfin_b[0:n, :], op0=ALU.mult, op1=ALU.mult),
                        [xr, rsr, res("gfin")], [xr])
                    S.dma(dstream("yo%d" % tl), G.outs["y"][ti * n:(ti + 1) * n, :], xt,
                          reads=[xr], writes=[out_res])
                if G.g == G.ng - 1:
                    for r_ in range(2):
                        S.dma(dstream("c%d" % (1 + r_)), G.outs["conv"][r_].rearrange("(j p) -> p j", p=128),
                              gstate[:, :, r_], reads=[res("gstate")], writes=[out_res],
                              allow_slow_non_contiguous=True)

            load_x(groups[0])
            if len(groups) > 1:
                load_x(groups[1])
            for stg in phaseA_stages(groups[0]):
                if stg is not None:
                    stg()
            for f_, G in enumerate(groups):
                attention(G)
                outproj(G)
                nxt = phaseA_stages(groups[f_ + 1]) if f_ + 1 < len(groups) else []
                dense = f_ + 1 < len(groups) and groups[f_ + 1].tpg == 1
                ffn(G, nxt, 1 if dense else 2)
                if f_ + 2 < len(groups):
                    load_x(groups[f_ + 2])
            S.barrier()
            return ring_log

        plan = emit_main(None)
        emit_main(plan)
    return nc


def rope_tables(seq, past, dec):
    nt = seq // 128
    half = 32
    inv = (np.float32(10000.0) ** (-np.arange(half, dtype=np.float32) * np.float32(2.0 / 64))).astype(np.float32)
    pos = np.zeros((128, nt + 1), np.float32)
    for t in range(nt):
        pos[:, t] = t * 128 + np.arange(128)
    pos[:, nt] = past + np.arange(128)
    ang = (pos[:, :, None] * inv[None, None, :]).astype(np.float32)
    return np.cos(ang).astype(np.float32), np.sin(ang).astype(np.float32)


def make_in_maps(inputs, n_cores, n_prompt, seq, past, dec):
    f = lambda a: np.ascontiguousarray(np.asarray(a, dtype=np.float32))
    cos, sin = rope_tables(seq, past, dec)
    common = {
        "g_attn_pk": f(inputs["g_attn"][0].reshape(8, 128).T),
        "g_ffn_pk": f(inputs["g_ffn"][0].reshape(8, 128).T),
        "g_q_pk": f(inputs["g_q_lora"][0].reshape(2, 128).T),
        "g_sub_p": f(inputs["g_diff_sub"][0].reshape(128, 1)),
        "g_kv": f(inputs["g_kv_lora"][0].reshape(1, 128)),
        "g_final": f(inputs["g_final"].reshape(1, D)),
        "lamv": f(np.stack([inputs["lambda_q1"][0], inputs["lambda_k1"][0], inputs["lambda_q2"][0],
                            inputs["lambda_k2"][0]], 0).reshape(1, 256)),
        "wconv_pk": f(inputs["w_conv"][0].reshape(3, NCH, 128).transpose(2, 1, 0)),
        "bconv_pk": f(inputs["b_conv"][0].reshape(NCH, 128).T),
        "w_in": f(inputs["w_in"][0]), "w_qb": f(inputs["w_q_b"][0]), "w_kvb": f(inputs["w_kv_b"][0]),
        "w_out": f(inputs["w_out"][0]), "w_up": f(inputs["w_up"][0]), "w_down": f(inputs["w_down"][0]),
        "ident": np.eye(128, dtype=np.float32).astype(ml_dtypes.bfloat16),
        "cos_t": cos, "sin_t": sin,
    }
    maps = []
    for c in range(n_cores):
        m = dict(common)
        m["xp"] = f(inputs["x_prompt"][c * n_prompt:(c + 1) * n_prompt])
        m["xs"] = f(inputs["x_sample"][c])
        m["cdk"] = f(inputs["cache_diff_k"][0, c].reshape(past, 512))
        m["cdv"] = f(inputs["cache_diff_v"][0, c].reshape(past, 512))
        m["cckv"] = f(inputs["cache_mla_ckv"][0, c])
        m["ckr"] = f(inputs["cache_mla_krope"][0, c])
        m["sconv"] = f(inputs["state_conv"][0, c])
        maps.append(m)
    return maps


_NC_CACHE = {}


def kernel(**inputs):
    inputs = {k: np.asarray(v) for k, v in inputs.items()}
    B, seq, _ = inputs["x_prompt"].shape
    n_cores = inputs["x_sample"].shape[0]
    n_prompt = B // n_cores
    dec = inputs["x_sample"].shape[1]
    past = inputs["cache_diff_k"].shape[2]
    key = (n_prompt, seq, past, dec)
    if key not in _NC_CACHE:
        _NC_CACHE[key] = build(n_prompt=n_prompt, seq=seq, sample=True, past=past, dec=dec)
    nc = _NC_CACHE[key]
    maps = make_in_maps(inputs, n_cores, n_prompt, seq, past, dec)
    res = run_bass_kernel_spmd(nc, maps, core_ids=list(range(n_cores)))
    r = res.results
    cat = lambda k: np.concatenate([np.asarray(x[k], dtype=np.float32) for x in r], axis=0)
    stk = lambda k: np.stack([np.asarray(x[k], dtype=np.float32) for x in r], axis=0)
    y_prompt = cat("yp")
    y_sample = stk("ys")
    dk_p = cat("dkp").reshape(1, B, seq, 4, 2, 64)
    dv_p = cat("dvp").reshape(1, B, seq, 4, 128)
    ckv_p = cat("ckvp").reshape(1, B, seq, 128)
    kr_p = cat("krp").reshape(1, B, seq, 64)
    conv_p = cat("convp").reshape(1, B, 2, D_FF)
    dk_s = stk("dks").reshape(1, n_cores, dec, 4, 2, 64)
    dv_s = stk("dvs").reshape(1, n_cores, dec, 4, 128)
    ckv_s = stk("ckvs").reshape(1, n_cores, dec, 128)
    kr_s = stk("krs").reshape(1, n_cores, dec, 64)
    conv_s = stk("convs").reshape(1, n_cores, 2, D_FF)
    return (y_prompt, y_sample, dk_p, dv_p, ckv_p, kr_p, conv_p, dk_s, dv_s, ckv_s, kr_s, conv_s)
```

```python
import math
from contextlib import ExitStack

import numpy as np
import ml_dtypes

import concourse.bass as bass
import concourse.mybir as mybir
from concourse.bass_utils import run_bass_kernel_spmd

F32 = mybir.dt.float32
BF16 = mybir.dt.bfloat16
ALU = mybir.AluOpType
AF = mybir.ActivationFunctionType

D = 1024
NCORES = 8
SEQ = 4096
DEC_SEQ = 64
PAST = 1024
IN_COLS = 1984
D_FF = 2816
NCH = 22
EPS = 1e-6
LAM_INIT = 0.8 - 0.6 * math.exp(-0.3 * 0)
MLA_SCALE = (128 + 64) ** -0.5
DIFF_SCALE = 64 ** -0.5
RING_SLOTS = 6
RINGD_SLOTS = 6
RING_W = 2048


class Res:
    __slots__ = ("name", "w", "r")

    def __init__(self, name):
        self.name = name
        self.w = None
        self.r = []


class Stream:
    def __init__(self, sched, name, eng, inc, limit):
        self.sched, self.name, self.eng, self.inc, self.limit = sched, name, eng, inc, limit
        self.epoch = 0
        self.sem = sched.new_sem(name + "_0")
        self.count = 0
        self.total = 0
        self.seen = {}

    def bump(self):
        self.count += 1
        self.total += 1
        ev = ((self.name, self.epoch), self.sem, self.count * self.inc, self.name)
        if self.count >= self.limit:
            self.epoch += 1
            self.sem = self.sched.new_sem("%s_%d" % (self.name, self.epoch))
            self.count = 0
        return ev


class Sched:
    def __init__(self, nc, stack):
        self.nc = nc
        self.stack = stack
        self.nsem = 0
        self.pe = Stream(self, "pe", nc.tensor, 1, 20000)
        self.act = Stream(self, "act", nc.scalar, 1, 20000)
        self.dve = Stream(self, "dve", nc.vector, 1, 20000)
        self.pool = Stream(self, "pool", nc.gpsimd, 1, 20000)
        self.sp = nc.sync
        self.sp_seen = {}
        self.dma_streams = []
        self.last_ev = {}

    def new_sem(self, name):
        self.nsem += 1
        return self.stack.enter_context(self.nc.semaphore(name))

    def dma_stream(self, name):
        s = Stream(self, name, None, 16, 2000)
        self.dma_streams.append(s)
        return s

    def _deps(self, stream_name, reads, writes, same_engine_raw=True):
        need = {}

        def add(ev, same_ok):
            if ev is None:
                return
            key, sem, val, sname = ev
            if sname == stream_name and not same_ok:
                return
            if key not in need or need[key][1] < val:
                need[key] = (sem, val)

        for r in reads:
            add(r.w, same_engine_raw)
        for w in writes:
            add(w.w, False)
            for ev in w.r:
                add(ev, False)
        return need

    @staticmethod
    def _record(ev, reads, writes):
        for r in reads:
            r.r.append(ev)
            if len(r.r) > 48:
                best = {}
                for e in r.r:
                    if e[0] not in best or best[e[0]][2] < e[2]:
                        best[e[0]] = e
                r.r = list(best.values())
        for w in writes:
            w.w = ev
            w.r = []

    def op(self, st, fn, reads=(), writes=()):
        need = self._deps(st.name, reads, writes, same_engine_raw=(st is not self.pe))
        for key, (sem, val) in need.items():
            if st.seen.get(key, 0) < val:
                st.eng.wait_ge(sem, val)
                st.seen[key] = val
        ins = fn()
        ins.then_inc(st.sem, 1)
        ev = st.bump()
        self.last_ev[st.name] = ev
        self._record(ev, reads, writes)
        return ins

    def dma(self, ds, out, in_, reads=(), writes=(), **kw):
        need = self._deps("sp:" + ds.name, reads, writes)
        for key, (sem, val) in need.items():
            if self.sp_seen.get(key, 0) < val:
                self.sp.wait_ge(sem, val)
                self.sp_seen[key] = val
        ins = self.sp.dma_start(out=out, in_=in_, **kw)
        ins.then_inc(ds.sem, 16)
        ev = ds.bump()
        self.last_ev[ds.name] = ev
        self._record(ev, reads, writes)
        return ins

    def barrier(self):
        evs = list(self.last_ev.values())
        for st in (self.pe, self.act, self.dve, self.pool):
            for key, sem, val, sname in evs:
                if sname == st.name:
                    continue
                if st.seen.get(key, 0) < val:
                    st.eng.wait_ge(sem, val)
                    st.seen[key] = val
        for key, sem, val, sname in evs:
            if self.sp_seen.get(key, 0) < val:
                self.sp.wait_ge(sem, val)
                self.sp_seen[key] = val


def build(n_prompt=2, seq=SEQ, sample=True, past=PAST, dec=DEC_SEQ):
    nc = bass.Bass("TRN2", target_bir_lowering=False)
    NT = seq // 128
    NKT = max(NT, past // 128 + 1)
    NTAB = NT + 1

    def din(name, shape, dt=F32):
        return nc.dram_tensor(name, list(shape), dt, kind="ExternalInput").ap()

    def dout(name, shape, dt=F32):
        return nc.dram_tensor(name, list(shape), dt, kind="ExternalOutput").ap()

    xp = din("xp", [n_prompt, seq, D])
    xs = din("xs", [dec, D])
    cdk = din("cdk", [past, 512])
    cdv = din("cdv", [past, 512])
    cckv = din("cckv", [past, 128])
    ckr = din("ckr", [past, 64])
    sconv = din("sconv", [2, D_FF])
    g_attn_pk = din("g_attn_pk", [128, 8])
    g_ffn_pk = din("g_ffn_pk", [128, 8])
    g_q_pk = din("g_q_pk", [128, 2])
    g_sub_p = din("g_sub_p", [128, 1])
    g_kv = din("g_kv", [1, 128])
    g_final = din("g_final", [1, D])
    lamv = din("lamv", [1, 256])
    wconv_pk = din("wconv_pk", [128, NCH, 3])
    bconv_pk = din("bconv_pk", [128, NCH])
    w_in = din("w_in", [D, IN_COLS])
    w_qb = din("w_qb", [256, 768])
    w_kvb = din("w_kvb", [128, 1024])
    w_out = din("w_out", [D, D])
    w_up = din("w_up", [D, 2 * D_FF])
    w_down = din("w_down", [D_FF, D])
    ident_d = din("ident", [128, 128], BF16)
    cos_d = din("cos_t", [128, NTAB, 32])
    sin_d = din("sin_t", [128, NTAB, 32])
    yp = dout("yp", [n_prompt, seq, D])
    dkp = dout("dkp", [n_prompt, seq, 512])
    dvp = dout("dvp", [n_prompt, seq, 512])
    ckvp = dout("ckvp", [n_prompt, seq, 128])
    krp = dout("krp", [n_prompt, seq, 64])
    convp = dout("convp", [n_prompt, 2, D_FF])
    ys = dout("ys", [dec, D])
    dks = dout("dks", [dec, 512])
    dvs = dout("dvs", [dec, 512])
    ckvs = dout("ckvs", [dec, 128])
    krs = dout("krs", [dec, 64])
    convs = dout("convs", [2, D_FF])
    win_s = nc.dram_tensor("win_s", [4, 128, 8, 512], BF16).ap()
    wout_s = nc.dram_tensor("wout_s", [2, 128, 8, 512], BF16).ap()
    ffn_s = nc.dram_tensor("ffn_s", [NCH, 128, 3072], BF16).ap()

    with ExitStack() as st:
        S = Sched(nc, st)
        pe, act, dve, pool = S.pe, S.act, S.dve, S.pool

        def sb(name, shape, dt):
            return st.enter_context(nc.sbuf_tensor("s_" + name, list(shape), dt))

        banks = [st.enter_context(nc.psum_tensor("bank%d" % i, [128, 512], F32)) for i in (0, 1)]
        pT = st.enter_context(nc.psum_tensor("bankT", [128, 1024], BF16))
        banks += [None]
        pS_all = st.enter_context(nc.psum_tensor("bankS", [128, 2048], F32))
        banks += [pS_all[:, i * 512:(i + 1) * 512] for i in range(4)]
        banks += [st.enter_context(nc.psum_tensor("bank7", [128, 512], F32))]
        bres = [Res("bank%d" % i) for i in range(8)]
        pTr = bres[2]

        out_res = Res("outputs")
        scr_res = {"win": Res("win_s"), "wout": Res("wout_s"), "ffn": Res("ffn_s")}

        ident = sb("ident", [128, 128], BF16)
        cos_t = sb("cos_t", [128, NTAB, 32], F32)
        sin_t = sb("sin_t", [128, NTAB, 32], F32)
        gkv_b = sb("gkv_b", [128, 128], F32)
        gfin_b = sb("gfin_b", [128, D], F32)
        wconv = sb("wconv", [128, NCH, 3], F32)
        bconv = sb("bconv", [128, NCH], F32)
        neg_lam = sb("neg_lam", [128, 1], F32)
        wq_eff = sb("wq_eff", [128, 2, 4, 128], BF16)
        wqr = sb("wqr", [128, 2, 4, 64], BF16)
        R = {}

        def res(name):
            if name not in R:
                R[name] = Res(name)
            return R[name]

        _stat_next = [0]

        def stat(name, width=1):
            i = _stat_next[0]
            _stat_next[0] += width
            assert _stat_next[0] <= 64
            return st_small[:, i:i + width], res("stat_" + name)

        dq = {}

        def dstream(name):
            if getattr(S, "dry", False):
                return None
            if name not in dq:
                dq[name] = S.dma_stream("d_" + name)
            return dq[name]

        def mm(out, lhsT, rhs, start, stop, reads, writes):
            S.op(pe, lambda: nc.tensor.matmul(out, lhsT=lhsT, rhs=rhs, start=start, stop=stop,
                                              skip_group_check=True), reads, writes)

        def tr(out, in_, reads, writes):
            n = in_.shape[0]
            S.op(pe, lambda: nc.tensor.transpose(out=out, in_=in_, identity=ident[0:n, 0:n]),
                 list(reads) + [res("ident")], writes)

        def eng_of(stm):
            return {"act": nc.scalar, "dve": nc.vector, "pool": nc.gpsimd}[stm.name]

        def copy(stm, out, in_, reads, writes):
            if stm is act:
                S.op(act, lambda: nc.scalar.copy(out=out, in_=in_), reads, writes)
            else:
                e = eng_of(stm)
                S.op(stm, lambda: e.tensor_copy(out=out, in_=in_), reads, writes)

        def rstd_from_ssq(ssq_ap, ssq_res, n_feat, nm, n):
            ms, msr = stat(nm + "_ms")
            rs, rsr = stat(nm + "_rs")
            S.op(dve, lambda: nc.vector.tensor_scalar(out=ms[0:n], in0=ssq_ap[0:n], scalar1=1.0 / n_feat,
                                                      scalar2=EPS, op0=ALU.mult, op1=ALU.add),
                 [ssq_res], [msr])
            S.op(act, lambda: nc.scalar.activation(out=ms[0:n], in_=ms[0:n], func=AF.Ln), [msr], [msr])
            S.op(act, lambda: nc.scalar.activation(out=rs[0:n], in_=ms[0:n], func=AF.Exp, scale=-0.5),
                 [msr], [rsr])
            return rs, rsr

        S.dma(dstream("k0"), ident[:], ident_d[:], writes=[res("ident")])
        S.dma(dstream("k1"), cos_t[:], cos_d[:], writes=[res("tab")])
        S.dma(dstream("k2"), sin_t[:], sin_d[:], writes=[res("tab")])
        S.dma(dstream("k3"), gkv_b[:], g_kv.partition_broadcast(128), writes=[res("gkv")])
        S.dma(dstream("k4"), gfin_b[:], g_final.partition_broadcast(128), writes=[res("gfin")])
        S.dma(dstream("k5"), wconv[:], wconv_pk[:], writes=[res("wconv")])
        S.dma(dstream("k6"), bconv[:], bconv_pk[:], writes=[res("wconv")])

        with ExitStack() as st2:
            def sb2(name, shape, dt):
                return st2.enter_context(nc.sbuf_tensor("t_" + name, list(shape), dt))

            gA = sb2("gA", [128, 8], F32)
            gF = sb2("gF", [128, 8], F32)
            gQ = sb2("gQ", [128, 2], F32)
            gS = sb2("gS", [128, 1], F32)
            lamb = sb2("lamb", [128, 4, 64], F32)
            lj = sb2("lj", [128, 64], F32)
            ls = sb2("ls", [128, 4], F32)
            S.dma(dstream("k7"), gA[:], g_attn_pk[:], writes=[res("gA")])
            S.dma(dstream("k8"), gF[:], g_ffn_pk[:], writes=[res("gF")])
            S.dma(dstream("k9"), gQ[:], g_q_pk[:], writes=[res("gQ")])
            S.dma(dstream("k10"), gS[:], g_sub_p[:], writes=[res("gS")])
            S.dma(dstream("klam"), lamb[:].rearrange("p a b -> p (a b)"), lamv.partition_broadcast(128),
                  writes=[res("lamb")])
            for i in range(2):
                S.op(dve, lambda i=i: nc.vector.scalar_tensor_tensor(
                    out=lj[:], in0=lamb[:, 2 * i, :], scalar=1.0, in1=lamb[:, 2 * i + 1, :],
                    op0=ALU.mult, op1=ALU.mult, accum_out=ls[:, i:i + 1]),
                    [res("lamb")], [res("lj"), res("ls")])
            S.op(act, lambda: nc.scalar.activation(out=ls[:, 2:4], in_=ls[:, 0:2], func=AF.Exp),
                 [res("ls")], [res("ls2")])
            S.op(dve, lambda: nc.vector.scalar_tensor_tensor(
                out=neg_lam[:], in0=ls[:, 3:4], scalar=-LAM_INIT, in1=ls[:, 2:3],
                op0=ALU.add, op1=ALU.subtract), [res("ls2")], [res("neg_lam")])
            S.op(dve, lambda: nc.vector.tensor_scalar(out=gS[:], in0=gS[:], scalar1=1.0 - LAM_INIT, scalar2=None,
                                                      op0=ALU.mult), [res("gS")], [res("gS")])

            NB = 4
            wst = sb2("wst", [128, NB, 2048], F32)
            wbf = sb2("wbf", [128, NB, 4, 512], BF16)
            S.op(pool, lambda: nc.gpsimd.memset(wbf[:], 0.0), [], [res("wbf%d" % b) for b in range(NB)])
            win_v = win_s.rearrange("s p k n -> p s k n")
            def win_load(k):
                b = k % NB
                S.dma(dstream("w%d" % b), wst[:, b, 0:IN_COLS], w_in[k * 128:(k + 1) * 128, :],
                      writes=[res("wst%d" % b)])
            for k in range(NB - 2):
                win_load(k)
            for k in range(8):
                b = k % NB
                if k + NB - 2 < 8:
                    win_load(k + NB - 2)
                for s_ in range(4):
                    wd = 512 if s_ < 3 else 448
                    if s_ % 2 == 0:
                        S.op(dve, lambda s_=s_, wd=wd, b=b, k=k: nc.vector.tensor_scalar(
                            out=wbf[:, b, s_, 0:wd], in0=wst[:, b, s_ * 512:s_ * 512 + wd],
                            scalar1=gA[:, k:k + 1], scalar2=None, op0=ALU.mult),
                            [res("wst%d" % b), res("gA")], [res("wbf%d" % b)])
                    else:
                        S.op(act, lambda s_=s_, wd=wd, b=b, k=k: nc.scalar.activation(
                            out=wbf[:, b, s_, 0:wd], in_=wst[:, b, s_ * 512:s_ * 512 + wd], func=AF.Copy,
                            scale=gA[:, k:k + 1]), [res("wst%d" % b), res("gA")], [res("wbf%d" % b)])
                S.dma(dstream("ws%d" % b), win_v[:, :, k, :], wbf[:, b], reads=[res("wbf%d" % b)],
                      writes=[scr_res["win"]])

            wqf = sb2("wqf", [128, 2, 768], F32)
            wqb_bf = sb2("wqb_bf", [128, 2, 768], BF16)
            wkvf = sb2("wkvf", [128, 1024], F32)
            wkvb_bf = sb2("wkvb_bf", [128, 1024], BF16)
            wkT = sb2("wkT", [128, 4, 128], BF16)
            wvT = sb2("wvT", [128, 4, 128], BF16)
            wqnT = sb2("wqnT", [128, 4, 2, 128], BF16)
            S.dma(dstream("k11"), wqf[:], w_qb.rearrange("(c p) n -> p c n", p=128), writes=[res("wqf")])
            S.dma(dstream("k12"), wkvf[:], w_kvb[:], writes=[res("wkvf")])
            for c in range(2):
                S.op(dve, lambda c=c: nc.vector.tensor_scalar(out=wqb_bf[:, c, :], in0=wqf[:, c, :],
                                                              scalar1=gQ[:, c:c + 1], scalar2=None, op0=ALU.mult),
                     [res("wqf"), res("gQ")], [res("wqb_bf")])
            S.op(pool, lambda: nc.gpsimd.tensor_copy(out=wkvb_bf[:], in_=wkvf[:]), [res("wkvf")], [res("wkvb_bf")])
            wqb_v = wqb_bf[:].rearrange("p c (h n) -> p c h n", h=4)
            S.op(dve, lambda: nc.vector.tensor_copy(out=wqr[:], in_=wqb_v[:, :, :, 128:192]),
                 [res("wqb_bf")], [res("wqr")])
            for h in range(4):
                tr(pT[:, h * 128:(h + 1) * 128], wkvb_bf[:, h * 256:h * 256 + 128], [res("wkvb_bf")], [pTr])
                tr(pT[:, 512 + h * 128:512 + (h + 1) * 128], wkvb_bf[:, h * 256 + 128:h * 256 + 256],
                   [res("wkvb_bf")], [pTr])
            S.op(dve, lambda: nc.vector.tensor_copy(out=wkT[:], in_=pT[:, 0:512].rearrange("p (h l) -> p h l", h=4)),
                 [pTr], [pTr, res("wkT")])
            S.op(dve, lambda: nc.vector.tensor_copy(out=wvT[:], in_=pT[:, 512:1024].rearrange("p (h l) -> p h l", h=4)),
                 [pTr], [pTr, res("wvT")])
            for h in range(4):
                for c in range(2):
                    tr(pT[:, (h * 2 + c) * 128:(h * 2 + c + 1) * 128], wqb_v[:, c, h, 0:128], [res("wqb_bf")], [pTr])
            S.op(dve, lambda: nc.vector.tensor_copy(
                out=wqnT[:], in_=pT[:, 0:1024].rearrange("p (h c q) -> p h c q", h=4, c=2)),
                [pTr], [pTr, res("wqnT")])
            for c in range(2):
                for h in range(4):
                    mm(banks[c][:, h * 128:(h + 1) * 128], wqnT[:, h, c, :], wkT[:, h, :], True, True,
                       [res("wqnT"), res("wkT")], [bres[c]])
                S.op(dve, lambda c=c: nc.vector.tensor_copy(
                    out=wq_eff[:, c], in_=banks[c][:, 0:512].rearrange("p (h l) -> p h l", h=4)),
                    [], [bres[c], res("wq_eff")])

            wof = sb2("wof", [128, 2, 1024], F32)
            wob = sb2("wob", [128, 2, 1024], BF16)
            woe = sb2("woe", [128, 8, 1024], BF16)
            for c in range(8):
                b = c % 2
                S.dma(dstream("w%d" % b), wof[:, b, :], w_out[c * 128:(c + 1) * 128, :], writes=[res("wof%d" % b)])
                if c < 4:
                    S.op(dve, lambda c=c, b=b: nc.vector.tensor_scalar(
                        out=woe[:, c, :], in0=wof[:, b, :], scalar1=gS[:, 0:1], scalar2=None, op0=ALU.mult),
                        [res("wof%d" % b), res("gS")], [res("woe")])
                else:
                    h = c - 4
                    S.op(pool, lambda b=b: nc.gpsimd.tensor_copy(out=wob[:, b, :], in_=wof[:, b, :]),
                         [res("wof%d" % b)], [res("wob%d" % b)])
                    for half in range(2):
                        mm(banks[half][:, 0:512], wvT[:, h, :], wob[:, b, half * 512:(half + 1) * 512], True, True,
                           [res("wvT"), res("wob%d" % b)], [bres[half]])
                        S.op(act, lambda c=c, half=half: nc.scalar.copy(
                            out=woe[:, c, half * 512:(half + 1) * 512], in_=banks[half][:, 0:512]),
                            [], [bres[half], res("woe")])
            for half in range(2):
                S.dma(dstream("ws%d" % half), wout_s[half], woe[:, :, half * 512:(half + 1) * 512],
                      reads=[res("woe")], writes=[scr_res["wout"]])

            NB = 5
            fst = sb2("fst", [128, NB, 8, 2, 128], F32)
            fdn = sb2("fdn", [128, NB, 1024], F32)
            fbf = sb2("fbf", [128, NB, 3072], BF16)
            w_up_v = w_up.rearrange("(k p) (u j c) -> p k u j c", p=128, u=2, c=128)
            def ffn_load(j):
                b = j % NB
                for u in range(2):
                    S.dma(dstream("f%d" % b), fst[:, b, :, u, :], w_up_v[:, :, u, j, :],
                          writes=[res("fst%d" % b)])
                S.dma(dstream("fd%d" % b), fdn[:, b, :], w_down[j * 128:(j + 1) * 128, :], writes=[res("fdn%d" % b)])
            for j in range(NB - 2):
                ffn_load(j)
            for j in range(NCH):
                b = j % NB
                if j + NB - 2 < NCH:
                    ffn_load(j + NB - 2)
                S.op(dve, lambda b=b: nc.vector.tensor_tensor(
                    out=fbf[:, b, 0:2048].rearrange("p (k x) -> p k x", k=8),
                    in0=fst[:, b].rearrange("p k u c -> p k (u c)"),
                    in1=gF[:, :].unsqueeze(2).to_broadcast([128, 8, 256]), op=ALU.mult),
                    [res("fst%d" % b), res("gF")], [res("fbfu%d" % b)])
                S.op(act, lambda b=b: nc.scalar.copy(out=fbf[:, b, 2048:3072], in_=fdn[:, b, :]),
                     [res("fdn%d" % b)], [res("fbfd%d" % b)])
                S.dma(dstream("fs%d" % b), ffn_s[j], fbf[:, b, :], reads=[res("fbfu%d" % b), res("fbfd%d" % b)],
                      writes=[scr_res["ffn"]])
            S.barrier()

        KT = sb("KT", [128, 4, NKT * 128], BF16)
        VX = sb("VX", [128, NKT, 4, 130], BF16)
        CT = sb("CT", [128, NKT * 128], BF16)
        RT = sb("RT", [128, NKT * 128], BF16)
        CX = sb("CX", [128, NKT, 130], BF16)
        ring = sb("ring", [128, RING_SLOTS, RING_W], BF16)
        ringD = sb("ringD", [128, RINGD_SLOTS, 1024], BF16)
        xg = sb("xg", [128, 2, 2, D], F32)
        xn = sb("xn", [128, D], BF16)
        xnT = sb("xnT", [128, 8, 256], BF16)
        mixT = xnT
        h2T = xnT
        qf = sb("qf", [128, 1, 512], F32)
        ra = sb("ra", [128, 1, 512], F32)
        rb = sb("rb", [128, 1, 512], F32)
        qrb = sb("qrb", [128, 2, 512], BF16)
        kout = sb("kout", [128, 1, 512], F32)
        vout = sb("vout", [128, 1, 512], F32)
        mf = qf
        cko = sb("cko", [128, 1, 128], F32)
        kro = sb("kro", [128, 2, 64], F32)
        krd = sb("krd", [128, 2, 128], BF16)
        cqn = sb("cqn", [128, 2, 256], BF16)
        cqT = sb("cqT", [128, 2, 256], BF16)
        QT = sb("QT", [128, 4, 256], BF16)
        QpT = sb("QpT", [128, 4, 256], BF16)
        QRT = sb("QRT", [128, 2, 256], BF16)
        PT = sb("PT", [128, 2, 2, 2, 256], BF16)
        st_small = sb("st_small", [128, 64], F32)
        t0 = sb("t0", [128, 2, 128], F32)
        od = sb("od", [128, 2, 128], F32)
        gbuf = sb("gbuf", [128, 2, 258], F32)
        cbuf = sb("cbuf", [128, 2, 256], F32)
        actT = sb("actT", [128, 3, 256], BF16)
        gstate = sb("gstate", [128, NCH, 2], F32)
        ub = sb("ub", [128, 2, 256], F32)

        xn2 = sb("xn2", [128, D], BF16)
        h2T = sb("h2T_", [128, 8, 256], BF16)
        osb = sb("osb", [128, 2, 2, 129], F32)
        onall = sb("onall", [128, 2, 2, 2, 128], BF16)

        S.op(pool, lambda: nc.gpsimd.memset(VX[:, :, :, 128:130], 1.0), [], [res("VXones")])
        S.op(pool, lambda: nc.gpsimd.memset(CX[:, :, 128:130], 1.0), [], [res("CXones")])
        mhalf = sb("mhalf", [128, 1], F32)
        S.op(pool, lambda: nc.gpsimd.memset(mhalf[:], -0.5), [], [res("mhalf")])
        S.barrier()

        statcache = {}
        rstdcache = {}

        def statc(name, width=1):
            if name not in statcache:
                statcache[name] = stat(name, width)
            return statcache[name]

        def rstd_cached(nm, ssq_ap, ssq_res, n_feat, n):
            if nm not in rstdcache:
                rstdcache[nm] = (statc(nm + "_ms"), statc(nm + "_rs"))
            (ms, msr), (rs, rsr) = rstdcache[nm]
            S.op(dve, lambda: nc.vector.tensor_scalar(out=ms[0:n], in0=ssq_ap[0:n], scalar1=1.0 / n_feat,
                                                      scalar2=EPS, op0=ALU.mult, op1=ALU.add), [ssq_res], [msr])
            S.op(pool, lambda: nc.gpsimd.tensor_tensor(out=rs[0:n], in0=ms[0:n], in1=mhalf[0:n], op=ALU.pow),
                 [msr, res("mhalf")], [rsr])
            return rs, rsr

        real_S = S

        class DrySched:
            dry = True

            def __init__(self):
                class _E:
                    def __init__(self, n):
                        self.name = n
                self.pe, self.act, self.dve, self.pool = _E("pe"), _E("act"), _E("dve"), _E("pool")

            def op(self, *a, **k):
                return None

            def dma(self, *a, **k):
                return None

            def barrier(self):
                return None

            def dma_stream(self, name):
                return None

        class Group:
            pass

        groups = []
        for sq_i in range(n_prompt):
            ng = seq // 256
            for g in range(ng):
                G = Group()
                G.x_ap, G.tsz, G.tpg, G.g, G.ng, G.npast, G.tab0 = xp[sq_i], 128, 2, g, ng, 0, 0
                G.outs = {"y": yp[sq_i], "dk": dkp[sq_i], "dv": dvp[sq_i], "ckv": ckvp[sq_i], "kr": krp[sq_i],
                          "conv": convp[sq_i]}
                G.conv_init = None
                G.sample = False
                groups.append(G)
        if sample:
            G = Group()
            G.x_ap, G.tsz, G.tpg, G.g, G.ng, G.npast, G.tab0 = xs, dec, 1, 0, 1, past // 128, NT
            G.outs = {"y": ys, "dk": dks, "dv": dvs, "ckv": ckvs, "kr": krs, "conv": convs}
            G.conv_init = sconv
            G.sample = True
            groups.append(G)
        for f_, G in enumerate(groups):
            G.f = f_
            G.slot = f_ % 2
            G.nq = G.tsz * G.tpg

        def emit_main(plan):
            nonlocal S, pe, act, dve, pool
            dry = plan is None
            if dry:
                S = DrySched()
            else:
                S = real_S
            pe, act, dve, pool = S.pe, S.act, S.dve, S.pool
            RINGS = {"M": (ring, RING_SLOTS), "D": (ringD, RINGD_SLOTS)}
            ring_log = {"M": [], "D": []}
            ring_res = {k: [Res("ring%s%d" % (k, i)) for i in range(v[1])] for k, v in RINGS.items()}
            rstate = {k: {"loaded": 0, "next": 0, "released": set()} for k in RINGS}
            kvres = {}

            def kvr(buf, kt):
                key = (buf, kt)
                if key not in kvres:
                    kvres[key] = Res("%s%d" % (buf, kt))
                return kvres[key]

            def ring_src(spec):
                kind, a, b = spec
                if kind == "win":
                    return win_s[a][:, b * 4:(b + 1) * 4, :], 2048, scr_res["win"]
                if kind == "wout":
                    return wout_s[a][:, b * 4:(b + 1) * 4, :], 2048, scr_res["wout"]
                if kind == "up":
                    return ffn_s[a][:, 0:2048], 2048, scr_res["ffn"]
                return ffn_s[a][:, 2048:3072], 1024, scr_res["ffn"]

            def ring_prefetch():
                if dry:
                    return
                progress = True
                while progress:
                    progress = False
                    for k in ("M", "D"):
                        rs_, (rt, nsl), pl = rstate[k], RINGS[k], plan[k]
                        u = rs_["loaded"]
                        if u >= len(pl):
                            continue
                        if u >= nsl and (u - nsl) not in rs_["released"]:
                            continue
                        if u > rs_["next"] + nsl - 1:
                            continue
                        ap, wd, sres = ring_src(pl[u])
                        slot = u % nsl
                        if pl[u][0] in ("win", "wout"):
                            dst = rt[:, slot, 0:2048].rearrange("p (k n) -> p k n", k=4)
                        else:
                            dst = rt[:, slot, 0:wd]
                        S.dma(dstream("ring%s%d" % (k, slot)), dst, ap, reads=[sres], writes=[ring_res[k][slot]])
                        rs_["loaded"] += 1
                        progress = True

            def ring_next(spec):
                k = "D" if spec[0] == "down" else "M"
                rs_, (rt, nsl) = rstate[k], RINGS[k]
                u = rs_["next"]
                ring_log[k].append(spec)
                if not dry:
                    assert plan[k][u] == spec, (k, u, plan[k][u], spec)
                rs_["next"] += 1
                ring_prefetch()
                if not dry:
                    assert rs_["loaded"] > u, "ring %s deadlock: unit %d not loadable" % (k, u)
                slot = u % nsl
                return (k, u), rt[:, slot, :], ring_res[k][slot]

            def ring_release(h):
                rstate[h[0]]["released"].add(h[1])
                ring_prefetch()

            def rope(src, n, nh, tab_i, out_ap, reads, out_res_list, e1, e2):
                xv = src.rearrange("p (h t j) -> p h t j", h=nh, t=2)
                ov = out_ap.rearrange("p (h t j) -> p h t j", h=nh, t=2)
                av = ra[0:n, 0, 0:nh * 64].rearrange("p (h t j) -> p h t j", h=nh, t=2)
                bv = rb[0:n, 0, 0:nh * 64].rearrange("p (h t j) -> p h t j", h=nh, t=2)
                cosb = cos_t[0:n, tab_i, :].unsqueeze(1).unsqueeze(1).to_broadcast([n, nh, 2, 32])
                sinb = sin_t[0:n, tab_i, :].unsqueeze(1).to_broadcast([n, nh, 32])
                rar, rbr = res("ra0"), res("rb0")
                S.op(e2, lambda: eng_of(e2).tensor_tensor(out=av, in0=xv, in1=cosb, op=ALU.mult),
                     list(reads) + [res("tab")], [rar])
                S.op(e1, lambda: eng_of(e1).tensor_tensor(out=bv[:, :, 0, :], in0=xv[:, :, 1, :], in1=sinb,
                                                          op=ALU.mult), list(reads) + [res("tab")], [rbr])
                S.op(e1, lambda: eng_of(e1).tensor_tensor(out=bv[:, :, 1, :], in0=xv[:, :, 0, :], in1=sinb,
                                                          op=ALU.mult), list(reads) + [res("tab")], [rbr])
                S.op(e1, lambda: eng_of(e1).tensor_tensor(out=ov[:, :, 0, :], in0=av[:, :, 0, :], in1=bv[:, :, 0, :],
                                                          op=ALU.subtract), [rar, rbr], out_res_list)
                S.op(e1, lambda: eng_of(e1).tensor_tensor(out=ov[:, :, 1, :], in0=av[:, :, 1, :], in1=bv[:, :, 1, :],
                                                          op=ALU.add), [rar, rbr], out_res_list)

            def load_x(G):
                for tl in range(G.tpg):
                    ti = G.g * G.tpg + tl
                    S.dma(dstream("x%d%d" % (G.slot, tl)), xg[0:G.tsz, G.slot, tl, :],
                          G.x_ap[ti * G.tsz:(ti + 1) * G.tsz, :], writes=[res("xg%d%d" % (G.slot, tl))])

            PA = 7

            def phaseA_stages(G):
                n, tpg, nq = G.tsz, G.tpg, G.nq
                stages = []
                xnb = [xn, xn2]

                if G.sample:
                    for kt in range(past // 128):
                        def st_past(kt=kt):
                            r0 = kt * 128
                            S.dma(dstream("c0"), qf[:, 0, :], cdk[r0:r0 + 128, :], writes=[res("qf0")])
                            S.dma(dstream("c1"), ra[:, 0, :], cdv[r0:r0 + 128, :], writes=[res("ra0")])
                            S.dma(dstream("c2"), rb[:, 0, 0:128], cckv[r0:r0 + 128, :], writes=[res("rb0")])
                            S.dma(dstream("c2"), rb[:, 0, 128:192], ckr[r0:r0 + 128, :], writes=[res("rb0")])
                            S.op(pool, lambda: nc.gpsimd.tensor_copy(out=qrb[:, 0, :], in_=qf[:, 0, :]),
                                 [res("qf0")], [res("qrb0")])
                            for h in range(4):
                                tr(pT[:, h * 128:(h + 1) * 128], qrb[:, 0, h * 128:(h + 1) * 128], [res("qrb0")], [pTr])
                            S.op(dve, lambda: nc.vector.tensor_copy(
                                out=KT[:, :, r0:r0 + 128], in_=pT[:, 0:512].rearrange("p (h t) -> p h t", h=4)),
                                [], [pTr, kvr("KT", kt)])
                            S.op(pool, lambda: nc.gpsimd.tensor_copy(
                                out=VX[:, kt, :, 0:128], in_=ra[:, 0, :].rearrange("p (h e) -> p h e", h=4)),
                                [res("ra0")], [kvr("VX", kt)])
                            S.op(pool, lambda: nc.gpsimd.tensor_copy(out=CX[:, kt, 0:128], in_=rb[:, 0, 0:128]),
                                 [res("rb0")], [kvr("CX", kt)])
                            for dd in range(2):
                                S.op(pool, lambda dd=dd: nc.gpsimd.tensor_copy(
                                    out=krd[:, 0, dd * 64:(dd + 1) * 64], in_=rb[:, 0, 128:192]),
                                    [res("rb0")], [res("krd0")])
                            tr(pT[:, 512:640], CX[:, kt, 0:128], [kvr("CX", kt)], [pTr])
                            tr(pT[:, 640:768], krd[:, 0, :], [res("krd0")], [pTr])
                            S.op(dve, lambda: nc.vector.tensor_copy(out=CT[:, r0:r0 + 128], in_=pT[:, 512:640]),
                                 [], [pTr, kvr("CT", kt)])
                            S.op(dve, lambda: nc.vector.tensor_copy(out=RT[:, r0:r0 + 128], in_=pT[:, 640:768]),
                                 [], [pTr, kvr("RT", kt)])
                        stages.append(st_past)

                def st_norm():
                    for tl in range(tpg):
                        xr = res("xg%d%d" % (G.slot, tl))
                        xt = xg[0:n, G.slot, tl, :]
                        xb = xnb[tl]
                        ssq, ssqr = statc("a_ssq%d" % tl)
                        S.op(act, lambda: nc.scalar.activation(out=xb[0:n, :], in_=xt, func=AF.Square,
                                                               accum_out=ssq[0:n]), [xr], [res("xn%d" % tl), ssqr])
                        rs, rsr = rstd_cached("a%d" % tl, ssq, ssqr, D, n)
                        S.op(dve, lambda: nc.vector.tensor_scalar(out=xb[0:n, :], in0=xt, scalar1=rs[0:n],
                                                                  scalar2=None, op0=ALU.mult),
                             [xr, rsr], [res("xn%d" % tl)])
                stages.append(st_norm)
                stages.extend([None, None])

                for tl in range(tpg):
                    def st_xT(tl=tl):
                        xb = xnb[tl]
                        for k in range(8):
                            tr(pT[:, k * 128:k * 128 + n], xb[0:n, k * 128:(k + 1) * 128], [res("xn%d" % tl)], [pTr])
                        S.op(dve, lambda: nc.vector.tensor_copy(
                            out=xnT[:, :, tl * 128:tl * 128 + n],
                            in_=pT[:, :].rearrange("p (k t) -> p k t", k=8)[:, :, 0:n]), [],
                            [pTr, res("xnT%d" % tl)] + [res("mixT%d" % c) for c in range(8)])
                    stages.append(st_xT)

                pending_T = []
                stage_no = [0]

                def flush_T(all_=False):
                    stage_no[0] += 1
                    while pending_T and (all_ or pending_T[0][0] <= stage_no[0] - tpg):
                        pending_T.pop(0)[1]()

                def defer_T(fn):
                    pending_T.append((stage_no[0], fn))

                for s_ in (3, 0, 1, 2):
                    wd = 512 if s_ < 3 else 448
                    units = {}
                    for tl in range(tpg):
                        def st_seg(s_=s_, tl=tl, wd=wd, units=units):
                            flush_T()
                            ti = G.g * tpg + tl
                            kt = G.npast + ti
                            tok0 = kt * 128
                            tab_i = G.tab0 + ti
                            if tl == 0:
                                for hk in range(2):
                                    units[hk] = ring_next(("win", s_, hk))
                            for k in range(8):
                                u_, unit, ur = units[k // 4]
                                uv = unit[:, 0:2048].rearrange("p (k n) -> p k n", k=4)
                                mm(banks[PA][0:n, 0:wd], xnT[:, k, tl * 128:tl * 128 + n], uv[:, k % 4, 0:wd],
                                   k == 0, k == 7, [res("xnT%d" % tl), ur], [bres[PA]])
                            if tl == tpg - 1:
                                for hk in range(2):
                                    ring_release(units[hk][0])
                            bk = PA
                            if s_ == 0:
                                copy(act, qf[0:n, 0, :], banks[bk][0:n, 0:512], [], [bres[bk], res("qf0")])
                                rope(qf[0:n, 0, :], n, 8, tab_i, qrb[0:n, tl, :], [res("qf0")],
                                     [res("qrb%d" % tl)], dve, pool)

                                def tq(tl=tl):
                                    for h in range(4):
                                        tr(pT[:, h * 128:h * 128 + n], qrb[0:n, tl, h * 128:(h + 1) * 128],
                                           [res("qrb%d" % tl)], [pTr])
                                    S.op(dve, lambda: nc.vector.tensor_copy(
                                        out=QT[:, :, tl * 128:tl * 128 + n],
                                        in_=pT[:, 0:512].rearrange("p (h t) -> p h t", h=4)[:, :, 0:n]),
                                        [], [pTr, res("QT")])
                                defer_T(tq)
                            elif s_ == 1:
                                copy(act, qf[0:n, 0, :], banks[bk][0:n, 0:512], [], [bres[bk], res("qf0")])
                                rope(qf[0:n, 0, :], n, 8, tab_i, kout[0:n, 0, :], [res("qf0")],
                                     [res("kout0")], dve, pool)
                                S.dma(dstream("ko0"), G.outs["dk"][ti * n:(ti + 1) * n, :], kout[0:n, 0, :],
                                      reads=[res("kout0")], writes=[out_res])
                                S.op(pool, lambda: nc.gpsimd.tensor_copy(out=qrb[0:n, tl, :], in_=kout[0:n, 0, :]),
                                     [res("kout0")], [res("qrb%d" % tl)])

                                def tk(tl=tl, kt=kt, tok0=tok0):
                                    for h in range(4):
                                        tr(pT[:, h * 128:h * 128 + n], qrb[0:n, tl, h * 128:(h + 1) * 128],
                                           [res("qrb%d" % tl)], [pTr])
                                    S.op(dve, lambda: nc.vector.tensor_copy(
                                        out=KT[:, :, tok0:tok0 + n],
                                        in_=pT[:, 0:512].rearrange("p (h t) -> p h t", h=4)[:, :, 0:n]),
                                        [], [pTr, kvr("KT", kt)])
                                defer_T(tk)
                            elif s_ == 2:
                                copy(act, vout[0:n, 0, :], banks[bk][0:n, 0:512], [], [bres[bk], res("vout0")])
                                S.dma(dstream("vo0"), G.outs["dv"][ti * n:(ti + 1) * n, :], vout[0:n, 0, :],
                                      reads=[res("vout0")], writes=[out_res])
                                S.op(pool, lambda: nc.gpsimd.tensor_copy(
                                    out=VX[0:n, kt, :, 0:128],
                                    in_=vout[0:n, 0, :].rearrange("p (h e) -> p h e", h=4)),
                                    [res("vout0")], [kvr("VX", kt)])
                            else:
                                copy(act, mf[0:n, 0, 0:448], banks[bk][0:n, 0:448], [], [bres[bk], res("qf0")])
                                mfr = res("qf0")
                                sq, sqr = statc("cq_ssq")
                                S.op(dve, lambda: nc.vector.scalar_tensor_tensor(
                                    out=ra[0:n, 0, 0:256], in0=mf[0:n, 0, 0:256], scalar=1.0, in1=mf[0:n, 0, 0:256],
                                    op0=ALU.mult, op1=ALU.mult, accum_out=sq[0:n]), [mfr], [res("ra0"), sqr])
                                rs, rsr = rstd_cached("cq", sq, sqr, 256, n)
                                S.op(dve, lambda: nc.vector.tensor_scalar(
                                    out=cqn[0:n, tl, :], in0=mf[0:n, 0, 0:256], scalar1=rs[0:n], scalar2=None,
                                    op0=ALU.mult), [mfr, rsr], [res("cqn%d" % tl)])
                                sk, skr = statc("ckv_ssq")
                                S.op(dve, lambda: nc.vector.scalar_tensor_tensor(
                                    out=rb[0:n, 0, 0:128], in0=mf[0:n, 0, 256:384], scalar=1.0,
                                    in1=mf[0:n, 0, 256:384], op0=ALU.mult, op1=ALU.mult, accum_out=sk[0:n]),
                                    [mfr], [res("rb0"), skr])
                                rs2, rsr2 = rstd_cached("ckv", sk, skr, 128, n)
                                S.op(dve, lambda: nc.vector.scalar_tensor_tensor(
                                    out=cko[0:n, 0, :], in0=mf[0:n, 0, 256:384], scalar=rs2[0:n], in1=gkv_b[0:n, :],
                                    op0=ALU.mult, op1=ALU.mult), [mfr, rsr2, res("gkv")], [res("cko0")])
                                S.dma(dstream("co0"), G.outs["ckv"][ti * n:(ti + 1) * n, :], cko[0:n, 0, :],
                                      reads=[res("cko0")], writes=[out_res])
                                S.op(pool, lambda: nc.gpsimd.tensor_copy(out=CX[0:n, kt, 0:128], in_=cko[0:n, 0, :]),
                                     [res("cko0")], [kvr("CX", kt)])
                                rope(mf[0:n, 0, 384:448], n, 1, tab_i, kro[0:n, tl, :], [mfr],
                                     [res("kro%d" % tl)], dve, pool)
                                S.dma(dstream("ro%d" % tl), G.outs["kr"][ti * n:(ti + 1) * n, :], kro[0:n, tl, :],
                                      reads=[res("kro%d" % tl)], writes=[out_res])
                                for dd in range(2):
                                    S.op(pool, lambda dd=dd: nc.gpsimd.tensor_copy(
                                        out=krd[0:n, tl, dd * 64:(dd + 1) * 64], in_=kro[0:n, tl, :]),
                                        [res("kro%d" % tl)], [res("krd%d" % tl)])

                                def tm(tl=tl, kt=kt, tok0=tok0):
                                    for c in range(2):
                                        tr(pT[:, c * 128:c * 128 + n], cqn[0:n, tl, c * 128:(c + 1) * 128],
                                           [res("cqn%d" % tl)], [pTr])
                                    tr(pT[:, 256:256 + n], CX[0:n, kt, 0:128], [kvr("CX", kt)], [pTr])
                                    tr(pT[:, 384:384 + n], krd[0:n, tl, :], [res("krd%d" % tl)], [pTr])
                                    S.op(dve, lambda: nc.vector.tensor_copy(
                                        out=cqT[:, :, tl * 128:tl * 128 + n],
                                        in_=pT[:, 0:256].rearrange("p (c t) -> p c t", c=2)[:, :, 0:n]),
                                        [], [pTr, res("cqT")])
                                    S.op(dve, lambda: nc.vector.tensor_copy(out=CT[:, tok0:tok0 + n],
                                                                            in_=pT[:, 256:256 + n]),
                                         [], [pTr, kvr("CT", kt)])
                                    S.op(dve, lambda: nc.vector.tensor_copy(out=RT[:, tok0:tok0 + n],
                                                                            in_=pT[:, 384:384 + n]),
                                         [], [pTr, kvr("RT", kt)])
                                defer_T(tm)
                        stages.append(st_seg)

                for hp in range(2):
                    def st_qp(hp=hp):
                        flush_T(hp == 0)
                        for hh in range(2):
                            h = hp * 2 + hh
                            for c in range(2):
                                mm(banks[PA][:, hh * 256:hh * 256 + nq], wq_eff[:, c, h, :], cqT[:, c, 0:nq],
                                   c == 0, c == 1, [res("cqT"), res("wq_eff")], [bres[PA]])
                        S.op(dve, lambda hp=hp: nc.vector.tensor_copy(
                            out=QpT[:, hp * 2:hp * 2 + 2, 0:nq],
                            in_=banks[PA][:, :].rearrange("p (h t) -> p h t", h=2)[:, :, 0:nq]),
                            [], [bres[PA], res("QpT")])
                    stages.append(st_qp)

                for tl in range(tpg):
                    def st_qr(tl=tl):
                        flush_T()
                        ti = G.g * tpg + tl
                        tab_i = G.tab0 + ti
                        for c in range(2):
                            mm(banks[PA][0:n, 0:256], cqT[:, c, tl * 128:tl * 128 + n],
                               wqr[:, c].rearrange("p h r -> p (h r)"), c == 0, c == 1, [res("cqT"), res("wqr")],
                               [bres[PA]])
                        copy(act, qf[0:n, 0, 0:256], banks[PA][0:n, 0:256], [], [bres[PA], res("qf0")])
                        rope(qf[0:n, 0, 0:256], n, 4, tab_i, qrb[0:n, tl, 0:256], [res("qf0")],
                             [res("qrb%d" % tl)], dve, pool)

                        def tqr(tl=tl):
                            for u in range(2):
                                tr(pT[:, u * 128:u * 128 + n], qrb[0:n, tl, u * 128:(u + 1) * 128],
                                   [res("qrb%d" % tl)], [pTr])
                            S.op(dve, lambda: nc.vector.tensor_copy(
                                out=QRT[:, :, tl * 128:tl * 128 + n],
                                in_=pT[:, 0:256].rearrange("p (u t) -> p u t", u=2)[:, :, 0:n]),
                                [], [pTr, res("QRT")])
                        defer_T(tqr)
                    stages.append(st_qr)
                stages.append(flush_T)
                stages.append(lambda: flush_T(True))
                return stages

            def attention(G):
                n, tpg, nq, g, npast = G.tsz, G.tpg, G.nq, G.g, G.npast
                pairs = []
                if not G.sample:
                    for j in range(g):
                        pairs.append([(2 * j, 128), (2 * j + 1, 128)])
                    pairs.append("diag")
                else:
                    for j in range(npast // 2):
                        pairs.append([(2 * j, 128), (2 * j + 1, 128)])
                    pairs.append([(npast, n)])
                qts = [(q0, min(128, nq - q0)) for q0 in range(0, nq, 128)]
                pS = [[3, 5], [4, 6]]
                pO = [7, 0]
                deferred = []

                def make_unit(unit_i):
                    is_diff = unit_i < 4
                    scale = DIFF_SCALE if is_diff else MLA_SCALE
                    first_av = [True, True]

                    def emit_S(pi, pr):
                        buf = pi % 2
                        tl_list = pr if pr != "diag" else [(2 * g, 128), (2 * g + 1, 128)]
                        for i, (kt, nk) in enumerate(tl_list):
                            c0 = kt * 128
                            for m in range(2):
                                bk = pS[m][buf]
                                outp = banks[bk][0:nk, i * 256:i * 256 + nq]
                                if is_diff:
                                    h = unit_i
                                    mm(outp, KT[m * 64:(m + 1) * 64, h, c0:c0 + nk], QT[m * 64:(m + 1) * 64, h, 0:nq],
                                       True, True, [kvr("KT", kt), res("QT")], [bres[bk]])
                                else:
                                    h = (unit_i - 4) * 2 + m
                                    mm(outp, CT[:, c0:c0 + nk], QpT[:, h, 0:nq], True, False,
                                       [kvr("CT", kt), res("QpT")], [bres[bk]])
                                    mm(outp, RT[m * 64:(m + 1) * 64, c0:c0 + nk],
                                       QRT[m * 64:(m + 1) * 64, unit_i - 4, 0:nq], False, True,
                                       [kvr("RT", kt), res("QRT")], [bres[bk]])
                        bks = [bres[pS[0][buf]], bres[pS[1][buf]]]
                        ptrs = [res("PT0%d" % buf), res("PT1%d" % buf)]
                        b0 = (pS[0][buf] - 3) * 512
                        bv = pS_all[:, b0:b0 + 1024].rearrange("p (m i q) -> p m i q", m=2, i=2)
                        if pr == "diag":
                            regs = [(0, 64, 0, 0, 256), (64, 128, 0, 64, 256), (0, 64, 1, 128, 256),
                                    (64, 128, 1, 192, 256)]
                            for (p0, p1, i, q0, q1) in [(64, 128, 0, 0, 64), (64, 128, 1, 128, 192)]:
                                S.op(pool, lambda: nc.gpsimd.memset(PT[p0:p1, :, buf, i, q0:q1], 0.0), [], ptrs)
                            for (p0, p1, i, q0, q1) in regs:
                                S.op(act, lambda: nc.scalar.activation(
                                    out=PT[p0:p1, :, buf, i, q0:q1], in_=bv[p0:p1, :, i, q0:q1], func=AF.Exp,
                                    scale=scale), [], bks + ptrs)
                        elif len(pr) == 2:
                            S.op(act, lambda: nc.scalar.activation(
                                out=PT[:, :, buf, :, 0:nq], in_=bv[:, :, :, 0:nq], func=AF.Exp, scale=scale),
                                [], bks + ptrs)
                        else:
                            nk = pr[0][1]
                            S.op(act, lambda: nc.scalar.activation(
                                out=PT[0:nk, :, buf, 0, 0:nq], in_=bv[0:nk, :, 0, 0:nq], func=AF.Exp,
                                scale=scale), [], bks + ptrs)

                    def emit_AV(pi, pr, last):
                        buf = pi % 2
                        if pr == "diag":
                            items = [(2 * g, 128, 0, (0, 1)), (2 * g + 1, 128, 1, (1,))]
                        else:
                            items = [(kt, nk, i, tuple(range(len(qts)))) for i, (kt, nk) in enumerate(pr)]
                        for qi, (q0, nqt) in enumerate(qts):
                            bk = pO[qi]
                            ov = banks[bk][:, :].rearrange("p (m x) -> p m x", m=2)
                            for m in range(2):
                                its = [it for it in items if qi in it[3]]
                                for ii, (kt, nk, i, _q) in enumerate(its):
                                    lhsT = PT[0:nk, m, buf, i, q0:q0 + nqt]
                                    ptr_ = res("PT%d%d" % (m, buf))
                                    if is_diff:
                                        rhs = VX[0:nk, kt, unit_i, 0:129]
                                        rr = [kvr("VX", kt), res("VXones")]
                                    else:
                                        rhs = CX[0:nk, kt, 0:129]
                                        rr = [kvr("CX", kt), res("CXones")]
                                    start = first_av[qi] and m == 0
                                    if start:
                                        first_av[qi] = False
                                    mm(ov[0:nqt, m, 0:129], lhsT, rhs, start, last and ii == len(its) - 1,
                                       [ptr_] + rr, [bres[bk]])

                    def finish_unit():
                        def each_q(fn):
                            for qi, (q0, nqt) in enumerate(qts):
                                fn(qi, nqt)

                        def e_copy(qi, nqt):
                            bk = pO[qi]
                            ov = banks[bk][:, :].rearrange("p (m x) -> p m x", m=2)
                            S.op(dve, lambda: nc.vector.tensor_copy(out=osb[0:nqt, qi, :, :], in_=ov[0:nqt, :, 0:129]),
                                 [], [bres[bk], res("osb%d" % qi)])
                        each_q(e_copy)

                        def e_recip(qi, nqt):
                            rsum, rsumr = statc("rsum%d" % qi, 2)
                            S.op(dve, lambda: nc.vector.reciprocal(out=rsum[0:nqt, 0:2], in_=osb[0:nqt, qi, :, 128]),
                                 [res("osb%d" % qi)], [rsumr])
                        each_q(e_recip)
                        if is_diff:
                            def e_r1(qi, nqt):
                                rsum, rsumr = statc("rsum%d" % qi, 2)
                                r1, r1r = statc("r1_%d" % qi)
                                S.op(dve, lambda: nc.vector.tensor_tensor(out=r1[0:nqt], in0=rsum[0:nqt, 1:2],
                                                                          in1=neg_lam[0:nqt], op=ALU.mult),
                                     [rsumr, res("neg_lam")], [r1r])
                            each_q(e_r1)

                            def e_t0(qi, nqt):
                                rsum, rsumr = statc("rsum%d" % qi, 2)
                                S.op(dve, lambda: nc.vector.tensor_scalar(
                                    out=t0[0:nqt, qi, :], in0=osb[0:nqt, qi, 0, 0:128], scalar1=rsum[0:nqt, 0:1],
                                    scalar2=None, op0=ALU.mult), [rsumr, res("osb%d" % qi)], [res("t0%d" % qi)])
                            each_q(e_t0)

                            def e_od(qi, nqt):
                                r1, r1r = statc("r1_%d" % qi)
                                S.op(dve, lambda: nc.vector.scalar_tensor_tensor(
                                    out=od[0:nqt, qi, :], in0=osb[0:nqt, qi, 1, 0:128], scalar=r1[0:nqt],
                                    in1=t0[0:nqt, qi, :], op0=ALU.mult, op1=ALU.add),
                                    [r1r, res("osb%d" % qi), res("t0%d" % qi)], [res("od%d" % qi)])
                            each_q(e_od)

                            def e_ssq(qi, nqt):
                                sq, sqr = statc("od_ssq%d" % qi)
                                S.op(dve, lambda: nc.vector.scalar_tensor_tensor(
                                    out=t0[0:nqt, qi, :], in0=od[0:nqt, qi, :], scalar=1.0, in1=od[0:nqt, qi, :],
                                    op0=ALU.mult, op1=ALU.mult, accum_out=sq[0:nqt]),
                                    [res("od%d" % qi)], [res("t0%d" % qi), sqr])
                            each_q(e_ssq)
                            rss = {}

                            def e_rstd(qi, nqt):
                                sq, sqr = statc("od_ssq%d" % qi)
                                rss[qi] = rstd_cached("od%d" % qi, sq, sqr, 128, nqt)
                            each_q(e_rstd)

                            def e_on(qi, nqt):
                                rs, rsr = rss[qi]
                                S.op(dve, lambda: nc.vector.tensor_scalar(
                                    out=onall[0:nqt, qi, unit_i % 2, 0, :], in0=od[0:nqt, qi, :], scalar1=rs[0:nqt],
                                    scalar2=None, op0=ALU.mult), [res("od%d" % qi), rsr],
                                    [res("onall%d_0" % (unit_i % 2))])
                            each_q(e_on)
                        else:
                            for m in range(2):
                                def e_onm(qi, nqt, m=m):
                                    rsum, rsumr = statc("rsum%d" % qi, 2)
                                    S.op(dve, lambda: nc.vector.tensor_scalar(
                                        out=onall[0:nqt, qi, unit_i % 2, m, :], in0=osb[0:nqt, qi, m, 0:128],
                                        scalar1=rsum[0:nqt, m:m + 1], scalar2=None, op0=ALU.mult),
                                        [rsumr, res("osb%d" % qi)], [res("onall%d_%d" % (unit_i % 2, m))])
                                each_q(e_onm)

                        def tail(unit_i=unit_i, is_diff=is_diff):
                            cs = [unit_i] if is_diff else [4 + (unit_i - 4) * 2, 5 + (unit_i - 4) * 2]
                            for ci, c in enumerate(cs):
                                for qi, (q0, nqt) in enumerate(qts):
                                    tr(pT[:, (ci * 2 + qi) * 128:(ci * 2 + qi) * 128 + nqt],
                                       onall[0:nqt, qi, unit_i % 2, ci, :], [res("onall%d_%d" % (unit_i % 2, ci))], [pTr])
                            S.op(dve, lambda: nc.vector.tensor_copy(
                                out=mixT[:, cs[0]:cs[0] + len(cs), 0:nq],
                                in_=pT[:, 0:512].rearrange("p (c t) -> p c t", c=2)[:, 0:len(cs), 0:nq]),
                                [], [pTr, res("xnT0"), res("xnT1")] + [res("mixT%d" % c) for c in cs])
                        deferred.append(tail)
                    return emit_S, emit_AV, finish_unit

                unit_fns = [make_unit(u) for u in range(6)]
                np_ = len(pairs)
                steps = [(u, pr) for u in range(6) for pr in pairs]

                def run_av(pd):
                    pu, psi, ppr, plast = pd
                    unit_fns[pu][1](psi, ppr, plast)
                    if plast:
                        while deferred:
                            deferred.pop(0)()
                        unit_fns[pu][2]()
                pend = None
                for si, (u, pr) in enumerate(steps):
                    unit_fns[u][0](si, pr)
                    if pend is not None:
                        run_av(pend)
                    pend = (u, si, pr, si % np_ == np_ - 1)
                run_av(pend)
                while deferred:
                    deferred.pop(0)()

            def outproj(G):
                n, tpg = G.tsz, G.tpg
                mixr = [res("mixT%d" % c) for c in range(8)] + [res("xnT0"), res("xnT1")]
                us = {}
                for half in range(2):
                    for hk in range(2):
                        us[(half, hk)] = ring_next(("wout", half, hk))
                for tl in range(tpg):
                    xr = res("xg%d%d" % (G.slot, tl))
                    for half in range(2):
                        bk = half
                        for c in range(8):
                            u_, unit, ur = us[(half, c // 4)]
                            uv = unit[:, 0:2048].rearrange("p (k n) -> p k n", k=4)
                            mm(banks[bk][0:n, 0:512], mixT[:, c, tl * 128:tl * 128 + n], uv[:, c % 4, :], c == 0,
                               c == 7, mixr + [ur], [bres[bk]])
                        S.op(dve, lambda: nc.vector.tensor_tensor(
                            out=xg[0:n, G.slot, tl, half * 512:(half + 1) * 512], in0=banks[bk][0:n, 0:512],
                            in1=xg[0:n, G.slot, tl, half * 512:(half + 1) * 512], op=ALU.add), [xr], [bres[bk], xr])
                    xt = xg[0:n, G.slot, tl, :]
                    xb = (xn, xn2)[tl]
                    ssq, ssqr = statc("f_ssq%d" % tl)
                    S.op(act, lambda: nc.scalar.activation(out=xb[0:n, :], in_=xt, func=AF.Square,
                                                           accum_out=ssq[0:n]), [xr], [res("xn%d" % tl), ssqr])
                    rs, rsr = rstd_cached("f%d" % tl, ssq, ssqr, D, n)
                    S.op(dve, lambda: nc.vector.tensor_scalar(out=xb[0:n, :], in0=xt, scalar1=rs[0:n], scalar2=None,
                                                              op0=ALU.mult), [xr, rsr], [res("xn%d" % tl)])
                for h_ in us.values():
                    ring_release(h_[0])

            def ffn(G, stages, depth=2):
                n, tpg, nq = G.tsz, G.tpg, G.nq
                if G.g == 0:
                    if G.conv_init is None:
                        S.op(pool, lambda: nc.gpsimd.memset(gstate[:], 0.0), [], [res("gstate")])
                    else:
                        for r_ in range(2):
                            S.dma(dstream("c0"), gstate[:, :, r_],
                                  G.conv_init[r_].rearrange("(j p) -> p j", p=128),
                                  writes=[res("gstate")], allow_slow_non_contiguous=True)
                for tl in range(tpg):
                    xb = (xn, xn2)[tl]
                    for k in range(8):
                        tr(pT[:, k * 128:k * 128 + n], xb[0:n, k * 128:(k + 1) * 128], [res("xn%d" % tl)], [pTr])
                    S.op(dve, lambda: nc.vector.tensor_copy(
                        out=h2T[:, :, tl * 128:tl * 128 + n],
                        in_=pT[:, :].rearrange("p (k t) -> p k t", k=8)[:, :, 0:n]), [], [pTr, res("h2T%d" % tl)])
                pY = [[3, 4], [5, 6]]
                pend = []
                h2r = [res("h2T%d" % tl) for tl in range(tpg)]

                def emit_down(j, b3, ud):
                    u_, unit, ur = ud
                    for tl in range(tpg):
                        for half in range(2):
                            bk = pY[tl][half]
                            mm(banks[bk][0:n, 0:512], actT[:, b3, tl * 128:tl * 128 + n],
                               unit[:, half * 512:(half + 1) * 512], j == 0, j == NCH - 1,
                               [res("actT%d_%d" % (b3, tl)), ur], [bres[bk]])
                    ring_release(u_)

                stages = list(stages)
                for j in range(NCH):
                    b = j % 2
                    uu = ring_next(("up", j, 0))
                    ud = ring_next(("down", j, 0))
                    unit = uu[1]
                    uvv = unit[:, 0:2048].rearrange("p (k u c) -> p k u c", k=8, u=2)
                    pv = banks[b][:, :].rearrange("p (u t) -> p u t", u=2)
                    for u in range(2):
                        for k in range(8):
                            mm(pv[:, u, 0:nq], uvv[:, k, u, :], h2T[:, k, 0:nq], k == 0, k == 7,
                               h2r + [uu[2]], [bres[b]])
                    ring_release(uu[0])
                    if len(pend) == depth:
                        emit_down(*pend.pop(0))
                    b3 = j % 3
                    gb = gbuf[:, b, :]
                    gbr = res("gbuf%d" % b)
                    ubr = res("ub%d" % b)
                    copy(act, gb[:, 2:2 + nq], pv[:, 1, 0:nq], [], [bres[b], gbr])
                    copy(act, ub[:, b, 0:nq], pv[:, 0, 0:nq], [], [bres[b], ubr])
                    S.op(pool, lambda: nc.gpsimd.tensor_copy(out=gb[:, 0:2], in_=gstate[:, j, :]),
                         [res("gstate")], [gbr])
                    S.op(pool, lambda: nc.gpsimd.tensor_copy(out=gstate[:, j, :], in_=gb[:, nq:nq + 2]),
                         [gbr], [res("gstate")])
                    halves = [(0, nq)] if nq <= 128 else [(0, 128), (128, nq)]
                    hres = [res("cbuf%d_%d" % (b, hi)) for hi in range(len(halves))]
                    for hi, (h0, h1) in enumerate(halves):
                        S.op(dve, lambda: nc.vector.tensor_scalar(
                            out=cbuf[:, b, h0:h1], in0=gb[:, h0:h1], scalar1=wconv[:, j, 0:1],
                            scalar2=bconv[:, j:j + 1], op0=ALU.mult, op1=ALU.add), [gbr, res("wconv")], [hres[hi]])
                    for hi, (h0, h1) in enumerate(halves):
                        S.op(dve, lambda: nc.vector.scalar_tensor_tensor(
                            out=cbuf[:, b, h0:h1], in0=gb[:, 1 + h0:1 + h1], scalar=wconv[:, j, 1:2],
                            in1=cbuf[:, b, h0:h1], op0=ALU.mult, op1=ALU.add), [gbr, res("wconv"), hres[hi]],
                            [hres[hi]])
                    for hi, (h0, h1) in enumerate(halves):
                        S.op(dve, lambda: nc.vector.scalar_tensor_tensor(
                            out=cbuf[:, b, h0:h1], in0=gb[:, 2 + h0:2 + h1], scalar=wconv[:, j, 2:3],
                            in1=cbuf[:, b, h0:h1], op0=ALU.mult, op1=ALU.add), [gbr, res("wconv"), hres[hi]],
                            [hres[hi]])
                    for hi, (h0, h1) in enumerate(halves):
                        S.op(act, lambda: nc.scalar.activation(out=cbuf[:, b, h0:h1], in_=cbuf[:, b, h0:h1],
                                                               func=AF.Silu), [hres[hi]], [hres[hi]])
                    for hi, (h0, h1) in enumerate(halves):
                        S.op(dve, lambda: nc.vector.tensor_tensor(out=actT[:, b3, h0:h1], in0=ub[:, b, h0:h1],
                                                                  in1=cbuf[:, b, h0:h1], op=ALU.mult),
                             [hres[hi], ubr], [res("actT%d_%d" % (b3, hi))])
                    pend.append((j, b3, ud))
                    if stages and j >= 1:
                        stg = stages.pop(0)
                        if stg is not None:
                            stg()
                while pend:
                    emit_down(*pend.pop(0))
                while stages:
                    stg = stages.pop(0)
                    if stg is not None:
                        stg()

                for tl in range(tpg):
                    ti = G.g * tpg + tl
                    xr = res("xg%d%d" % (G.slot, tl))
                    for half in range(2):
                        bk = pY[tl][half]
                        S.op(dve, lambda: nc.vector.tensor_tensor(
                            out=xg[0:n, G.slot, tl, half * 512:(half + 1) * 512], in0=banks[bk][0:n, 0:512],
                            in1=xg[0:n, G.slot, tl, half * 512:(half + 1) * 512], op=ALU.add), [xr], [bres[bk], xr])
                    xt = xg[0:n, G.slot, tl, :]
                    ssq, ssqr = statc("y_ssq%d" % tl)
                    xb = (xn, xn2)[tl]
                    S.op(act, lambda: nc.scalar.activation(out=xb[0:n, :], in_=xt, func=AF.Square,
                                                           accum_out=ssq[0:n]), [xr], [res("xn%d" % tl), ssqr])
                    rs, rsr = rstd_cached("y%d" % tl, ssq, ssqr, D, n)
                    S.op(dve, lambda: nc.vector.scalar_tensor_tensor(
                        out=xt, in0=xt, scalar=rs[0:n], in1=gfin_b[0:n, :], op0=ALU.mult, op1=ALU.mult),
                        [xr, rsr, res("gfin")], [xr])
                    S.dma(dstream("yo%d" % tl), G.outs["y"][ti * n:(ti + 1) * n, :], xt,
                          reads=[xr], writes=[out_res])
                if G.g == G.ng - 1:
                    for r_ in range(2):
                        S.dma(dstream("c%d" % (1 + r_)), G.outs["conv"][r_].rearrange("(j p) -> p j", p=128),
                              gstate[:, :, r_], reads=[res("gstate")], writes=[out_res],
                              allow_slow_non_contiguous=True)

            load_x(groups[0])
            if len(groups) > 1:
                load_x(groups[1])
            for stg in phaseA_stages(groups[0]):
                if stg is not None:
                    stg()
            for f_, G in enumerate(groups):
                attention(G)
                outproj(G)
                nxt = phaseA_stages(groups[f_ + 1]) if f_ + 1 < len(groups) else []
                dense = f_ + 1 < len(groups) and groups[f_ + 1].tpg == 1
                ffn(G, nxt, 1 if dense else 2)
                if f_ + 2 < len(groups):
                    load_x(groups[f_ + 2])
            S.barrier()
            return ring_log

        plan = emit_main(None)
        emit_main(plan)
    return nc


def rope_tables(seq, past, dec):
    nt = seq // 128
    half = 32
    inv = (np.float32(10000.0) ** (-np.arange(half, dtype=np.float32) * np.float32(2.0 / 64))).astype(np.float32)
    pos = np.zeros((128, nt + 1), np.float32)
    for t in range(nt):
        pos[:, t] = t * 128 + np.arange(128)
    pos[:, nt] = past + np.arange(128)
    ang = (pos[:, :, None] * inv[None, None, :]).astype(np.float32)
    return np.cos(ang).astype(np.float32), np.sin(ang).astype(np.float32)


def make_in_maps(inputs, n_cores, n_prompt, seq, past, dec):
    f = lambda a: np.ascontiguousarray(np.asarray(a, dtype=np.float32))
    cos, sin = rope_tables(seq, past, dec)
    common = {
        "g_attn_pk": f(inputs["g_attn"][0].reshape(8, 128).T),
        "g_ffn_pk": f(inputs["g_ffn"][0].reshape(8, 128).T),
        "g_q_pk": f(inputs["g_q_lora"][0].reshape(2, 128).T),
        "g_sub_p": f(inputs["g_diff_sub"][0].reshape(128, 1)),
        "g_kv": f(inputs["g_kv_lora"][0].reshape(1, 128)),
        "g_final": f(inputs["g_final"].reshape(1, D)),
        "lamv": f(np.stack([inputs["lambda_q1"][0], inputs["lambda_k1"][0], inputs["lambda_q2"][0],
                            inputs["lambda_k2"][0]], 0).reshape(1, 256)),
        "wconv_pk": f(inputs["w_conv"][0].reshape(3, NCH, 128).transpose(2, 1, 0)),
        "bconv_pk": f(inputs["b_conv"][0].reshape(NCH, 128).T),
        "w_in": f(inputs["w_in"][0]), "w_qb": f(inputs["w_q_b"][0]), "w_kvb": f(inputs["w_kv_b"][0]),
        "w_out": f(inputs["w_out"][0]), "w_up": f(inputs["w_up"][0]), "w_down": f(inputs["w_down"][0]),
        "ident": np.eye(128, dtype=np.float32).astype(ml_dtypes.bfloat16),
        "cos_t": cos, "sin_t": sin,
    }
    maps = []
    for c in range(n_cores):
        m = dict(common)
        m["xp"] = f(inputs["x_prompt"][c * n_prompt:(c + 1) * n_prompt])
        m["xs"] = f(inputs["x_sample"][c])
        m["cdk"] = f(inputs["cache_diff_k"][0, c].reshape(past, 512))
        m["cdv"] = f(inputs["cache_diff_v"][0, c].reshape(past, 512))
        m["cckv"] = f(inputs["cache_mla_ckv"][0, c])
        m["ckr"] = f(inputs["cache_mla_krope"][0, c])
        m["sconv"] = f(inputs["state_conv"][0, c])
        maps.append(m)
    return maps


_NC_CACHE = {}


def kernel(**inputs):
    inputs = {k: np.asarray(v) for k, v in inputs.items()}
    B, seq, _ = inputs["x_prompt"].shape
    n_cores = inputs["x_sample"].shape[0]
    n_prompt = B // n_cores
    dec = inputs["x_sample"].shape[1]
    past = inputs["cache_diff_k"].shape[2]
    key = (n_prompt, seq, past, dec)
    if key not in _NC_CACHE:
        _NC_CACHE[key] = build(n_prompt=n_prompt, seq=seq, sample=True, past=past, dec=dec)
    nc = _NC_CACHE[key]
    maps = make_in_maps(inputs, n_cores, n_prompt, seq, past, dec)
    res = run_bass_kernel_spmd(nc, maps, core_ids=list(range(n_cores)))
    r = res.results
    cat = lambda k: np.concatenate([np.asarray(x[k], dtype=np.float32) for x in r], axis=0)
    stk = lambda k: np.stack([np.asarray(x[k], dtype=np.float32) for x in r], axis=0)
    y_prompt = cat("yp")
    y_sample = stk("ys")
    dk_p = cat("dkp").reshape(1, B, seq, 4, 2, 64)
    dv_p = cat("dvp").reshape(1, B, seq, 4, 128)
    ckv_p = cat("ckvp").reshape(1, B, seq, 128)
    kr_p = cat("krp").reshape(1, B, seq, 64)
    conv_p = cat("convp").reshape(1, B, 2, D_FF)
    dk_s = stk("dks").reshape(1, n_cores, dec, 4, 2, 64)
    dv_s = stk("dvs").reshape(1, n_cores, dec, 4, 128)
    ckv_s = stk("ckvs").reshape(1, n_cores, dec, 128)
    kr_s = stk("krs").reshape(1, n_cores, dec, 64)
    conv_s = stk("convs").reshape(1, n_cores, 2, D_FF)
    return (y_prompt, y_sample, dk_p, dv_p, ckv_p, kr_p, conv_p, dk_s, dv_s, ckv_s, kr_s, conv_s)
```

```python
import math
from contextlib import ExitStack

import numpy as np
import ml_dtypes

import concourse.bass as bass
import concourse.mybir as mybir
from concourse.bass_utils import run_bass_kernel_spmd

F32 = mybir.dt.float32
BF16 = mybir.dt.bfloat16
ALU = mybir.AluOpType
AF = mybir.ActivationFunctionType

D = 1024
NCORES = 8
SEQ = 4096
DEC_SEQ = 64
PAST = 1024
IN_COLS = 1984
D_FF = 2816
NCH = 22
EPS = 1e-6
LAM_INIT = 0.8 - 0.6 * math.exp(-0.3 * 0)
MLA_SCALE = (128 + 64) ** -0.5
DIFF_SCALE = 64 ** -0.5
RING_SLOTS = 6
RINGD_SLOTS = 6
RING_W = 2048


class Res:
    __slots__ = ("name", "w", "r")

    def __init__(self, name):
        self.name = name
        self.w = None
        self.r = []


class Stream:
    def __init__(self, sched, name, eng, inc, limit):
        self.sched, self.name, self.eng, self.inc, self.limit = sched, name, eng, inc, limit
        self.epoch = 0
        self.sem = sched.new_sem(name + "_0")
        self.count = 0
        self.total = 0
        self.seen = {}

    def bump(self):
        self.count += 1
        self.total += 1
        ev = ((self.name, self.epoch), self.sem, self.count * self.inc, self.name)
        if self.count >= self.limit:
            self.epoch += 1
            self.sem = self.sched.new_sem("%s_%d" % (self.name, self.epoch))
            self.count = 0
        return ev


class Sched:
    def __init__(self, nc, stack):
        self.nc = nc
        self.stack = stack
        self.nsem = 0
        self.pe = Stream(self, "pe", nc.tensor, 1, 20000)
        self.act = Stream(self, "act", nc.scalar, 1, 20000)
        self.dve = Stream(self, "dve", nc.vector, 1, 20000)
        self.pool = Stream(self, "pool", nc.gpsimd, 1, 20000)
        self.sp = nc.sync
        self.sp_seen = {}
        self.dma_streams = []
        self.last_ev = {}

    def new_sem(self, name):
        self.nsem += 1
        return self.stack.enter_context(self.nc.semaphore(name))

    def dma_stream(self, name):
        s = Stream(self, name, None, 16, 2000)
        self.dma_streams.append(s)
        return s

    def _deps(self, stream_name, reads, writes, same_engine_raw=True):
        need = {}

        def add(ev, same_ok):
            if ev is None:
                return
            key, sem, val, sname = ev
            if sname == stream_name and not same_ok:
                return
            if key not in need or need[key][1] < val:
                need[key] = (sem, val)

        for r in reads:
            add(r.w, same_engine_raw)
        for w in writes:
            add(w.w, False)
            for ev in w.r:
                add(ev, False)
        return need

    @staticmethod
    def _record(ev, reads, writes):
        for r in reads:
            r.r.append(ev)
            if len(r.r) > 48:
                best = {}
                for e in r.r:
                    if e[0] not in best or best[e[0]][2] < e[2]:
                        best[e[0]] = e
                r.r = list(best.values())
        for w in writes:
            w.w = ev
            w.r = []

    def op(self, st, fn, reads=(), writes=()):
        need = self._deps(st.name, reads, writes, same_engine_raw=(st is not self.pe))
        for key, (sem, val) in need.items():
            if st.seen.get(key, 0) < val:
                st.eng.wait_ge(sem, val)
                st.seen[key] = val
        ins = fn()
        ins.then_inc(st.sem, 1)
        ev = st.bump()
        self.last_ev[st.name] = ev
        self._record(ev, reads, writes)
        return ins

    def dma(self, ds, out, in_, reads=(), writes=(), **kw):
        need = self._deps("sp:" + ds.name, reads, writes)
        for key, (sem, val) in need.items():
            if self.sp_seen.get(key, 0) < val:
                self.sp.wait_ge(sem, val)
                self.sp_seen[key] = val
        ins = self.sp.dma_start(out=out, in_=in_, **kw)
        ins.then_inc(ds.sem, 16)
        ev = ds.bump()
        self.last_ev[ds.name] = ev
        self._record(ev, reads, writes)
        return ins

    def barrier(self):
        evs = list(self.last_ev.values())
        for st in (self.pe, self.act, self.dve, self.pool):
            for key, sem, val, sname in evs:
                if sname == st.name:
                    continue
                if st.seen.get(key, 0) < val:
                    st.eng.wait_ge(sem, val)
                    st.seen[key] = val
        for key, sem, val, sname in evs:
            if self.sp_seen.get(key, 0) < val:
                self.sp.wait_ge(sem, val)
                self.sp_seen[key] = val


def build(n_prompt=2, seq=SEQ, sample=True, past=PAST, dec=DEC_SEQ):
    nc = bass.Bass("TRN2", target_bir_lowering=False)
    NT = seq // 128
    NKT = max(NT, past // 128 + 1)
    NTAB = NT + 1

    def din(name, shape, dt=F32):
        return nc.dram_tensor(name, list(shape), dt, kind="ExternalInput").ap()

    def dout(name, shape, dt=F32):
        return nc.dram_tensor(name, list(shape), dt, kind="ExternalOutput").ap()

    xp = din("xp", [n_prompt, seq, D])
    xs = din("xs", [dec, D])
    cdk = din("cdk", [past, 512])
    cdv = din("cdv", [past, 512])
    cckv = din("cckv", [past, 128])
    ckr = din("ckr", [past, 64])
    sconv = din("sconv", [2, D_FF])
    g_attn_pk = din("g_attn_pk", [128, 8])
    g_ffn_pk = din("g_ffn_pk", [128, 8])
    g_q_pk = din("g_q_pk", [128, 2])
    g_sub_p = din("g_sub_p", [128, 1])
    g_kv = din("g_kv", [1, 128])
    g_final = din("g_final", [1, D])
    lamv = din("lamv", [1, 256])
    wconv_pk = din("wconv_pk", [128, NCH, 3])
    bconv_pk = din("bconv_pk", [128, NCH])
    w_in = din("w_in", [D, IN_COLS])
    w_qb = din("w_qb", [256, 768])
    w_kvb = din("w_kvb", [128, 1024])
    w_out = din("w_out", [D, D])
    w_up = din("w_up", [D, 2 * D_FF])
    w_down = din("w_down", [D_FF, D])
    ident_d = din("ident", [128, 128], BF16)
    cos_d = din("cos_t", [128, NTAB, 32])
    sin_d = din("sin_t", [128, NTAB, 32])
    yp = dout("yp", [n_prompt, seq, D])
    dkp = dout("dkp", [n_prompt, seq, 512])
    dvp = dout("dvp", [n_prompt, seq, 512])
    ckvp = dout("ckvp", [n_prompt, seq, 128])
    krp = dout("krp", [n_prompt, seq, 64])
    convp = dout("convp", [n_prompt, 2, D_FF])
    ys = dout("ys", [dec, D])
    dks = dout("dks", [dec, 512])
    dvs = dout("dvs", [dec, 512])
    ckvs = dout("ckvs", [dec, 128])
    krs = dout("krs", [dec, 64])
    convs = dout("convs", [2, D_FF])
    win_s = nc.dram_tensor("win_s", [4, 128, 8, 512], BF16).ap()
    wout_s = nc.dram_tensor("wout_s", [2, 128, 8, 512], BF16).ap()
    ffn_s = nc.dram_tensor("ffn_s", [NCH, 128, 3072], BF16).ap()

    with ExitStack() as st:
        S = Sched(nc, st)
        pe, act, dve, pool = S.pe, S.act, S.dve, S.pool

        def sb(name, shape, dt):
            return st.enter_context(nc.sbuf_tensor("s_" + name, list(shape), dt))

        banks = [st.enter_context(nc.psum_tensor("bank%d" % i, [128, 512], F32)) for i in (0, 1)]
        pT = st.enter_context(nc.psum_tensor("bankT", [128, 1024], BF16))
        banks += [None]
        pS_all = st.enter_context(nc.psum_tensor("bankS", [128, 2048], F32))
        banks += [pS_all[:, i * 512:(i + 1) * 512] for i in range(4)]
        banks += [st.enter_context(nc.psum_tensor("bank7", [128, 512], F32))]
        bres = [Res("bank%d" % i) for i in range(8)]
        pTr = bres[2]

        out_res = Res("outputs")
        scr_res = {"win": Res("win_s"), "wout": Res("wout_s"), "ffn": Res("ffn_s")}

        ident = sb("ident", [128, 128], BF16)
        cos_t = sb("cos_t", [128, NTAB, 32], F32)
        sin_t = sb("sin_t", [128, NTAB, 32], F32)
        gkv_b = sb("gkv_b", [128, 128], F32)
        gfin_b = sb("gfin_b", [128, D], F32)
        wconv = sb("wconv", [128, NCH, 3], F32)
        bconv = sb("bconv", [128, NCH], F32)
        neg_lam = sb("neg_lam", [128, 1], F32)
        wq_eff = sb("wq_eff", [128, 2, 4, 128], BF16)
        wqr = sb("wqr", [128, 2, 4, 64], BF16)
        R = {}

        def res(name):
            if name not in R:
                R[name] = Res(name)
            return R[name]

        _stat_next = [0]

        def stat(name, width=1):
            i = _stat_next[0]
            _stat_next[0] += width
            assert _stat_next[0] <= 64
            return st_small[:, i:i + width], res("stat_" + name)

        dq = {}

        def dstream(name):
            if getattr(S, "dry", False):
                return None
            if name not in dq:
                dq[name] = S.dma_stream("d_" + name)
            return dq[name]

        def mm(out, lhsT, rhs, start, stop, reads, writes):
            S.op(pe, lambda: nc.tensor.matmul(out, lhsT=lhsT, rhs=rhs, start=start, stop=stop,
                                              skip_group_check=True), reads, writes)

        def tr(out, in_, reads, writes):
            n = in_.shape[0]
            S.op(pe, lambda: nc.tensor.transpose(out=out, in_=in_, identity=ident[0:n, 0:n]),
                 list(reads) + [res("ident")], writes)

        def eng_of(stm):
            return {"act": nc.scalar, "dve": nc.vector, "pool": nc.gpsimd}[stm.name]

        def copy(stm, out, in_, reads, writes):
            if stm is act:
                S.op(act, lambda: nc.scalar.copy(out=out, in_=in_), reads, writes)
            else:
                e = eng_of(stm)
                S.op(stm, lambda: e.tensor_copy(out=out, in_=in_), reads, writes)

        def rstd_from_ssq(ssq_ap, ssq_res, n_feat, nm, n):
            ms, msr = stat(nm + "_ms")
            rs, rsr = stat(nm + "_rs")
            S.op(dve, lambda: nc.vector.tensor_scalar(out=ms[0:n], in0=ssq_ap[0:n], scalar1=1.0 / n_feat,
                                                      scalar2=EPS, op0=ALU.mult, op1=ALU.add),
                 [ssq_res], [msr])
            S.op(act, lambda: nc.scalar.activation(out=ms[0:n], in_=ms[0:n], func=AF.Ln), [msr], [msr])
            S.op(act, lambda: nc.scalar.activation(out=rs[0:n], in_=ms[0:n], func=AF.Exp, scale=-0.5),
                 [msr], [rsr])
            return rs, rsr

        S.dma(dstream("k0"), ident[:], ident_d[:], writes=[res("ident")])
        S.dma(dstream("k1"), cos_t[:], cos_d[:], writes=[res("tab")])
        S.dma(dstream("k2"), sin_t[:], sin_d[:], writes=[res("tab")])
        S.dma(dstream("k3"), gkv_b[:], g_kv.partition_broadcast(128), writes=[res("gkv")])
        S.dma(dstream("k4"), gfin_b[:], g_final.partition_broadcast(128), writes=[res("gfin")])
        S.dma(dstream("k5"), wconv[:], wconv_pk[:], writes=[res("wconv")])
        S.dma(dstream("k6"), bconv[:], bconv_pk[:], writes=[res("wconv")])

        with ExitStack() as st2:
            def sb2(name, shape, dt):
                return st2.enter_context(nc.sbuf_tensor("t_" + name, list(shape), dt))

            gA = sb2("gA", [128, 8], F32)
            gF = sb2("gF", [128, 8], F32)
            gQ = sb2("gQ", [128, 2], F32)
            gS = sb2("gS", [128, 1], F32)
            lamb = sb2("lamb", [128, 4, 64], F32)
            lj = sb2("lj", [128, 64], F32)
            ls = sb2("ls", [128, 4], F32)
            S.dma(dstream("k7"), gA[:], g_attn_pk[:], writes=[res("gA")])
            S.dma(dstream("k8"), gF[:], g_ffn_pk[:], writes=[res("gF")])
            S.dma(dstream("k9"), gQ[:], g_q_pk[:], writes=[res("gQ")])
            S.dma(dstream("k10"), gS[:], g_sub_p[:], writes=[res("gS")])
            S.dma(dstream("klam"), lamb[:].rearrange("p a b -> p (a b)"), lamv.partition_broadcast(128),
                  writes=[res("lamb")])
            for i in range(2):
                S.op(dve, lambda i=i: nc.vector.scalar_tensor_tensor(
                    out=lj[:], in0=lamb[:, 2 * i, :], scalar=1.0, in1=lamb[:, 2 * i + 1, :],
                    op0=ALU.mult, op1=ALU.mult, accum_out=ls[:, i:i + 1]),
                    [res("lamb")], [res("lj"), res("ls")])
            S.op(act, lambda: nc.scalar.activation(out=ls[:, 2:4], in_=ls[:, 0:2], func=AF.Exp),
                 [res("ls")], [res("ls2")])
            S.op(dve, lambda: nc.vector.scalar_tensor_tensor(
                out=neg_lam[:], in0=ls[:, 3:4], scalar=-LAM_INIT, in1=ls[:, 2:3],
                op0=ALU.add, op1=ALU.subtract), [res("ls2")], [res("neg_lam")])
            S.op(dve, lambda: nc.vector.tensor_scalar(out=gS[:], in0=gS[:], scalar1=1.0 - LAM_INIT, scalar2=None,
                                                      op0=ALU.mult), [res("gS")], [res("gS")])

            NB = 4
            wst = sb2("wst", [128, NB, 2048], F32)
            wbf = sb2("wbf", [128, NB, 4, 512], BF16)
            S.op(pool, lambda: nc.gpsimd.memset(wbf[:], 0.0), [], [res("wbf%d" % b) for b in range(NB)])
            win_v = win_s.rearrange("s p k n -> p s k n")
            def win_load(k):
                b = k % NB
                S.dma(dstream("w%d" % b), wst[:, b, 0:IN_COLS], w_in[k * 128:(k + 1) * 128, :],
                      writes=[res("wst%d" % b)])
            for k in range(NB - 2):
                win_load(k)
            for k in range(8):
                b = k % NB
                if k + NB - 2 < 8:
                    win_load(k + NB - 2)
                for s_ in range(4):
                    wd = 512 if s_ < 3 else 448
                    if s_ % 2 == 0:
                        S.op(dve, lambda s_=s_, wd=wd, b=b, k=k: nc.vector.tensor_scalar(
                            out=wbf[:, b, s_, 0:wd], in0=wst[:, b, s_ * 512:s_ * 512 + wd],
                            scalar1=gA[:, k:k + 1], scalar2=None, op0=ALU.mult),
                            [res("wst%d" % b), res("gA")], [res("wbf%d" % b)])
                    else:
                        S.op(act, lambda s_=s_, wd=wd, b=b, k=k: nc.scalar.activation(
                            out=wbf[:, b, s_, 0:wd], in_=wst[:, b, s_ * 512:s_ * 512 + wd], func=AF.Copy,
                            scale=gA[:, k:k + 1]), [res("wst%d" % b), res("gA")], [res("wbf%d" % b)])
                S.dma(dstream("ws%d" % b), win_v[:, :, k, :], wbf[:, b], reads=[res("wbf%d" % b)],
                      writes=[scr_res["win"]])

            wqf = sb2("wqf", [128, 2, 768], F32)
            wqb_bf = sb2("wqb_bf", [128, 2, 768], BF16)
            wkvf = sb2("wkvf", [128, 1024], F32)
            wkvb_bf = sb2("wkvb_bf", [128, 1024], BF16)
            wkT = sb2("wkT", [128, 4, 128], BF16)
            wvT = sb2("wvT", [128, 4, 128], BF16)
            wqnT = sb2("wqnT", [128, 4, 2, 128], BF16)
            S.dma(dstream("k11"), wqf[:], w_qb.rearrange("(c p) n -> p c n", p=128), writes=[res("wqf")])
            S.dma(dstream("k12"), wkvf[:], w_kvb[:], writes=[res("wkvf")])
            for c in range(2):
                S.op(dve, lambda c=c: nc.vector.tensor_scalar(out=wqb_bf[:, c, :], in0=wqf[:, c, :],
                                                              scalar1=gQ[:, c:c + 1], scalar2=None, op0=ALU.mult),
                     [res("wqf"), res("gQ")], [res("wqb_bf")])
            S.op(pool, lambda: nc.gpsimd.tensor_copy(out=wkvb_bf[:], in_=wkvf[:]), [res("wkvf")], [res("wkvb_bf")])
            wqb_v = wqb_bf[:].rearrange("p c (h n) -> p c h n", h=4)
            S.op(dve, lambda: nc.vector.tensor_copy(out=wqr[:], in_=wqb_v[:, :, :, 128:192]),
                 [res("wqb_bf")], [res("wqr")])
            for h in range(4):
                tr(pT[:, h * 128:(h + 1) * 128], wkvb_bf[:, h * 256:h * 256 + 128], [res("wkvb_bf")], [pTr])
                tr(pT[:, 512 + h * 128:512 + (h + 1) * 128], wkvb_bf[:, h * 256 + 128:h * 256 + 256],
                   [res("wkvb_bf")], [pTr])
            S.op(dve, lambda: nc.vector.tensor_copy(out=wkT[:], in_=pT[:, 0:512].rearrange("p (h l) -> p h l", h=4)),
                 [pTr], [pTr, res("wkT")])
            S.op(dve, lambda: nc.vector.tensor_copy(out=wvT[:], in_=pT[:, 512:1024].rearrange("p (h l) -> p h l", h=4)),
                 [pTr], [pTr, res("wvT")])
            for h in range(4):
                for c in range(2):
                    tr(pT[:, (h * 2 + c) * 128:(h * 2 + c + 1) * 128], wqb_v[:, c, h, 0:128], [res("wqb_bf")], [pTr])
            S.op(dve, lambda: nc.vector.tensor_copy(
                out=wqnT[:], in_=pT[:, 0:1024].rearrange("p (h c q) -> p h c q", h=4, c=2)),
                [pTr], [pTr, res("wqnT")])
            for c in range(2):
                for h in range(4):
                    mm(banks[c][:, h * 128:(h + 1) * 128], wqnT[:, h, c, :], wkT[:, h, :], True, True,
                       [res("wqnT"), res("wkT")], [bres[c]])
                S.op(dve, lambda c=c: nc.vector.tensor_copy(
                    out=wq_eff[:, c], in_=banks[c][:, 0:512].rearrange("p (h l) -> p h l", h=4)),
                    [], [bres[c], res("wq_eff")])

            wof = sb2("wof", [128, 2, 1024], F32)
            wob = sb2("wob", [128, 2, 1024], BF16)
            woe = sb2("woe", [128, 8, 1024], BF16)
            for c in range(8):
                b = c % 2
                S.dma(dstream("w%d" % b), wof[:, b, :], w_out[c * 128:(c + 1) * 128, :], writes=[res("wof%d" % b)])
                if c < 4:
                    S.op(dve, lambda c=c, b=b: nc.vector.tensor_scalar(
                        out=woe[:, c, :], in0=wof[:, b, :], scalar1=gS[:, 0:1], scalar2=None, op0=ALU.mult),
                        [res("wof%d" % b), res("gS")], [res("woe")])
                else:
                    h = c - 4
                    S.op(pool, lambda b=b: nc.gpsimd.tensor_copy(out=wob[:, b, :], in_=wof[:, b, :]),
                         [res("wof%d" % b)], [res("wob%d" % b)])
                    for half in range(2):
                        mm(banks[half][:, 0:512], wvT[:, h, :], wob[:, b, half * 512:(half + 1) * 512], True, True,
                           [res("wvT"), res("wob%d" % b)], [bres[half]])
                        S.op(act, lambda c=c, half=half: nc.scalar.copy(
                            out=woe[:, c, half * 512:(half + 1) * 512], in_=banks[half][:, 0:512]),
                            [], [bres[half], res("woe")])
            for half in range(2):
                S.dma(dstream("ws%d" % half), wout_s[half], woe[:, :, half * 512:(half + 1) * 512],
                      reads=[res("woe")], writes=[scr_res["wout"]])

            NB = 5
            fst = sb2("fst", [128, NB, 8, 2, 128], F32)
            fdn = sb2("fdn", [128, NB, 1024], F32)
            fbf = sb2("fbf", [128, NB, 3072], BF16)
            w_up_v = w_up.rearrange("(k p) (u j c) -> p k u j c", p=128, u=2, c=128)
            def ffn_load(j):
                b = j % NB
                for u in range(2):
                    S.dma(dstream("f%d" % b), fst[:, b, :, u, :], w_up_v[:, :, u, j, :],
                          writes=[res("fst%d" % b)])
                S.dma(dstream("fd%d" % b), fdn[:, b, :], w_down[j * 128:(j + 1) * 128, :], writes=[res("fdn%d" % b)])
            for j in range(NB - 2):
                ffn_load(j)
            for j in range(NCH):
                b = j % NB
                if j + NB - 2 < NCH:
                    ffn_load(j + NB - 2)
                S.op(dve, lambda b=b: nc.vector.tensor_tensor(
                    out=fbf[:, b, 0:2048].rearrange("p (k x) -> p k x", k=8),
                    in0=fst[:, b].rearrange("p k u c -> p k (u c)"),
                    in1=gF[:, :].unsqueeze(2).to_broadcast([128, 8, 256]), op=ALU.mult),
                    [res("fst%d" % b), res("gF")], [res("fbfu%d" % b)])
                S.op(act, lambda b=b: nc.scalar.copy(out=fbf[:, b, 2048:3072], in_=fdn[:, b, :]),
                     [res("fdn%d" % b)], [res("fbfd%d" % b)])
                S.dma(dstream("fs%d" % b), ffn_s[j], fbf[:, b, :], reads=[res("fbfu%d" % b), res("fbfd%d" % b)],
                      writes=[scr_res["ffn"]])
            S.barrier()

        KT = sb("KT", [128, 4, NKT * 128], BF16)
        VX = sb("VX", [128, NKT, 4, 130], BF16)
        CT = sb("CT", [128, NKT * 128], BF16)
        RT = sb("RT", [128, NKT * 128], BF16)
        CX = sb("CX", [128, NKT, 130], BF16)
        ring = sb("ring", [128, RING_SLOTS, RING_W], BF16)
        ringD = sb("ringD", [128, RINGD_SLOTS, 1024], BF16)
        xg = sb("xg", [128, 2, 2, D], F32)
        xn = sb("xn", [128, D], BF16)
        xnT = sb("xnT", [128, 8, 256], BF16)
        mixT = xnT
        h2T = xnT
        qf = sb("qf", [128, 1, 512], F32)
        ra = sb("ra", [128, 1, 512], F32)
        rb = sb("rb", [128, 1, 512], F32)
        qrb = sb("qrb", [128, 2, 512], BF16)
        kout = sb("kout", [128, 1, 512], F32)
        vout = sb("vout", [128, 1, 512], F32)
        mf = qf
        cko = sb("cko", [128, 1, 128], F32)
        kro = sb("kro", [128, 2, 64], F32)
        krd = sb("krd", [128, 2, 128], BF16)
        cqn = sb("cqn", [128, 2, 256], BF16)
        cqT = sb("cqT", [128, 2, 256], BF16)
        QT = sb("QT", [128, 4, 256], BF16)
        QpT = sb("QpT", [128, 4, 256], BF16)
        QRT = sb("QRT", [128, 2, 256], BF16)
        PT = sb("PT", [128, 2, 2, 2, 256], BF16)
        st_small = sb("st_small", [128, 64], F32)
        t0 = sb("t0", [128, 2, 128], F32)
        od = sb("od", [128, 2, 128], F32)
        gbuf = sb("gbuf", [128, 2, 258], F32)
        cbuf = sb("cbuf", [128, 2, 256], F32)
        actT = sb("actT", [128, 3, 256], BF16)
        gstate = sb("gstate", [128, NCH, 2], F32)
        ub = sb("ub", [128, 2, 256], F32)

        xn2 = sb("xn2", [128, D], BF16)
        h2T = sb("h2T_", [128, 8, 256], BF16)
        osb = sb("osb", [128, 2, 2, 129], F32)
        onall = sb("onall", [128, 2, 2, 2, 128], BF16)

        S.op(pool, lambda: nc.gpsimd.memset(VX[:, :, :, 128:130], 1.0), [], [res("VXones")])
        S.op(pool, lambda: nc.gpsimd.memset(CX[:, :, 128:130], 1.0), [], [res("CXones")])
        mhalf = sb("mhalf", [128, 1], F32)
        S.op(pool, lambda: nc.gpsimd.memset(mhalf[:], -0.5), [], [res("mhalf")])
        S.barrier()

        statcache = {}
        rstdcache = {}

        def statc(name, width=1):
            if name not in statcache:
                statcache[name] = stat(name, width)
            return statcache[name]

        def rstd_cached(nm, ssq_ap, ssq_res, n_feat, n):
            if nm not in rstdcache:
                rstdcache[nm] = (statc(nm + "_ms"), statc(nm + "_rs"))
            (ms, msr), (rs, rsr) = rstdcache[nm]
            S.op(dve, lambda: nc.vector.tensor_scalar(out=ms[0:n], in0=ssq_ap[0:n], scalar1=1.0 / n_feat,
                                                      scalar2=EPS, op0=ALU.mult, op1=ALU.add), [ssq_res], [msr])
            S.op(pool, lambda: nc.gpsimd.tensor_tensor(out=rs[0:n], in0=ms[0:n], in1=mhalf[0:n], op=ALU.pow),
                 [msr, res("mhalf")], [rsr])
            return rs, rsr

        real_S = S

        class DrySched:
            dry = True

            def __init__(self):
                class _E:
                    def __init__(self, n):
                        self.name = n
                self.pe, self.act, self.dve, self.pool = _E("pe"), _E("act"), _E("dve"), _E("pool")

            def op(self, *a, **k):
                return None

            def dma(self, *a, **k):
                return None

            def barrier(self):
                return None

            def dma_stream(self, name):
                return None

        class Group:
            pass

        groups = []
        for sq_i in range(n_prompt):
            ng = seq // 256
            for g in range(ng):
                G = Group()
                G.x_ap, G.tsz, G.tpg, G.g, G.ng, G.npast, G.tab0 = xp[sq_i], 128, 2, g, ng, 0, 0
                G.outs = {"y": yp[sq_i], "dk": dkp[sq_i], "dv": dvp[sq_i], "ckv": ckvp[sq_i], "kr": krp[sq_i],
                          "conv": convp[sq_i]}
                G.conv_init = None
                G.sample = False
                groups.append(G)
        if sample:
            G = Group()
            G.x_ap, G.tsz, G.tpg, G.g, G.ng, G.npast, G.tab0 = xs, dec, 1, 0, 1, past // 128, NT
            G.outs = {"y": ys, "dk": dks, "dv": dvs, "ckv": ckvs, "kr": krs, "conv": convs}
            G.conv_init = sconv
            G.sample = True
            groups.append(G)
        for f_, G in enumerate(groups):
            G.f = f_
            G.slot = f_ % 2
            G.nq = G.tsz * G.tpg

        def emit_main(plan):
            nonlocal S, pe, act, dve, pool
            dry = plan is None
            if dry:
                S = DrySched()
            else:
                S = real_S
            pe, act, dve, pool = S.pe, S.act, S.dve, S.pool
            RINGS = {"M": (ring, RING_SLOTS), "D": (ringD, RINGD_SLOTS)}
            ring_log = {"M": [], "D": []}
            ring_res = {k: [Res("ring%s%d" % (k, i)) for i in range(v[1])] for k, v in RINGS.items()}
            rstate = {k: {"loaded": 0, "next": 0, "released": set()} for k in RINGS}
            kvres = {}

            def kvr(buf, kt):
                key = (buf, kt)
                if key not in kvres:
                    kvres[key] = Res("%s%d" % (buf, kt))
                return kvres[key]

            def ring_src(spec):
                kind, a, b = spec
                if kind == "win":
                    return win_s[a][:, b * 4:(b + 1) * 4, :], 2048, scr_res["win"]
                if kind == "wout":
                    return wout_s[a][:, b * 4:(b + 1) * 4, :], 2048, scr_res["wout"]
                if kind == "up":
                    return ffn_s[a][:, 0:2048], 2048, scr_res["ffn"]
                return ffn_s[a][:, 2048:3072], 1024, scr_res["ffn"]

            def ring_prefetch():
                if dry:
                    return
                progress = True
                while progress:
                    progress = False
                    for k in ("M", "D"):
                        rs_, (rt, nsl), pl = rstate[k], RINGS[k], plan[k]
                        u = rs_["loaded"]
                        if u >= len(pl):
                            continue
                        if u >= nsl and (u - nsl) not in rs_["released"]:
                            continue
                        if u > rs_["next"] + nsl - 1:
                            continue
                        ap, wd, sres = ring_src(pl[u])
                        slot = u % nsl
                        if pl[u][0] in ("win", "wout"):
                            dst = rt[:, slot, 0:2048].rearrange("p (k n) -> p k n", k=4)
                        else:
                            dst = rt[:, slot, 0:wd]
                        S.dma(dstream("ring%s%d" % (k, slot)), dst, ap, reads=[sres], writes=[ring_res[k][slot]])
                        rs_["loaded"] += 1
                        progress = True

            def ring_next(spec):
                k = "D" if spec[0] == "down" else "M"
                rs_, (rt, nsl) = rstate[k], RINGS[k]
                u = rs_["next"]
                ring_log[k].append(spec)
                if not dry:
                    assert plan[k][u] == spec, (k, u, plan[k][u], spec)
                rs_["next"] += 1
                ring_prefetch()
                if not dry:
                    assert rs_["loaded"] > u, "ring %s deadlock: unit %d not loadable" % (k, u)
                slot = u % nsl
                return (k, u), rt[:, slot, :], ring_res[k][slot]

            def ring_release(h):
                rstate[h[0]]["released"].add(h[1])
                ring_prefetch()

            def rope(src, n, nh, tab_i, out_ap, reads, out_res_list, e1, e2):
                xv = src.rearrange("p (h t j) -> p h t j", h=nh, t=2)
                ov = out_ap.rearrange("p (h t j) -> p h t j", h=nh, t=2)
                av = ra[0:n, 0, 0:nh * 64].rearrange("p (h t j) -> p h t j", h=nh, t=2)
                bv = rb[0:n, 0, 0:nh * 64].rearrange("p (h t j) -> p h t j", h=nh, t=2)
                cosb = cos_t[0:n, tab_i, :].unsqueeze(1).unsqueeze(1).to_broadcast([n, nh, 2, 32])
                sinb = sin_t[0:n, tab_i, :].unsqueeze(1).to_broadcast([n, nh, 32])
                rar, rbr = res("ra0"), res("rb0")
                S.op(e2, lambda: eng_of(e2).tensor_tensor(out=av, in0=xv, in1=cosb, op=ALU.mult),
                     list(reads) + [res("tab")], [rar])
                S.op(e1, lambda: eng_of(e1).tensor_tensor(out=bv[:, :, 0, :], in0=xv[:, :, 1, :], in1=sinb,
                                                          op=ALU.mult), list(reads) + [res("tab")], [rbr])
                S.op(e1, lambda: eng_of(e1).tensor_tensor(out=bv[:, :, 1, :], in0=xv[:, :, 0, :], in1=sinb,
                                                          op=ALU.mult), list(reads) + [res("tab")], [rbr])
                S.op(e1, lambda: eng_of(e1).tensor_tensor(out=ov[:, :, 0, :], in0=av[:, :, 0, :], in1=bv[:, :, 0, :],
                                                          op=ALU.subtract), [rar, rbr], out_res_list)
                S.op(e1, lambda: eng_of(e1).tensor_tensor(out=ov[:, :, 1, :], in0=av[:, :, 1, :], in1=bv[:, :, 1, :],
                                                          op=ALU.add), [rar, rbr], out_res_list)

            def load_x(G):
                for tl in range(G.tpg):
                    ti = G.g * G.tpg + tl
                    S.dma(dstream("x%d%d" % (G.slot, tl)), xg[0:G.tsz, G.slot, tl, :],
                          G.x_ap[ti * G.tsz:(ti + 1) * G.tsz, :], writes=[res("xg%d%d" % (G.slot, tl))])

            PA = 7

            def phaseA_stages(G):
                n, tpg, nq = G.tsz, G.tpg, G.nq
                stages = []
                xnb = [xn, xn2]

                if G.sample:
                    for kt in range(past // 128):
                        def st_past(kt=kt):
                            r0 = kt * 128
                            S.dma(dstream("c0"), qf[:, 0, :], cdk[r0:r0 + 128, :], writes=[res("qf0")])
                            S.dma(dstream("c1"), ra[:, 0, :], cdv[r0:r0 + 128, :], writes=[res("ra0")])
                            S.dma(dstream("c2"), rb[:, 0, 0:128], cckv[r0:r0 + 128, :], writes=[res("rb0")])
                            S.dma(dstream("c2"), rb[:, 0, 128:192], ckr[r0:r0 + 128, :], writes=[res("rb0")])
                            S.op(pool, lambda: nc.gpsimd.tensor_copy(out=qrb[:, 0, :], in_=qf[:, 0, :]),
                                 [res("qf0")], [res("qrb0")])
                            for h in range(4):
                                tr(pT[:, h * 128:(h + 1) * 128], qrb[:, 0, h * 128:(h + 1) * 128], [res("qrb0")], [pTr])
                            S.op(dve, lambda: nc.vector.tensor_copy(
                                out=KT[:, :, r0:r0 + 128], in_=pT[:, 0:512].rearrange("p (h t) -> p h t", h=4)),
                                [], [pTr, kvr("KT", kt)])
                            S.op(pool, lambda: nc.gpsimd.tensor_copy(
                                out=VX[:, kt, :, 0:128], in_=ra[:, 0, :].rearrange("p (h e) -> p h e", h=4)),
                                [res("ra0")], [kvr("VX", kt)])
                            S.op(pool, lambda: nc.gpsimd.tensor_copy(out=CX[:, kt, 0:128], in_=rb[:, 0, 0:128]),
                                 [res("rb0")], [kvr("CX", kt)])
                            for dd in range(2):
                                S.op(pool, lambda dd=dd: nc.gpsimd.tensor_copy(
                                    out=krd[:, 0, dd * 64:(dd + 1) * 64], in_=rb[:, 0, 128:192]),
                                    [res("rb0")], [res("krd0")])
                            tr(pT[:, 512:640], CX[:, kt, 0:128], [kvr("CX", kt)], [pTr])
                            tr(pT[:, 640:768], krd[:, 0, :], [res("krd0")], [pTr])
                            S.op(dve, lambda: nc.vector.tensor_copy(out=CT[:, r0:r0 + 128], in_=pT[:, 512:640]),
                                 [], [pTr, kvr("CT", kt)])
                            S.op(dve, lambda: nc.vector.tensor_copy(out=RT[:, r0:r0 + 128], in_=pT[:, 640:768]),
                                 [], [pTr, kvr("RT", kt)])
                        stages.append(st_past)

                def st_norm():
                    for tl in range(tpg):
                        xr = res("xg%d%d" % (G.slot, tl))
                        xt = xg[0:n, G.slot, tl, :]
                        xb = xnb[tl]
                        ssq, ssqr = statc("a_ssq%d" % tl)
                        S.op(act, lambda: nc.scalar.activation(out=xb[0:n, :], in_=xt, func=AF.Square,
                                                               accum_out=ssq[0:n]), [xr], [res("xn%d" % tl), ssqr])
                        rs, rsr = rstd_cached("a%d" % tl, ssq, ssqr, D, n)
                        S.op(dve, lambda: nc.vector.tensor_scalar(out=xb[0:n, :], in0=xt, scalar1=rs[0:n],
                                                                  scalar2=None, op0=ALU.mult),
                             [xr, rsr], [res("xn%d" % tl)])
                stages.append(st_norm)
                stages.extend([None, None])

                for tl in range(tpg):
                    def st_xT(tl=tl):
                        xb = xnb[tl]
                        for k in range(8):
                            tr(pT[:, k * 128:k * 128 + n], xb[0:n, k * 128:(k + 1) * 128], [res("xn%d" % tl)], [pTr])
                        S.op(dve, lambda: nc.vector.tensor_copy(
                            out=xnT[:, :, tl * 128:tl * 128 + n],
                            in_=pT[:, :].rearrange("p (k t) -> p k t", k=8)[:, :, 0:n]), [],
                            [pTr, res("xnT%d" % tl)] + [res("mixT%d" % c) for c in range(8)])
                    stages.append(st_xT)

                pending_T = []
                stage_no = [0]

                def flush_T(all_=False):
                    stage_no[0] += 1
                    while pending_T and (all_ or pending_T[0][0] <= stage_no[0] - tpg):
                        pending_T.pop(0)[1]()

                def defer_T(fn):
                    pending_T.append((stage_no[0], fn))

                for s_ in (3, 0, 1, 2):
                    wd = 512 if s_ < 3 else 448
                    units = {}
                    for tl in range(tpg):
                        def st_seg(s_=s_, tl=tl, wd=wd, units=units):
                            flush_T()
                            ti = G.g * tpg + tl
                            kt = G.npast + ti
                            tok0 = kt * 128
                            tab_i = G.tab0 + ti
                            if tl == 0:
                                for hk in range(2):
                                    units[hk] = ring_next(("win", s_, hk))
                            for k in range(8):
                                u_, unit, ur = units[k // 4]
                                uv = unit[:, 0:2048].rearrange("p (k n) -> p k n", k=4)
                                mm(banks[PA][0:n, 0:wd], xnT[:, k, tl * 128:tl * 128 + n], uv[:, k % 4, 0:wd],
                                   k == 0, k == 7, [res("xnT%d" % tl), ur], [bres[PA]])
                            if tl == tpg - 1:
                                for hk in range(2):
                                    ring_release(units[hk][0])
                            bk = PA
                            if s_ == 0:
                                copy(act, qf[0:n, 0, :], banks[bk][0:n, 0:512], [], [bres[bk], res("qf0")])
                                rope(qf[0:n, 0, :], n, 8, tab_i, qrb[0:n, tl, :], [res("qf0")],
                                     [res("qrb%d" % tl)], dve, pool)

                                def tq(tl=tl):
                                    for h in range(4):
                                        tr(pT[:, h * 128:h * 128 + n], qrb[0:n, tl, h * 128:(h + 1) * 128],
                                           [res("qrb%d" % tl)], [pTr])
                                    S.op(dve, lambda: nc.vector.tensor_copy(
                                        out=QT[:, :, tl * 128:tl * 128 + n],
                                        in_=pT[:, 0:512].rearrange("p (h t) -> p h t", h=4)[:, :, 0:n]),
                                        [], [pTr, res("QT")])
                                defer_T(tq)
                            elif s_ == 1:
                                copy(act, qf[0:n, 0, :], banks[bk][0:n, 0:512], [], [bres[bk], res("qf0")])
                                rope(qf[0:n, 0, :], n, 8, tab_i, kout[0:n, 0, :], [res("qf0")],
                                     [res("kout0")], dve, pool)
                                S.dma(dstream("ko0"), G.outs["dk"][ti * n:(ti + 1) * n, :], kout[0:n, 0, :],
                                      reads=[res("kout0")], writes=[out_res])
                                S.op(pool, lambda: nc.gpsimd.tensor_copy(out=qrb[0:n, tl, :], in_=kout[0:n, 0, :]),
                                     [res("kout0")], [res("qrb%d" % tl)])

                                def tk(tl=tl, kt=kt, tok0=tok0):
                                    for h in range(4):
                                        tr(pT[:, h * 128:h * 128 + n], qrb[0:n, tl, h * 128:(h + 1) * 128],
                                           [res("qrb%d" % tl)], [pTr])
                                    S.op(dve, lambda: nc.vector.tensor_copy(
                                        out=KT[:, :, tok0:tok0 + n],
                                        in_=pT[:, 0:512].rearrange("p (h t) -> p h t", h=4)[:, :, 0:n]),
                                        [], [pTr, kvr("KT", kt)])
                                defer_T(tk)
                            elif s_ == 2:
                                copy(act, vout[0:n, 0, :], banks[bk][0:n, 0:512], [], [bres[bk], res("vout0")])
                                S.dma(dstream("vo0"), G.outs["dv"][ti * n:(ti + 1) * n, :], vout[0:n, 0, :],
                                      reads=[res("vout0")], writes=[out_res])
                                S.op(pool, lambda: nc.gpsimd.tensor_copy(
                                    out=VX[0:n, kt, :, 0:128],
                                    in_=vout[0:n, 0, :].rearrange("p (h e) -> p h e", h=4)),
                                    [res("vout0")], [kvr("VX", kt)])
                            else:
                                copy(act, mf[0:n, 0, 0:448], banks[bk][0:n, 0:448], [], [bres[bk], res("qf0")])
                                mfr = res("qf0")
                                sq, sqr = statc("cq_ssq")
                                S.op(dve, lambda: nc.vector.scalar_tensor_tensor(
                                    out=ra[0:n, 0, 0:256], in0=mf[0:n, 0, 0:256], scalar=1.0, in1=mf[0:n, 0, 0:256],
                                    op0=ALU.mult, op1=ALU.mult, accum_out=sq[0:n]), [mfr], [res("ra0"), sqr])
                                rs, rsr = rstd_cached("cq", sq, sqr, 256, n)
                                S.op(dve, lambda: nc.vector.tensor_scalar(
                                    out=cqn[0:n, tl, :], in0=mf[0:n, 0, 0:256], scalar1=rs[0:n], scalar2=None,
                                    op0=ALU.mult), [mfr, rsr], [res("cqn%d" % tl)])
                                sk, skr = statc("ckv_ssq")
                                S.op(dve, lambda: nc.vector.scalar_tensor_tensor(
                                    out=rb[0:n, 0, 0:128], in0=mf[0:n, 0, 256:384], scalar=1.0,
                                    in1=mf[0:n, 0, 256:384], op0=ALU.mult, op1=ALU.mult, accum_out=sk[0:n]),
                                    [mfr], [res("rb0"), skr])
                                rs2, rsr2 = rstd_cached("ckv", sk, skr, 128, n)
                                S.op(dve, lambda: nc.vector.scalar_tensor_tensor(
                                    out=cko[0:n, 0, :], in0=mf[0:n, 0, 256:384], scalar=rs2[0:n], in1=gkv_b[0:n, :],
                                    op0=ALU.mult, op1=ALU.mult), [mfr, rsr2, res("gkv")], [res("cko0")])
                                S.dma(dstream("co0"), G.outs["ckv"][ti * n:(ti + 1) * n, :], cko[0:n, 0, :],
                                      reads=[res("cko0")], writes=[out_res])
                                S.op(pool, lambda: nc.gpsimd.tensor_copy(out=CX[0:n, kt, 0:128], in_=cko[0:n, 0, :]),
                                     [res("cko0")], [kvr("CX", kt)])
                                rope(mf[0:n, 0, 384:448], n, 1, tab_i, kro[0:n, tl, :], [mfr],
                                     [res("kro%d" % tl)], dve, pool)
                                S.dma(dstream("ro%d" % tl), G.outs["kr"][ti * n:(ti + 1) * n, :], kro[0:n, tl, :],
                                      reads=[res("kro%d" % tl)], writes=[out_res])
                                for dd in range(2):
                                    S.op(pool, lambda dd=dd: nc.gpsimd.tensor_copy(
                                        out=krd[0:n, tl, dd * 64:(dd + 1) * 64], in_=kro[0:n, tl, :]),
                                        [res("kro%d" % tl)], [res("krd%d" % tl)])

                                def tm(tl=tl, kt=kt, tok0=tok0):
                                    for c in range(2):
                                        tr(pT[:, c * 128:c * 128 + n], cqn[0:n, tl, c * 128:(c + 1) * 128],
                                           [res("cqn%d" % tl)], [pTr])
                                    tr(pT[:, 256:256 + n], CX[0:n, kt, 0:128], [kvr("CX", kt)], [pTr])
                                    tr(pT[:, 384:384 + n], krd[0:n, tl, :], [res("krd%d" % tl)], [pTr])
                                    S.op(dve, lambda: nc.vector.tensor_copy(
                                        out=cqT[:, :, tl * 128:tl * 128 + n],
                                        in_=pT[:, 0:256].rearrange("p (c t) -> p c t", c=2)[:, :, 0:n]),
                                        [], [pTr, res("cqT")])
                                    S.op(dve, lambda: nc.vector.tensor_copy(out=CT[:, tok0:tok0 + n],
                                                                            in_=pT[:, 256:256 + n]),
                                         [], [pTr, kvr("CT", kt)])
                                    S.op(dve, lambda: nc.vector.tensor_copy(out=RT[:, tok0:tok0 + n],
                                                                            in_=pT[:, 384:384 + n]),
                                         [], [pTr, kvr("RT", kt)])
                                defer_T(tm)
                        stages.append(st_seg)

                for hp in range(2):
                    def st_qp(hp=hp):
                        flush_T(hp == 0)
                        for hh in range(2):
                            h = hp * 2 + hh
                            for c in range(2):
                                mm(banks[PA][:, hh * 256:hh * 256 + nq], wq_eff[:, c, h, :], cqT[:, c, 0:nq],
                                   c == 0, c == 1, [res("cqT"), res("wq_eff")], [bres[PA]])
                        S.op(dve, lambda hp=hp: nc.vector.tensor_copy(
                            out=QpT[:, hp * 2:hp * 2 + 2, 0:nq],
                            in_=banks[PA][:, :].rearrange("p (h t) -> p h t", h=2)[:, :, 0:nq]),
                            [], [bres[PA], res("QpT")])
                    stages.append(st_qp)

                for tl in range(tpg):
                    def st_qr(tl=tl):
                        flush_T()
                        ti = G.g * tpg + tl
                        tab_i = G.tab0 + ti
                        for c in range(2):
                            mm(banks[PA][0:n, 0:256], cqT[:, c, tl * 128:tl * 128 + n],
                               wqr[:, c].rearrange("p h r -> p (h r)"), c == 0, c == 1, [res("cqT"), res("wqr")],
                               [bres[PA]])
                        copy(act, qf[0:n, 0, 0:256], banks[PA][0:n, 0:256], [], [bres[PA], res("qf0")])
                        rope(qf[0:n, 0, 0:256], n, 4, tab_i, qrb[0:n, tl, 0:256], [res("qf0")],
                             [res("qrb%d" % tl)], dve, pool)

                        def tqr(tl=tl):
                            for u in range(2):
                                tr(pT[:, u * 128:u * 128 + n], qrb[0:n, tl, u * 128:(u + 1) * 128],
                                   [res("qrb%d" % tl)], [pTr])
                            S.op(dve, lambda: nc.vector.tensor_copy(
                                out=QRT[:, :, tl * 128:tl * 128 + n],
                                in_=pT[:, 0:256].rearrange("p (u t) -> p u t", u=2)[:, :, 0:n]),
                                [], [pTr, res("QRT")])
                        defer_T(tqr)
                    stages.append(st_qr)
                stages.append(flush_T)
                stages.append(lambda: flush_T(True))
                return stages

            def attention(G):
                n, tpg, nq, g, npast = G.tsz, G.tpg, G.nq, G.g, G.npast
                pairs = []
                if not G.sample:
                    for j in range(g):
                        pairs.append([(2 * j, 128), (2 * j + 1, 128)])
                    pairs.append("diag")
                else:
                    for j in range(npast // 2):
                        pairs.append([(2 * j, 128), (2 * j + 1, 128)])
                    pairs.append([(npast, n)])
                qts = [(q0, min(128, nq - q0)) for q0 in range(0, nq, 128)]
                pS = [[3, 5], [4, 6]]
                pO = [7, 0]
                deferred = []

                def make_unit(unit_i):
                    is_diff = unit_i < 4
                    scale = DIFF_SCALE if is_diff else MLA_SCALE
                    first_av = [True, True]

                    def emit_S(pi, pr):
                        buf = pi % 2
                        tl_list = pr if pr != "diag" else [(2 * g, 128), (2 * g + 1, 128)]
                        for i, (kt, nk) in enumerate(tl_list):
                            c0 = kt * 128
                            for m in range(2):
                                bk = pS[m][buf]
                                outp = banks[bk][0:nk, i * 256:i * 256 + nq]
                                if is_diff:
                                    h = unit_i
                                    mm(outp, KT[m * 64:(m + 1) * 64, h, c0:c0 + nk], QT[m * 64:(m + 1) * 64, h, 0:nq],
                                       True, True, [kvr("KT", kt), res("QT")], [bres[bk]])
                                else:
                                    h = (unit_i - 4) * 2 + m
                                    mm(outp, CT[:, c0:c0 + nk], QpT[:, h, 0:nq], True, False,
                                       [kvr("CT", kt), res("QpT")], [bres[bk]])
                                    mm(outp, RT[m * 64:(m + 1) * 64, c0:c0 + nk],
                                       QRT[m * 64:(m + 1) * 64, unit_i - 4, 0:nq], False, True,
                                       [kvr("RT", kt), res("QRT")], [bres[bk]])
                        bks = [bres[pS[0][buf]], bres[pS[1][buf]]]
                        ptrs = [res("PT0%d" % buf), res("PT1%d" % buf)]
                        b0 = (pS[0][buf] - 3) * 512
                        bv = pS_all[:, b0:b0 + 1024].rearrange("p (m i q) -> p m i q", m=2, i=2)
                        if pr == "diag":
                            regs = [(0, 64, 0, 0, 256), (64, 128, 0, 64, 256), (0, 64, 1, 128, 256),
                                    (64, 128, 1, 192, 256)]
                            for (p0, p1, i, q0, q1) in [(64, 128, 0, 0, 64), (64, 128, 1, 128, 192)]:
                                S.op(pool, lambda: nc.gpsimd.memset(PT[p0:p1, :, buf, i, q0:q1], 0.0), [], ptrs)
                            for (p0, p1, i, q0, q1) in regs:
                                S.op(act, lambda: nc.scalar.activation(
                                    out=PT[p0:p1, :, buf, i, q0:q1], in_=bv[p0:p1, :, i, q0:q1], func=AF.Exp,
                                    scale=scale), [], bks + ptrs)
                        elif len(pr) == 2:
                            S.op(act, lambda: nc.scalar.activation(
                                out=PT[:, :, buf, :, 0:nq], in_=bv[:, :, :, 0:nq], func=AF.Exp, scale=scale),
                                [], bks + ptrs)
                        else:
                            nk = pr[0][1]
                            S.op(act, lambda: nc.scalar.activation(
                                out=PT[0:nk, :, buf, 0, 0:nq], in_=bv[0:nk, :, 0, 0:nq], func=AF.Exp,
                                scale=scale), [], bks + ptrs)

                    def emit_AV(pi, pr, last):
                        buf = pi % 2
                        if pr == "diag":
                            items = [(2 * g, 128, 0, (0, 1)), (2 * g + 1, 128, 1, (1,))]
                        else:
                            items = [(kt, nk, i, tuple(range(len(qts)))) for i, (kt, nk) in enumerate(pr)]
                        for qi, (q0, nqt) in enumerate(qts):
                            bk = pO[qi]
                            ov = banks[bk][:, :].rearrange("p (m x) -> p m x", m=2)
                            for m in range(2):
                                its = [it for it in items if qi in it[3]]
                                for ii, (kt, nk, i, _q) in enumerate(its):
                                    lhsT = PT[0:nk, m, buf, i, q0:q0 + nqt]
                                    ptr_ = res("PT%d%d" % (m, buf))
                                    if is_diff:
                                        rhs = VX[0:nk, kt, unit_i, 0:129]
                                        rr = [kvr("VX", kt), res("VXones")]
                                    else:
                                        rhs = CX[0:nk, kt, 0:129]
                                        rr = [kvr("CX", kt), res("CXones")]
                                    start = first_av[qi] and m == 0
                                    if start:
                                        first_av[qi] = False
                                    mm(ov[0:nqt, m, 0:129], lhsT, rhs, start, last and ii == len(its) - 1,
                                       [ptr_] + rr, [bres[bk]])

                    def finish_unit():
                        def each_q(fn):
                            for qi, (q0, nqt) in enumerate(qts):
                                fn(qi, nqt)

                        def e_copy(qi, nqt):
                            bk = pO[qi]
                            ov = banks[bk][:, :].rearrange("p (m x) -> p m x", m=2)
                            S.op(dve, lambda: nc.vector.tensor_copy(out=osb[0:nqt, qi, :, :], in_=ov[0:nqt, :, 0:129]),
                                 [], [bres[bk], res("osb%d" % qi)])
                        each_q(e_copy)

                        def e_recip(qi, nqt):
                            rsum, rsumr = statc("rsum%d" % qi, 2)
                            S.op(dve, lambda: nc.vector.reciprocal(out=rsum[0:nqt, 0:2], in_=osb[0:nqt, qi, :, 128]),
                                 [res("osb%d" % qi)], [rsumr])
                        each_q(e_recip)
                        if is_diff:
                            def e_r1(qi, nqt):
                                rsum, rsumr = statc("rsum%d" % qi, 2)
                                r1, r1r = statc("r1_%d" % qi)
                                S.op(dve, lambda: nc.vector.tensor_tensor(out=r1[0:nqt], in0=rsum[0:nqt, 1:2],
                                                                          in1=neg_lam[0:nqt], op=ALU.mult),
                                     [rsumr, res("neg_lam")], [r1r])
                            each_q(e_r1)

                            def e_t0(qi, nqt):
                                rsum, rsumr = statc("rsum%d" % qi, 2)
                                S.op(dve, lambda: nc.vector.tensor_scalar(
                                    out=t0[0:nqt, qi, :], in0=osb[0:nqt, qi, 0, 0:128], scalar1=rsum[0:nqt, 0:1],
                                    scalar2=None, op0=ALU.mult), [rsumr, res("osb%d" % qi)], [res("t0%d" % qi)])
                            each_q(e_t0)

                            def e_od(qi, nqt):
                                r1, r1r = statc("r1_%d" % qi)
                                S.op(dve, lambda: nc.vector.scalar_tensor_tensor(
                                    out=od[0:nqt, qi, :], in0=osb[0:nqt, qi, 1, 0:128], scalar=r1[0:nqt],
                                    in1=t0[0:nqt, qi, :], op0=ALU.mult, op1=ALU.add),
                                    [r1r, res("osb%d" % qi), res("t0%d" % qi)], [res("od%d" % qi)])
                            each_q(e_od)

                            def e_ssq(qi, nqt):
                                sq, sqr = statc("od_ssq%d" % qi)
                                S.op(dve, lambda: nc.vector.scalar_tensor_tensor(
                                    out=t0[0:nqt, qi, :], in0=od[0:nqt, qi, :], scalar=1.0, in1=od[0:nqt, qi, :],
                                    op0=ALU.mult, op1=ALU.mult, accum_out=sq[0:nqt]),
                                    [res("od%d" % qi)], [res("t0%d" % qi), sqr])
                            each_q(e_ssq)
                            rss = {}

                            def e_rstd(qi, nqt):
                                sq, sqr = statc("od_ssq%d" % qi)
                                rss[qi] = rstd_cached("od%d" % qi, sq, sqr, 128, nqt)
                            each_q(e_rstd)

                            def e_on(qi, nqt):
                                rs, rsr = rss[qi]
                                S.op(dve, lambda: nc.vector.tensor_scalar(
                                    out=onall[0:nqt, qi, unit_i % 2, 0, :], in0=od[0:nqt, qi, :], scalar1=rs[0:nqt],
                                    scalar2=None, op0=ALU.mult), [res("od%d" % qi), rsr],
                                    [res("onall%d_0" % (unit_i % 2))])
                            each_q(e_on)
                        else:
                            for m in range(2):
                                def e_onm(qi, nqt, m=m):
                                    rsum, rsumr = statc("rsum%d" % qi, 2)
                                    S.op(dve, lambda: nc.vector.tensor_scalar(
                                        out=onall[0:nqt, qi, unit_i % 2, m, :], in0=osb[0:nqt, qi, m, 0:128],
                                        scalar1=rsum[0:nqt, m:m + 1], scalar2=None, op0=ALU.mult),
                                        [rsumr, res("osb%d" % qi)], [res("onall%d_%d" % (unit_i % 2, m))])
                                each_q(e_onm)

                        def tail(unit_i=unit_i, is_diff=is_diff):
                            cs = [unit_i] if is_diff else [4 + (unit_i - 4) * 2, 5 + (unit_i - 4) * 2]
                            for ci, c in enumerate(cs):
                                for qi, (q0, nqt) in enumerate(qts):
                                    tr(pT[:, (ci * 2 + qi) * 128:(ci * 2 + qi) * 128 + nqt],
                                       onall[0:nqt, qi, unit_i % 2, ci, :], [res("onall%d_%d" % (unit_i % 2, ci))], [pTr])
                            S.op(dve, lambda: nc.vector.tensor_copy(
                                out=mixT[:, cs[0]:cs[0] + len(cs), 0:nq],
                                in_=pT[:, 0:512].rearrange("p (c t) -> p c t", c=2)[:, 0:len(cs), 0:nq]),
                                [], [pTr, res("xnT0"), res("xnT1")] + [res("mixT%d" % c) for c in cs])
                        deferred.append(tail)
                    return emit_S, emit_AV, finish_unit

                unit_fns = [make_unit(u) for u in range(6)]
                np_ = len(pairs)
                steps = [(u, pr) for u in range(6) for pr in pairs]

                def run_av(pd):
                    pu, psi, ppr, plast = pd
                    unit_fns[pu][1](psi, ppr, plast)
                    if plast:
                        while deferred:
                            deferred.pop(0)()
                        unit_fns[pu][2]()
                pend = None
                for si, (u, pr) in enumerate(steps):
                    unit_fns[u][0](si, pr)
                    if pend is not None:
                        run_av(pend)
                    pend = (u, si, pr, si % np_ == np_ - 1)
                run_av(pend)
                while deferred:
                    deferred.pop(0)()

            def outproj(G):
                n, tpg = G.tsz, G.tpg
                mixr = [res("mixT%d" % c) for c in range(8)] + [res("xnT0"), res("xnT1")]
                us = {}
                for half in range(2):
                    for hk in range(2):
                        us[(half, hk)] = ring_next(("wout", half, hk))
                for tl in range(tpg):
                    xr = res("xg%d%d" % (G.slot, tl))
                    for half in range(2):
                        bk = half
                        for c in range(8):
                            u_, unit, ur = us[(half, c // 4)]
                            uv = unit[:, 0:2048].rearrange("p (k n) -> p k n", k=4)
                            mm(banks[bk][0:n, 0:512], mixT[:, c, tl * 128:tl * 128 + n], uv[:, c % 4, :], c == 0,
                               c == 7, mixr + [ur], [bres[bk]])
                        S.op(dve, lambda: nc.vector.tensor_tensor(
                            out=xg[0:n, G.slot, tl, half * 512:(half + 1) * 512], in0=banks[bk][0:n, 0:512],
                            in1=xg[0:n, G.slot, tl, half * 512:(half + 1) * 512], op=ALU.add), [xr], [bres[bk], xr])
                    xt = xg[0:n, G.slot, tl, :]
                    xb = (xn, xn2)[tl]
                    ssq, ssqr = statc("f_ssq%d" % tl)
                    S.op(act, lambda: nc.scalar.activation(out=xb[0:n, :], in_=xt, func=AF.Square,
                                                           accum_out=ssq[0:n]), [xr], [res("xn%d" % tl), ssqr])
                    rs, rsr = rstd_cached("f%d" % tl, ssq, ssqr, D, n)
                    S.op(dve, lambda: nc.vector.tensor_scalar(out=xb[0:n, :], in0=xt, scalar1=rs[0:n], scalar2=None,
                                                              op0=ALU.mult), [xr, rsr], [res("xn%d" % tl)])
                for h_ in us.values():
                    ring_release(h_[0])

            def ffn(G, stages, depth=2):
                n, tpg, nq = G.tsz, G.tpg, G.nq
                if G.g == 0:
                    if G.conv_init is None:
                        S.op(pool, lambda: nc.gpsimd.memset(gstate[:], 0.0), [], [res("gstate")])
                    else:
                        for r_ in range(2):
                            S.dma(dstream("c0"), gstate[:, :, r_],
                                  G.conv_init[r_].rearrange("(j p) -> p j", p=128),
                                  writes=[res("gstate")], allow_slow_non_contiguous=True)
                for tl in range(tpg):
                    xb = (xn, xn2)[tl]
                    for k in range(8):
                        tr(pT[:, k * 128:k * 128 + n], xb[0:n, k * 128:(k + 1) * 128], [res("xn%d" % tl)], [pTr])
                    S.op(dve, lambda: nc.vector.tensor_copy(
                        out=h2T[:, :, tl * 128:tl * 128 + n],
                        in_=pT[:, :].rearrange("p (k t) -> p k t", k=8)[:, :, 0:n]), [], [pTr, res("h2T%d" % tl)])
                pY = [[3, 4], [5, 6]]
                pend = []
                h2r = [res("h2T%d" % tl) for tl in range(tpg)]

                def emit_down(j, b3, ud):
                    u_, unit, ur = ud
                    for tl in range(tpg):
                        for half in range(2):
                            bk = pY[tl][half]
                            mm(banks[bk][0:n, 0:512], actT[:, b3, tl * 128:tl * 128 + n],
                               unit[:, half * 512:(half + 1) * 512], j == 0, j == NCH - 1,
                               [res("actT%d_%d" % (b3, tl)), ur], [bres[bk]])
                    ring_release(u_)

                stages = list(stages)
                for j in range(NCH):
                    b = j % 2
                    uu = ring_next(("up", j, 0))
                    ud = ring_next(("down", j, 0))
                    unit = uu[1]
                    uvv = unit[:, 0:2048].rearrange("p (k u c) -> p k u c", k=8, u=2)
                    pv = banks[b][:, :].rearrange("p (u t) -> p u t", u=2)
                    for u in range(2):
                        for k in range(8):
                            mm(pv[:, u, 0:nq], uvv[:, k, u, :], h2T[:, k, 0:nq], k == 0, k == 7,
                               h2r + [uu[2]], [bres[b]])
                    ring_release(uu[0])
                    if len(pend) == depth:
                        emit_down(*pend.pop(0))
                    b3 = j % 3
                    gb = gbuf[:, b, :]
                    gbr = res("gbuf%d" % b)
                    ubr = res("ub%d" % b)
                    copy(act, gb[:, 2:2 + nq], pv[:, 1, 0:nq], [], [bres[b], gbr])
                    copy(act, ub[:, b, 0:nq], pv[:, 0, 0:nq], [], [bres[b], ubr])
                    S.op(pool, lambda: nc.gpsimd.tensor_copy(out=gb[:, 0:2], in_=gstate[:, j, :]),
                         [res("gstate")], [gbr])
                    S.op(pool, lambda: nc.gpsimd.tensor_copy(out=gstate[:, j, :], in_=gb[:, nq:nq + 2]),
                         [gbr], [res("gstate")])
                    halves = [(0, nq)] if nq <= 128 else [(0, 128), (128, nq)]
                    hres = [res("cbuf%d_%d" % (b, hi)) for hi in range(len(halves))]
                    for hi, (h0, h1) in enumerate(halves):
                        S.op(dve, lambda: nc.vector.tensor_scalar(
                            out=cbuf[:, b, h0:h1], in0=gb[:, h0:h1], scalar1=wconv[:, j, 0:1],
                            scalar2=bconv[:, j:j + 1], op0=ALU.mult, op1=ALU.add), [gbr, res("wconv")], [hres[hi]])
                    for hi, (h0, h1) in enumerate(halves):
                        S.op(dve, lambda: nc.vector.scalar_tensor_tensor(
                            out=cbuf[:, b, h0:h1], in0=gb[:, 1 + h0:1 + h1], scalar=wconv[:, j, 1:2],
                            in1=cbuf[:, b, h0:h1], op0=ALU.mult, op1=ALU.add), [gbr, res("wconv"), hres[hi]],
                            [hres[hi]])
                    for hi, (h0, h1) in enumerate(halves):
                        S.op(dve, lambda: nc.vector.scalar_tensor_tensor(
                            out=cbuf[:, b, h0:h1], in0=gb[:, 2 + h0:2 + h1], scalar=wconv[:, j, 2:3],
                            in1=cbuf[:, b, h0:h1], op0=ALU.mult, op1=ALU.add), [gbr, res("wconv"), hres[hi]],
                            [hres[hi]])
                    for hi, (h0, h1) in enumerate(halves):
                        S.op(act, lambda: nc.scalar.activation(out=cbuf[:, b, h0:h1], in_=cbuf[:, b, h0:h1],
                                                               func=AF.Silu), [hres[hi]], [hres[hi]])
                    for hi, (h0, h1) in enumerate(halves):
                        S.op(dve, lambda: nc.vector.tensor_tensor(out=actT[:, b3, h0:h1], in0=ub[:, b, h0:h1],
                                                                  in1=cbuf[:, b, h0:h1], op=ALU.mult),
                             [hres[hi], ubr], [res("actT%d_%d" % (b3, hi))])
                    pend.append((j, b3, ud))
                    if stages:
                        stg = stages.pop(0)
                        if stg is not None:
                            stg()
                while pend:
                    emit_down(*pend.pop(0))
                while stages:
                    stg = stages.pop(0)
                    if stg is not None:
                        stg()

                for tl in range(tpg):
                    ti = G.g * tpg + tl
                    xr = res("xg%d%d" % (G.slot, tl))
                    for half in range(2):
                        bk = pY[tl][half]
                        S.op(dve, lambda: nc.vector.tensor_tensor(
                            out=xg[0:n, G.slot, tl, half * 512:(half + 1) * 512], in0=banks[bk][0:n, 0:512],
                            in1=xg[0:n, G.slot, tl, half * 512:(half + 1) * 512], op=ALU.add), [xr], [bres[bk], xr])
                    xt = xg[0:n, G.slot, tl, :]
                    ssq, ssqr = statc("y_ssq%d" % tl)
                    xb = (xn, xn2)[tl]
                    S.op(act, lambda: nc.scalar.activation(out=xb[0:n, :], in_=xt, func=AF.Square,
                                                           accum_out=ssq[0:n]), [xr], [res("xn%d" % tl), ssqr])
                    rs, rsr = rstd_cached("y%d" % tl, ssq, ssqr, D, n)
                    S.op(dve, lambda: nc.vector.scalar_tensor_tensor(
                        out=xt, in0=xt, scalar=rs[0:n], in1=gfin_b[0:n, :], op0=ALU.mult, op1=ALU.mult),
                        [xr, rsr, res("gfin")], [xr])
                    S.dma(dstream("yo%d" % tl), G.outs["y"][ti * n:(ti + 1) * n, :], xt,
                          reads=[xr], writes=[out_res])
                if G.g == G.ng - 1:
                    for r_ in range(2):
                        S.dma(dstream("c%d" % (1 + r_)), G.outs["conv"][r_].rearrange("(j p) -> p j", p=128),
                              gstate[:, :, r_], reads=[res("gstate")], writes=[out_res],
                              allow_slow_non_contiguous=True)

            load_x(groups[0])
            if len(groups) > 1:
                load_x(groups[1])
            for stg in phaseA_stages(groups[0]):
                if stg is not None:
                    stg()
            for f_, G in enumerate(groups):
                attention(G)
                outproj(G)
                nxt = phaseA_stages(groups[f_ + 1]) if f_ + 1 < len(groups) else []
                dense = f_ + 1 < len(groups) and groups[f_ + 1].tpg == 1
                ffn(G, nxt, 1 if dense else 2)
                if f_ + 2 < len(groups):
                    load_x(groups[f_ + 2])
            S.barrier()
            return ring_log

        plan = emit_main(None)
        emit_main(plan)
    return nc


def rope_tables(seq, past, dec):
    nt = seq // 128
    half = 32
    inv = (np.float32(10000.0) ** (-np.arange(half, dtype=np.float32) * np.float32(2.0 / 64))).astype(np.float32)
    pos = np.zeros((128, nt + 1), np.float32)
    for t in range(nt):
        pos[:, t] = t * 128 + np.arange(128)
    pos[:, nt] = past + np.arange(128)
    ang = (pos[:, :, None] * inv[None, None, :]).astype(np.float32)
    return np.cos(ang).astype(np.float32), np.sin(ang).astype(np.float32)


def make_in_maps(inputs, n_cores, n_prompt, seq, past, dec):
    f = lambda a: np.ascontiguousarray(np.asarray(a, dtype=np.float32))
    cos, sin = rope_tables(seq, past, dec)
    common = {
        "g_attn_pk": f(inputs["g_attn"][0].reshape(8, 128).T),
        "g_ffn_pk": f(inputs["g_ffn"][0].reshape(8, 128).T),
        "g_q_pk": f(inputs["g_q_lora"][0].reshape(2, 128).T),
        "g_sub_p": f(inputs["g_diff_sub"][0].reshape(128, 1)),
        "g_kv": f(inputs["g_kv_lora"][0].reshape(1, 128)),
        "g_final": f(inputs["g_final"].reshape(1, D)),
        "lamv": f(np.stack([inputs["lambda_q1"][0], inputs["lambda_k1"][0], inputs["lambda_q2"][0],
                            inputs["lambda_k2"][0]], 0).reshape(1, 256)),
        "wconv_pk": f(inputs["w_conv"][0].reshape(3, NCH, 128).transpose(2, 1, 0)),
        "bconv_pk": f(inputs["b_conv"][0].reshape(NCH, 128).T),
        "w_in": f(inputs["w_in"][0]), "w_qb": f(inputs["w_q_b"][0]), "w_kvb": f(inputs["w_kv_b"][0]),
        "w_out": f(inputs["w_out"][0]), "w_up": f(inputs["w_up"][0]), "w_down": f(inputs["w_down"][0]),
        "ident": np.eye(128, dtype=np.float32).astype(ml_dtypes.bfloat16),
        "cos_t": cos, "sin_t": sin,
    }
    maps = []
    for c in range(n_cores):
        m = dict(common)
        m["xp"] = f(inputs["x_prompt"][c * n_prompt:(c + 1) * n_prompt])
        m["xs"] = f(inputs["x_sample"][c])
        m["cdk"] = f(inputs["cache_diff_k"][0, c].reshape(past, 512))
        m["cdv"] = f(inputs["cache_diff_v"][0, c].reshape(past, 512))
        m["cckv"] = f(inputs["cache_mla_ckv"][0, c])
        m["ckr"] = f(inputs["cache_mla_krope"][0, c])
        m["sconv"] = f(inputs["state_conv"][0, c])
        maps.append(m)
    return maps


_NC_CACHE = {}


def kernel(**inputs):
    inputs = {k: np.asarray(v) for k, v in inputs.items()}
    B, seq, _ = inputs["x_prompt"].shape
    n_cores = inputs["x_sample"].shape[0]
    n_prompt = B // n_cores
    dec = inputs["x_sample"].shape[1]
    past = inputs["cache_diff_k"].shape[2]
    key = (n_prompt, seq, past, dec)
    if key not in _NC_CACHE:
        _NC_CACHE[key] = build(n_prompt=n_prompt, seq=seq, sample=True, past=past, dec=dec)
    nc = _NC_CACHE[key]
    maps = make_in_maps(inputs, n_cores, n_prompt, seq, past, dec)
    res = run_bass_kernel_spmd(nc, maps, core_ids=list(range(n_cores)))
    r = res.results
    cat = lambda k: np.concatenate([np.asarray(x[k], dtype=np.float32) for x in r], axis=0)
    stk = lambda k: np.stack([np.asarray(x[k], dtype=np.float32) for x in r], axis=0)
    y_prompt = cat("yp")
    y_sample = stk("ys")
    dk_p = cat("dkp").reshape(1, B, seq, 4, 2, 64)
    dv_p = cat("dvp").reshape(1, B, seq, 4, 128)
    ckv_p = cat("ckvp").reshape(1, B, seq, 128)
    kr_p = cat("krp").reshape(1, B, seq, 64)
    conv_p = cat("convp").reshape(1, B, 2, D_FF)
    dk_s = stk("dks").reshape(1, n_cores, dec, 4, 2, 64)
    dv_s = stk("dvs").reshape(1, n_cores, dec, 4, 128)
    ckv_s = stk("ckvs").reshape(1, n_cores, dec, 128)
    kr_s = stk("krs").reshape(1, n_cores, dec, 64)
    conv_s = stk("convs").reshape(1, n_cores, 2, D_FF)
    return (y_prompt, y_sample, dk_p, dv_p, ckv_p, kr_p, conv_p, dk_s, dv_s, ckv_s, kr_s, conv_s)
```
